# Optimizing a Trainium2 kernel written in Bass

```python
import math
import jax
import jax.numpy as jnp
from jax import lax
import numpy as np

D_MODEL = 2048
BATCH = 4
SEQ = 4096
DEPTH = 2

CTX_LEN = 256
GRID_W = 64
HEAD_DIM = 128
N_HEADS_TOTAL = D_MODEL // HEAD_DIM
A_HEADS = N_HEADS_TOTAL // 4
A_QK_DIM = HEAD_DIM // 2
B_HEADS = N_HEADS_TOTAL // 2
B_KV_HEADS = B_HEADS // 4
C_HEADS = N_HEADS_TOTAL // 4
NA_ROWS_MAX = 8
NA_COLS = 16
Q_BLOCK = 128
ROPE_THETA = 10000.0
EPS = 1e-6
D_FF = ((8 * D_MODEL + 3 * 256 - 1) // (3 * 256)) * 256
A_QK = A_HEADS * 2 * A_QK_DIM
A_V = A_HEADS * HEAD_DIM
B_Q = B_HEADS * HEAD_DIM
B_KV = B_KV_HEADS * HEAD_DIM
C_W = C_HEADS * HEAD_DIM
IN_SIZES = (A_QK, A_QK, A_V, B_Q, B_KV, B_KV, C_W, C_W, C_W)
IN_COLS = sum(IN_SIZES)
MIX_OUT = A_V + B_Q + C_W

kernel_name = 'hybrid_diffusion_parallel_heads'


def _rms(x, g):
    xf = x.astype(jnp.float32)
    y = xf * lax.rsqrt(jnp.mean(xf * xf, axis=-1, keepdims=True) + EPS)
    return (y * g.astype(jnp.float32)).astype(x.dtype)


def _rope_axis(xp, pos):
    quarter = xp.shape[-1] // 2
    inv = ROPE_THETA ** (-jnp.arange(quarter, dtype=jnp.float32) / quarter)
    ang = pos.astype(jnp.float32)[:, None] * inv[None, :]
    cos, sin = jnp.cos(ang), jnp.sin(ang)
    x1, x2 = xp[..., :quarter], xp[..., quarter:]
    return jnp.concatenate([x1 * cos - x2 * sin, x1 * sin + x2 * cos], axis=-1)


def axial_rope(x, row, col):
    half = x.shape[-1] // 2
    xf = x.astype(jnp.float32)
    return jnp.concatenate([_rope_axis(xf[..., :half], row),
                            _rope_axis(xf[..., half:], col)], axis=-1).astype(x.dtype)


def split_cols(p):
    outs, start = [], 0
    for n in IN_SIZES:
        outs.append(p[..., start:start + n])
        start += n
    return outs


def to_heads(t, h):
    b, n, _ = t.shape
    return t.reshape(b, n, h, -1).transpose(0, 2, 1, 3)


def from_heads(t):
    b, h, n, d = t.shape
    return t.transpose(0, 2, 1, 3).reshape(b, n, h * d)


def sweep_query_blocks(fn, *qs):
    s = qs[0].shape[-2]
    nb = s // Q_BLOCK
    blocks = tuple(jnp.moveaxis(q.reshape(q.shape[:-2] + (nb, Q_BLOCK, q.shape[-1])), -3, 0)
                   for q in qs)
    out = lax.map(lambda bl: fn(*bl), blocks)
    out = jnp.moveaxis(out, 0, -3)
    return out.reshape(out.shape[:-3] + (s, out.shape[-1]))


def dense_attention(q, k, v):
    s = jnp.einsum('bhqd,bhtd->bhqt', q, k).astype(jnp.float32) * (q.shape[-1] ** -0.5)
    p = jax.nn.softmax(s, axis=-1).astype(v.dtype)
    return jnp.einsum('bhqt,bhtd->bhqd', p, v)


def diff_attention(q1, q2, k1, k2, v, lam):
    scale = A_QK_DIM ** -0.5
    s1 = jnp.einsum('bhqd,bhtd->bhqt', q1, k1).astype(jnp.float32) * scale
    s2 = jnp.einsum('bhqd,bhtd->bhqt', q2, k2).astype(jnp.float32) * scale
    p = jax.nn.softmax(s1, axis=-1) - lam * jax.nn.softmax(s2, axis=-1)
    return jnp.einsum('bhqt,bhtd->bhqd', p.astype(v.dtype), v)


def diff_mixer(q, k, v, qc, kc, vc, row, col, lq1, lk1, lq2, lk2, g_sub, layer_idx, need_ctx):
    lam_init = 0.8 - 0.6 * math.exp(-0.3 * layer_idx)
    f32 = jnp.float32
    lam = (jnp.exp(jnp.sum(lq1.astype(f32) * lk1.astype(f32)))
           - jnp.exp(jnp.sum(lq2.astype(f32) * lk2.astype(f32))) + lam_init)

    def qk_pair(t):
        b, n, _ = t.shape
        t = t.reshape(b, n, A_HEADS, 2, A_QK_DIM).transpose(3, 0, 2, 1, 4)
        return t[0], t[1]

    q1, q2 = qk_pair(q)
    k1, k2 = qk_pair(k)
    q1, q2, k1, k2 = (axial_rope(t, row, col) for t in (q1, q2, k1, k2))
    k1c, k2c = qk_pair(kc)
    vch = to_heads(vc, A_HEADS)
    k1a = jnp.concatenate([k1c, k1], axis=2)
    k2a = jnp.concatenate([k2c, k2], axis=2)
    va = jnp.concatenate([vch, to_heads(v, A_HEADS)], axis=2)
    o = sweep_query_blocks(lambda a, b2: diff_attention(a, b2, k1a, k2a, va, lam), q1, q2)
    out = from_heads(_rms(o, g_sub) * (1.0 - lam_init))
    out_c = None
    if need_ctx:
        q1c, q2c = qk_pair(qc)
        oc = diff_attention(q1c, q2c, k1c, k2c, vch, lam)
        out_c = from_heads(_rms(oc, g_sub) * (1.0 - lam_init))
    return out, out_c


def gqa_attention(q, k, v):
    s = jnp.einsum('bkgqd,bktd->bkgqt', q, k).astype(jnp.float32) * (HEAD_DIM ** -0.5)
    p = jax.nn.softmax(s, axis=-1).astype(v.dtype)
    return jnp.einsum('bkgqt,bktd->bkgqd', p, v)


def gqa_mixer(q, k, v, qc, kc, vc, row, col, g_qn, g_kn, need_ctx):
    group = B_HEADS // B_KV_HEADS

    def prep_q(t, rope):
        b, n, _ = t.shape
        t = _rms(t.reshape(b, n, B_HEADS, HEAD_DIM), g_qn).transpose(0, 2, 1, 3)
        if rope:
            t = axial_rope(t, row, col)
        return t.reshape(b, B_KV_HEADS, group, n, HEAD_DIM)

    def prep_k(t, rope):
        b, n, _ = t.shape
        t = _rms(t.reshape(b, n, B_KV_HEADS, HEAD_DIM), g_kn).transpose(0, 2, 1, 3)
        if rope:
            t = axial_rope(t, row, col)
        return t

    kch, vch = prep_k(kc, False), to_heads(vc, B_KV_HEADS)
    ka = jnp.concatenate([kch, prep_k(k, True)], axis=2)
    va = jnp.concatenate([vch, to_heads(v, B_KV_HEADS)], axis=2)
    o = sweep_query_blocks(lambda qb: gqa_attention(qb, ka, va), prep_q(q, True))
    b, _, _, n, _ = o.shape
    out = from_heads(o.reshape(b, B_HEADS, n, HEAD_DIM))
    out_c = None
    if need_ctx:
        oc = gqa_attention(prep_q(qc, False), kch, vch)
        out_c = from_heads(oc.reshape(oc.shape[0], B_HEADS, oc.shape[3], HEAD_DIM))
    return out, out_c


def na_mixer(q, k, v, qc, kc, vc, rpb_l, need_ctx):
    qh, kh, vh = to_heads(q, C_HEADS), to_heads(k, C_HEADS), to_heads(v, C_HEADS)
    kch, vch = to_heads(kc, C_HEADS), to_heads(vc, C_HEADS)
    b, hh, s, d = qh.shape
    rows_n = s // GRID_W
    wr = min(NA_ROWS_MAX, rows_n)
    scale = d ** -0.5
    grid = lambda t: t.reshape(b, hh, rows_n, GRID_W, d)
    qg, kg, vg = grid(qh), grid(kh), grid(vh)
    r = jnp.arange(rows_n)
    r0 = jnp.clip(r - wr // 2, 0, rows_n - wr)
    row_idx = r0[:, None] + jnp.arange(wr)[None, :]
    k_band = kg[:, :, row_idx]
    v_band = vg[:, :, row_idx]
    cidx = jnp.arange(GRID_W)
    c0 = jnp.clip(cidx - NA_COLS // 2, 0, GRID_W - NA_COLS)
    col_ok = (cidx[None, :] >= c0[:, None]) & (cidx[None, :] < c0[:, None] + NA_COLS)
    roff = row_idx - r[:, None] + NA_ROWS_MAX - 1
    coff = jnp.clip(cidx[None, :] - cidx[:, None], -(NA_COLS - 1), NA_COLS - 1) + NA_COLS - 1
    bias = rpb_l[:, roff[:, None, :, None], coff[None, :, None, :]]
    s_win = (jnp.einsum('bhrqd,bhrikd->bhrqik', qg, k_band).astype(jnp.float32) * scale
             + bias.astype(jnp.float32)[None])
    s_win = jnp.where(col_ok[:, None, :], s_win, -jnp.inf)
    s_ctx = jnp.einsum('bhrqd,bhtd->bhrqt', qg, kch).astype(jnp.float32) * scale
    nwin = wr * GRID_W
    p = jax.nn.softmax(jnp.concatenate(
        [s_win.reshape(b, hh, rows_n, GRID_W, nwin), s_ctx], axis=-1), axis=-1).astype(vh.dtype)
    o = (jnp.einsum('bhrqik,bhrikd->bhrqd',
                    p[..., :nwin].reshape(b, hh, rows_n, GRID_W, wr, GRID_W), v_band)
         + jnp.einsum('bhrqt,bhtd->bhrqd', p[..., nwin:], vch))
    out = from_heads(o.reshape(b, hh, s, d))
    out_c = None
    if need_ctx:
        out_c = from_heads(dense_attention(to_heads(qc, C_HEADS), kch, vch))
    return out, out_c


def modulate(h, g, shift, scale):
    return _rms(h, g) * (1.0 + scale) + shift


def swiglu(f, wg, wu, wd):
    return (jax.nn.silu(f @ wg) * (f @ wu)) @ wd


def setup_inputs(seed: int = 0) -> dict:
    key = jax.random.key(seed)
    ks = jax.random.split(key, 23)
    f32 = jnp.float32

    def nrm(k, shape, scale):
        return jax.random.normal(k, shape, f32) * scale

    def gain(k, shape):
        return 1.0 + 0.05 * jax.random.normal(k, shape, f32)

    return {
        'x': nrm(ks[0], (BATCH, SEQ, D_MODEL), 1.0),
        'c': nrm(ks[1], (BATCH, D_MODEL), 1.0),
        'ctx': nrm(ks[2], (BATCH, CTX_LEN, D_MODEL), 1.0),
        'c_ctx': nrm(ks[3], (D_MODEL,), 1.0),
        'w_mod': nrm(ks[4], (DEPTH, D_MODEL, 6 * D_MODEL), 0.5 * D_MODEL ** -0.5),
        'b_mod': nrm(ks[5], (DEPTH, 6 * D_MODEL), 0.01),
        'g_pre1': gain(ks[6], (DEPTH, D_MODEL)),
        'g_post1': gain(ks[7], (DEPTH, D_MODEL)),
        'g_pre2': gain(ks[8], (DEPTH, D_MODEL)),
        'g_post2': gain(ks[9], (DEPTH, D_MODEL)),
        'w_in': nrm(ks[10], (DEPTH, D_MODEL, IN_COLS), D_MODEL ** -0.5),
        'w_out': nrm(ks[11], (DEPTH, MIX_OUT, D_MODEL), MIX_OUT ** -0.5),
        'lam_q1': nrm(ks[12], (DEPTH, A_QK_DIM), 0.1),
        'lam_k1': nrm(ks[13], (DEPTH, A_QK_DIM), 0.1),
        'lam_q2': nrm(ks[14], (DEPTH, A_QK_DIM), 0.1),
        'lam_k2': nrm(ks[15], (DEPTH, A_QK_DIM), 0.1),
        'g_diff': gain(ks[16], (DEPTH, HEAD_DIM)),
        'g_qn': gain(ks[17], (DEPTH, HEAD_DIM)),
        'g_kn': gain(ks[18], (DEPTH, HEAD_DIM)),
        'rpb': nrm(ks[19], (DEPTH, C_HEADS, 2 * NA_ROWS_MAX - 1, 2 * NA_COLS - 1), 0.02),
        'w_gate': nrm(ks[20], (DEPTH, D_MODEL, D_FF), D_MODEL ** -0.5),
        'w_up': nrm(ks[21], (DEPTH, D_MODEL, D_FF), D_MODEL ** -0.5),
        'w_down': nrm(ks[22], (DEPTH, D_FF, D_MODEL), D_FF ** -0.5),
    }


def reference(x, c, ctx, c_ctx, w_mod, b_mod, g_pre1, g_post1, g_pre2, g_post2,
              w_in, w_out, lam_q1, lam_k1, lam_q2, lam_k2, g_diff, g_qn, g_kn, rpb,
              w_gate, w_up, w_down):
    s = x.shape[1]
    t = jnp.arange(s)
    row, col = t // GRID_W, t % GRID_W
    cond = jax.nn.silu(c)[:, None, :]
    cond_c = jax.nn.silu(c_ctx)[None, None, :]
    h, hc = x, ctx
    for l in range(DEPTH):
        need_ctx = l < DEPTH - 1
        sh1, sc1, gt1, sh2, sc2, gt2 = jnp.split(cond @ w_mod[l] + b_mod[l], 6, axis=-1)
        sh1c, sc1c, gt1c, sh2c, sc2c, gt2c = jnp.split(cond_c @ w_mod[l] + b_mod[l], 6, axis=-1)
        qa, ka, va, qb, kb, vb, qn, kn, vn = split_cols(modulate(h, g_pre1[l], sh1, sc1) @ w_in[l])
        qac, kac, vac, qbc, kbc, vbc, qnc, knc, vnc = split_cols(
            modulate(hc, g_pre1[l], sh1c, sc1c) @ w_in[l])
        oa, oac = diff_mixer(qa, ka, va, qac, kac, vac, row, col, lam_q1[l], lam_k1[l],
                             lam_q2[l], lam_k2[l], g_diff[l], l, need_ctx)
        ob, obc = gqa_mixer(qb, kb, vb, qbc, kbc, vbc, row, col, g_qn[l], g_kn[l], need_ctx)
        on, onc = na_mixer(qn, kn, vn, qnc, knc, vnc, rpb[l], need_ctx)
        h = h + gt1 * _rms(jnp.concatenate([oa, ob, on], axis=-1) @ w_out[l], g_post1[l])
        h = h + gt2 * _rms(swiglu(modulate(h, g_pre2[l], sh2, sc2), w_gate[l], w_up[l], w_down[l]),
                           g_post2[l])
        if need_ctx:
            hc = hc + gt1c * _rms(jnp.concatenate([oac, obc, onc], axis=-1) @ w_out[l], g_post1[l])
            hc = hc + gt2c * _rms(swiglu(modulate(hc, g_pre2[l], sh2c, sc2c),
                                         w_gate[l], w_up[l], w_down[l]), g_post2[l])
    return h
```

```python
import math
import numpy as np
import concourse.bass as bass
import concourse.mybir as mybir
from concourse.bass_utils import run_bass_kernel_spmd
from contextlib import ExitStack

F32 = mybir.dt.float32
BF16 = mybir.dt.bfloat16
AF = mybir.ActivationFunctionType
ALU = mybir.AluOpType

P = 128
D = 2048
KC = D // P
S = 4096
CTX = 256
T = S + CTX
DEPTH = 2
GRID_W = 64
DFF = 5632
FC = DFF // P
INC = 4608
EPS = 1e-6
NCH = T // P
N_CORES = 4

BLOCKS = [(i * 512, 512) for i in range(8)] + [(4096, 256)]
RANGES = [[0, 1, 2, 3], [4, 5, 6, 7, 8]]

DEBUG = {}


class Trk:
    __slots__ = ("w", "r", "dsem", "excl")

    def __init__(self, excl=False):
        self.w = None
        self.r = {}
        self.dsem = None
        self.excl = excl


class Sync:
    CE = ("pe", "act", "dve", "pool")

    def __init__(self, nc, n_dma_sems=80):
        self.nc = nc
        self.eng = {"pe": nc.tensor, "act": nc.scalar, "dve": nc.vector, "pool": nc.gpsimd, "sp": nc.sync}
        self.sem = {e: nc.alloc_semaphore(f"prog_{e}") for e in self.CE}
        self.cnt = {e: 0 for e in self.CE}
        self.known = {e: {} for e in self.eng}
        self.free_dsems = [nc.alloc_semaphore(f"dsem{i}") for i in range(n_dma_sems)]
        self.dcnt = {}
        self.dsem_by_num = {}
        self.n_wait = 0
        self.n_ins = 0

    def _wait(self, e, waits):
        eng = self.eng[e]
        kn = self.known[e]
        for num, (sem, val) in waits.items():
            if kn.get(num, 0) >= val:
                continue
            eng.wait_ge(sem, val)
            kn[num] = val
            self.n_wait += 1

    @staticmethod
    def _need(waits, tk):
        if tk is None:
            return
        sem, val = tk
        cur = waits.get(sem.num)
        if cur is None or cur[1] < val:
            waits[sem.num] = (sem, val)

    def _collect(self, e, reads, writes):
        waits = {}
        own = self.sem[e].num if e in self.sem else None
        for t in reads:
            self._need(waits, t.w)
            if t.excl:
                for num, tk in t.r.items():
                    if num != own:
                        self._need(waits, tk)
        for t in writes:
            self._need(waits, t.w)
            for tk in t.r.values():
                self._need(waits, tk)
        if e == "pe":
            waits.pop(self.sem["pe"].num, None)
        return waits

    @staticmethod
    def _mark(tk, reads, writes):
        sem, val = tk
        for t in reads:
            cur = t.r.get(sem.num)
            if cur is None or cur[1] < val:
                t.r[sem.num] = tk
        for t in writes:
            t.w = tk
            t.r = {}

    def op(self, e, fn, reads=(), writes=(), signal=True):
        self._wait(e, self._collect(e, reads, writes))
        ins = fn(self.eng[e])
        self.n_ins += 1
        if signal:
            self.cnt[e] += 1
            ins.then_inc(self.sem[e], 1)
            tk = (self.sem[e], self.cnt[e])
        else:
            tk = (self.sem[e], self.cnt[e] + 1)
        self._mark(tk, reads, writes)
        return ins

    def _dsem(self, t):
        if t.dsem is None:
            t.dsem = self.free_dsems.pop()
            self.dsem_by_num[t.dsem.num] = t.dsem
            self.dcnt.setdefault(t.dsem.num, 0)
        return t.dsem

    def release(self, trks):
        for t in trks:
            if t.dsem is not None:
                self.free_dsems.append(t.dsem)
                t.dsem = None

    def load(self, q, trks, out_ap, in_ap, **kw):
        self._wait(q, self._collect(q, (), trks))
        sem = self._dsem(trks[0])
        ins = self.eng[q].dma_start(out=out_ap, in_=in_ap, **kw)
        self.dcnt[sem.num] += 1
        ins.then_inc(sem, 16)
        self.n_ins += 1
        self._mark((sem, 16 * self.dcnt[sem.num]), (), trks)

    def store(self, q, trks, out_ap, in_ap, **kw):
        self._wait(q, self._collect(q, trks, ()))
        sem = self._dsem(trks[0])
        ins = self.eng[q].dma_start(out=out_ap, in_=in_ap, **kw)
        self.dcnt[sem.num] += 1
        ins.then_inc(sem, 16)
        self.n_ins += 1
        self._mark((sem, 16 * self.dcnt[sem.num]), trks, ())

    def barrier(self):
        waits = {}
        for e in self.CE:
            if self.cnt[e] > 0:
                waits[self.sem[e].num] = (self.sem[e], self.cnt[e])
        for num, c in self.dcnt.items():
            if c > 0:
                waits[num] = (self.dsem_by_num[num], 16 * c)
        for e in self.eng:
            self._wait(e, dict(waits))


def _rope_tables():
    t = np.arange(S)
    row = (t // GRID_W).astype(np.float32)
    col = (t % GRID_W).astype(np.float32)

    def tab(d):
        half = d // 2
        quarter = half // 2
        inv = (np.float32(10000.0) ** (-np.arange(quarter, dtype=np.float32) / np.float32(quarter))).astype(np.float32)
        C = np.zeros((P, S), np.float32)
        Sg = np.zeros((P, S), np.float32)
        PM = np.zeros((P, P), np.float32)
        for p in range(P):
            m = p % d
            pos = row if (m // half) == 0 else col
            j = m % half
            i = j % quarter
            second = j // quarter
            ang = (pos * inv[i]).astype(np.float32)
            C[p] = np.cos(ang)
            Sg[p] = np.sin(ang) * (-1.0 if second == 0 else 1.0)
            partner = p + quarter if second == 0 else p - quarter
            PM[partner, p] = 1.0
        return C, Sg, PM

    return tab(64), tab(128)


def _na_index_tables():
    idx = np.zeros((3, 8, P, 512), np.int64)
    val = np.zeros((3, 8, P, 512), bool)
    for v, j in enumerate((0, 3, 7)):
        for i in range(8):
            kc = 4 * j - 2 + i
            kk = np.arange(P)
            krow = 2 * kc + kk // 64
            kcol = kk % 64
            qq = np.arange(512)
            qrow = 8 * j + qq // 64
            qcol = qq % 64
            r0 = np.clip(qrow - 4, 0, 64 - 8)
            c0 = np.clip(qcol - 8, 0, 64 - 16)
            rok = (krow[:, None] >= r0[None, :]) & (krow[:, None] < r0[None, :] + 8)
            cok = (kcol[:, None] >= c0[None, :]) & (kcol[:, None] < c0[None, :] + 16)
            roff = krow[:, None] - qrow[None, :] + 7
            coff = np.clip(kcol[:, None] - qcol[None, :], -15, 15) + 15
            ok = rok & cok & (krow[:, None] >= 0) & (krow[:, None] < 64)
            idx[v, i] = np.where(ok, roff * 31 + coff, 0)
            val[v, i] = ok
    return idx, val


def build_program():
    nc = bass.Bass("TRN2", target_bir_lowering=False)
    dt = nc.dram_tensor

    def din(name, shape, dtype=F32):
        return dt(name, list(shape), dtype, kind="ExternalInput").ap()

    x_in = din("x", [S, D])
    ctx_in = din("ctx", [CTX, D])
    cc_in = din("cc", [2, D])
    w_mod = din("w_mod", [DEPTH, D, 6 * D])
    b_mod = din("b_mod", [DEPTH, 6 * D])
    gains = din("gains", [4, DEPTH, D])
    w_in = din("w_in", [DEPTH, D, INC])
    w_out = din("w_out", [DEPTH, D, D])
    lam_in = din("lam", [4, DEPTH, 64])
    hg_in = din("hg", [3, DEPTH, P])
    w_gate = din("w_gate", [DEPTH, D, DFF])
    w_up = din("w_up", [DEPTH, D, DFF])
    w_down = din("w_down", [DEPTH, DFF, D])
    rope_in = din("rope", [4, P, S])
    pm_in = din("pm", [2, P, P])
    nab_in = din("nab", [DEPTH, 4, 3, 8, P, 512])
    ident_in = din("ident", [P, P])
    out = dt("out", [S, D], F32, kind="ExternalOutput").ap()

    hT = dt("hT", [D, T], F32, kind="Internal").ap()
    yT = dt("yT", [D, T], F32, kind="Internal").ap()
    qkT = dt("qkT", [26 * P, T], BF16, kind="Internal").ap()
    vS = dt("vS", [10, P, NCH * P], BF16, kind="Internal").ap()
    hidT = dt("hidT", [DFF, T], BF16, kind="Internal").ap()

    dbg = {}
    for name, shape, dtype in DEBUG.get("outs", []):
        dbg[name] = dt(name, list(shape), dtype, kind="ExternalOutput").ap()

    sy = Sync(nc)
    stop_after = DEBUG.get("stop_after")

    with ExitStack() as top:
        uid = [0]

        def sb(stack, name, shape, dtype):
            uid[0] += 1
            return stack.enter_context(nc.sbuf_tensor(f"s{uid[0]}_{name}", list(shape), dtype))

        ps_h = [top.enter_context(nc.psum_tensor(f"ps{i}", [P, 512], F32)) for i in range(8)]
        PS = [Trk(excl=True) for _ in range(8)]

        ident = sb(top, "ident", [P, P], F32)
        ones_bf = sb(top, "ones_bf", [P, P], BF16)
        ones_f = sb(top, "ones_f", [P, P], F32)
        pm = sb(top, "pm", [P, 2, P], F32)
        cst = Trk()
        sy.load("sp", [cst], ident[:], ident_in[:, :])
        sy.load("sp", [cst], pm[:], pm_in.rearrange("a p q -> p a q"))
        sy.op("dve", lambda e: e.memset(ones_bf[:], 1.0), writes=[cst])
        sy.op("dve", lambda e: e.memset(ones_f[:], 1.0), writes=[cst])
        pmb = sb(top, "pmb", [P, 2, P], BF16)
        sy.op("dve", lambda e: e.tensor_copy(pmb[:], pm[:]), reads=[cst], writes=[cst])
        modb = sb(top, "modb", [P, 96], F32)
        modc = sb(top, "modc", [P, 96], F32)
        der = {n: [sb(top, f"{n}{i}", [P, KC], F32) for i in range(2)]
               for n in ("a1", "gt1g", "a2", "gt2g")}
        gT = sb(top, "gT", [P, 4, KC], F32)
        hgT = sb(top, "hgT", [P, 3], F32)
        gdT = sb(top, "gdT", [P, 1], F32)
        nlam = sb(top, "nlam", [P, 1], F32)
        small = Trk()

        sy.barrier()

        def stage_transpose_in():
            with ExitStack() as st:
                xin = [sb(st, f"xin{i}", [P, 4, D], F32) for i in range(2)]
                xin_t = [Trk() for _ in range(2)]
                hb = [sb(st, f"hbo{i}", [P, KC, 512], F32) for i in range(2)]
                hb_t = [[Trk() for _ in range(KC)] for _ in range(2)]
                pi = 0
                for bi, (t0, n) in enumerate(BLOCKS):
                    nt = n // P
                    X, Xt = xin[bi % 2], xin_t[bi % 2]
                    H, Ht = hb[bi % 2], hb_t[bi % 2]
                    if t0 < S:
                        src = x_in[t0:t0 + n, :]
                    else:
                        src = ctx_in[:, :]
                    sy.load("sp", [Xt], X[:, 0:nt, :], src.rearrange("(j p) f -> p j f", p=P))
                    for k in range(KC):
                        ps, pst = ps_h[pi % 8], PS[pi % 8]
                        pi += 1
                        for j in range(nt):
                            sy.op("pe", lambda e, j=j, k=k, ps=ps, X=X: e.transpose(
                                ps[:, j * P:(j + 1) * P], X[:, j, k * P:(k + 1) * P], ident[:]),
                                reads=[Xt, cst], writes=[pst], signal=(j == nt - 1))
                        eng = "act" if k % 2 == 0 else "dve"
                        if eng == "act":
                            sy.op("act", lambda e, ps=ps, H=H, k=k, n=n: e.copy(H[:, k, 0:n], ps[:, 0:n]),
                                  reads=[pst], writes=[Ht[k]])
                        else:
                            sy.op("dve", lambda e, ps=ps, H=H, k=k, n=n: e.tensor_copy(H[:, k, 0:n], ps[:, 0:n]),
                                  reads=[pst], writes=[Ht[k]])
                    sy.store("sp", Ht, hT[:, t0:t0 + n].rearrange("(k p) t -> p k t", p=P), H[:, :, 0:n])
                sy.barrier()
                sy.release(xin_t + [t for l in hb_t for t in l])

        def stage_mod(l):
            with ExitStack() as st:
                cT = sb(st, "cT", [P, 2, KC], F32)
                scf = sb(st, "scf", [P, KC, 2], F32)
                scb = sb(st, "scb", [P, KC, 2], BF16)
                bmT = sb(st, "bmT", [P, 96], F32)
                lamc = sb(st, "lamc", [64, 4], F32)
                prod = sb(st, "prod", [64, 2], F32)
                ex = sb(st, "ex", [P, 2], F32)
                tmp = sb(st, "tmpm", [P, KC], F32)
                wm = [sb(st, f"wm{i}", [P, KC, 512], BF16) for i in range(2)]
                wm_t = [Trk() for _ in range(2)]
                tk = Trk()
                with nc.allow_non_contiguous_dma(reason="tiny transposed vector loads"):
                    sy.load("sp", [tk], cT[:], cc_in.rearrange("a (k p) -> p a k", p=P))
                    sy.load("sp", [tk], bmT[:], b_mod[l].rearrange("(c p) -> p c", p=P))
                    for a_ in range(4):
                        sy.load("sp", [small], gT[:, a_, :], gains[a_, l, :].rearrange("(k p) -> p k", p=P))
                    sy.load("sp", [small], hgT[:], hg_in[:, l, :].rearrange("a p -> p a"))
                    sy.load("sp", [tk], lamc[:], lam_in[:, l, :].rearrange("a p -> p a"))
                for a in range(2):
                    sy.op("act", lambda e, a=a: e.activation(scf[:, :, a], cT[:, a, :], AF.Silu),
                          reads=[tk], writes=[tk])
                sy.op("dve", lambda e: e.tensor_copy(scb[:], scf[:]), reads=[tk], writes=[tk])
                lam_init = 0.8 - 0.6 * math.exp(-0.3 * l)
                sy.op("dve", lambda e: e.tensor_tensor(prod[:, 0:1], lamc[:, 0:1], lamc[:, 1:2], ALU.mult),
                      reads=[tk], writes=[tk])
                sy.op("dve", lambda e: e.tensor_tensor(prod[:, 1:2], lamc[:, 2:3], lamc[:, 3:4], ALU.mult),
                      reads=[tk], writes=[tk])
                sy.op("pe", lambda e: e.matmul(ps_h[7][:, 0:2], lhsT=ones_f[0:64, :], rhs=prod[:, :],
                                               start=True, stop=True), reads=[tk, cst], writes=[PS[7]])
                sy.op("act", lambda e: e.activation(ex[:], ps_h[7][:, 0:2], AF.Exp), reads=[PS[7]], writes=[tk])
                sy.op("dve", lambda e: e.tensor_tensor(nlam[:], ex[:, 1:2], ex[:, 0:1], ALU.subtract),
                      reads=[tk], writes=[small])
                sy.op("dve", lambda e: e.tensor_scalar(nlam[:], nlam[:], -lam_init, None, ALU.add),
                      reads=[small], writes=[small])
                sy.op("dve", lambda e: e.tensor_scalar(gdT[:], hgT[:, 0:1], 1.0 - lam_init, None, ALU.mult),
                      reads=[small], writes=[small])
                NG = 24
                sy.load("pool", [wm_t[0]], wm[0][:], w_mod[l, :, 0:512].rearrange("(k p) n -> p k n", p=P))
                for g in range(NG):
                    if g + 1 < NG:
                        sy.load("pool", [wm_t[(g + 1) % 2]], wm[(g + 1) % 2][:],
                                w_mod[l, :, (g + 1) * 512:(g + 2) * 512].rearrange("(k p) n -> p k n", p=P))
                    W, Wt = wm[g % 2], wm_t[g % 2]
                    for j in range(4):
                        c = 4 * g + j
                        for k in range(KC):
                            sy.op("pe", lambda e, W=W, j=j, k=k, c=c: e.matmul(
                                ps_h[6][:, 2 * c:2 * c + 2], lhsT=W[:, k, j * P:(j + 1) * P], rhs=scb[:, k, :],
                                start=(k == 0), stop=(k == KC - 1)),
                                reads=[Wt, tk], writes=[PS[6]], signal=(k == KC - 1))
                psv = ps_h[6][:, 0:192].rearrange("p (c a) -> p c a", a=2)
                sy.op("dve", lambda e: e.tensor_tensor(modb[:], psv[:, :, 0], bmT[:], ALU.add),
                      reads=[PS[6], tk], writes=[small])
                sy.op("dve", lambda e: e.tensor_tensor(modc[:], psv[:, :, 1], bmT[:], ALU.add),
                      reads=[PS[6], tk], writes=[small])
                for i, mod in enumerate((modb, modc)):
                    for nm, sc_off, gidx in (("a1", 16, 0), ("a2", 64, 2)):
                        sy.op("dve", lambda e, mod=mod, sc_off=sc_off: e.tensor_scalar(
                            tmp[:], mod[:, sc_off:sc_off + 16], 1.0, None, ALU.add), reads=[small, tk], writes=[tk])
                        sy.op("dve", lambda e, nm=nm, i=i, gidx=gidx: e.tensor_tensor(
                            der[nm][i][:], tmp[:], gT[:, gidx, :], ALU.mult), reads=[tk, small], writes=[small])
                    for nm, gt_off, gidx in (("gt1g", 32, 1), ("gt2g", 80, 3)):
                        sy.op("dve", lambda e, nm=nm, i=i, gidx=gidx, mod=mod, gt_off=gt_off: e.tensor_tensor(
                            der[nm][i][:], mod[:, gt_off:gt_off + 16], gT[:, gidx, :], ALU.mult),
                            reads=[small], writes=[small])
                sy.barrier()
                sy.release([tk] + wm_t)

        def ssq_rstd(st_tiles, src_fn, srck_trks, n, d_count, ps_idx):
            sq, sq_t = st_tiles["sq"], st_tiles["sq_t"]
            ps, pst = ps_h[ps_idx], PS[ps_idx]
            for k in range(KC):
                s_, s_t = sq[k % len(sq)], sq_t[k % len(sq)]
                sy.op("act", lambda e, s_=s_, k=k: e.activation(s_[:, 0:n], src_fn(k), AF.Square),
                      reads=[srck_trks[k]], writes=[s_t])
                sy.op("pe", lambda e, s_=s_, k=k: e.matmul(ps[:, 0:n], lhsT=ones_bf[:], rhs=s_[:, 0:n],
                                                          start=(k == 0), stop=(k == KC - 1)),
                      reads=[s_t, cst], writes=[pst], signal=True)
            rt, rt_t = st_tiles["rt"], st_tiles["rt_t"]
            sy.op("act", lambda e: e.activation(rt[:, 0:n], ps[:, 0:n], AF.Sqrt, bias=st_tiles["eps"][:, 0:1],
                                                scale=1.0 / d_count),
                  reads=[pst, st_tiles["eps_t"]], writes=[rt_t])
            sy.op("dve", lambda e: e.reciprocal(rt[:, 0:n], rt[:, 0:n]), reads=[rt_t], writes=[rt_t])
            return rt, rt_t

        def stage_norm(l, blocks, XT, XT_t, resid, modn):
            with ExitStack() as st:
                NB = 2
                Hs = [sb(st, f"nH{i}", [P, KC, 256], F32) for i in range(NB)]
                H_t = [[Trk() for _ in range(KC)] for _ in range(NB)]
                Ys = [sb(st, f"nY{i}", [P, KC, 256], F32) for i in range(NB)] if resid else None
                Y_t = [[Trk() for _ in range(KC)] for _ in range(NB)]
                tiles = {
                    "sq": [sb(st, f"nsq{i}", [P, 512], BF16) for i in range(3)],
                    "sq_t": [Trk() for _ in range(3)],
                    "rt": None, "rt_t": None,
                    "eps": sb(st, "neps", [P, 1], F32),
                }
                eps_t = Trk()
                sy.op("dve", lambda e: e.memset(tiles["eps"][:], EPS), writes=[eps_t])
                tiles["eps_t"] = eps_t
                rts = [sb(st, f"nrt{i}", [P, 512], F32) for i in range(4)]
                rt_ts = [Trk() for _ in range(4)]
                tmps = [sb(st, f"ntmp{i}", [P, 512], F32) for i in range(4)]
                tmp_ts = [Trk() for _ in range(4)]
                ri = 0
                ti = 0
                x0 = blocks[0][0] if XT is not None else 0
                subs = []
                for (bt0, bn) in blocks:
                    for o in range(0, bn, 256):
                        subs.append((bt0 + o, 256, bt0))
                for bi, (t0, n, kt0) in enumerate(subs):
                    is_ctx = 1 if t0 >= S else 0
                    H, Ht = Hs[bi % NB], H_t[bi % NB]
                    sy.load("sp", Ht, H[:, :, 0:n], hT[:, t0:t0 + n].rearrange("(k p) t -> p k t", p=P))
                    if resid:
                        Y, Yt = Ys[bi % NB], Y_t[bi % NB]
                        sy.load("sp", Yt, Y[:, :, 0:n], yT[:, t0:t0 + n].rearrange("(k p) t -> p k t", p=P))
                        tiles["rt"], tiles["rt_t"] = rts[ri % 4], rt_ts[ri % 4]
                        ri += 1
                        r1, r1t = ssq_rstd(tiles, lambda k, Y=Y: Y[:, k, 0:n], Yt, n, D, 6)
                        gtg = der[resid][is_ctx]
                        for k in range(KC):
                            tm, tmt = tmps[ti % 4], tmp_ts[ti % 4]
                            ti += 1
                            sy.op("dve", lambda e, tm=tm, Y=Y, k=k, r1=r1: e.scalar_tensor_tensor(
                                tm[:, 0:n], Y[:, k, 0:n], gtg[:, k:k + 1], r1[:, 0:n], ALU.mult, ALU.mult),
                                reads=[Yt[k], r1t, small], writes=[tmt])
                            sy.op("pool", lambda e, tm=tm, H=H, k=k: e.tensor_tensor(
                                H[:, k, 0:n], tm[:, 0:n], H[:, k, 0:n], ALU.add),
                                reads=[tmt, Ht[k]], writes=[Ht[k]])
                    if modn:
                        tiles["rt"], tiles["rt_t"] = rts[ri % 4], rt_ts[ri % 4]
                        ri += 1
                        r2, r2t = ssq_rstd(tiles, lambda k, H=H: H[:, k, 0:n], Ht, n, D, 7)
                        a = der["a1" if modn == 1 else "a2"][is_ctx]
                        mod = modc if is_ctx else modb
                        sh_off = 0 if modn == 1 else 48
                        for k in range(KC):
                            tm, tmt = tmps[ti % 4], tmp_ts[ti % 4]
                            ti += 1
                            sy.op("dve", lambda e, tm=tm, H=H, k=k, r2=r2: e.tensor_tensor(
                                tm[:, 0:n], H[:, k, 0:n], r2[:, 0:n], ALU.mult),
                                reads=[Ht[k], r2t], writes=[tmt])
                            sy.op("act", lambda e, tm=tm, k=k, a=a, mod=mod: e.activation(
                                XT[:, k, t0 - x0:t0 - x0 + n], tm[:, 0:n], AF.Identity,
                                bias=mod[:, sh_off + k:sh_off + k + 1], scale=a[:, k:k + 1]),
                                reads=[tmt, small], writes=[XT_t[(k, kt0)]])
                    if resid:
                        sy.store("sp", Ht, hT[:, t0:t0 + n].rearrange("(k p) t -> p k t", p=P), H[:, :, 0:n])
                sy.barrier()
                sy.release([t for l_ in H_t for t in l_] + [t for l_ in Y_t for t in l_])

        def run_pending(pending):
            alive = []
            for g in pending:
                try:
                    next(g)
                    alive.append(g)
                except StopIteration:
                    pass
            pending[:] = alive

        def drain(pending):
            while pending:
                run_pending(pending)

        def inproj_plan(c):
            if c < 4:
                return ("rope64", c)
            if c < 8:
                return ("rope64", c)
            if c < 12:
                return ("v", c - 8)
            if c < 20:
                return ("nrope_q", 8 + (c - 12))
            if c < 22:
                return ("nrope_k", 16 + (c - 20))
            if c < 24:
                return ("v", 4 + (c - 22))
            if c < 28:
                return ("plain", 18 + (c - 24))
            if c < 32:
                return ("plain", 22 + (c - 28))
            return ("v", 6 + (c - 32))

        def stage_inproj(l, blocks, XT, XT_t):
            x0 = blocks[0][0]
            with ExitStack() as st:
                wt = [sb(st, f"wi{i}", [P, KC, 512], BF16) for i in range(2)]
                wt_t = [Trk() for _ in range(2)]
                rp = [sb(st, f"rp{i}", [P, 2, 512], F32) for i in range(2)]
                rp_t = [Trk() for _ in range(2)]
                NR = 6
                xf = [sb(st, f"xf{i}", [P, 512], F32) for i in range(NR)]
                xf_t = [Trk() for _ in range(NR)]
                xb = [sb(st, f"xb{i}", [P, 512], BF16) for i in range(NR)]
                xb_t = [Trk() for _ in range(NR)]
                sq = [sb(st, f"isq{i}", [P, 512], BF16) for i in range(2)]
                sq_t = [Trk() for _ in range(2)]
                rt = [sb(st, f"irt{i}", [P, 512], F32) for i in range(2)]
                rt_t = [Trk() for _ in range(2)]
                t1 = [sb(st, f"it1{i}", [P, 512], F32) for i in range(NR)]
                t1_t = [Trk() for _ in range(NR)]
                t2 = [sb(st, f"it2{i}", [P, 512], F32) for i in range(NR)]
                t2_t = [Trk() for _ in range(NR)]
                ob = [sb(st, f"iob{i}", [P, 512], BF16) for i in range(NR)]
                ob_t = [Trk() for _ in range(NR)]
                epsT = sb(st, "ieps", [P, 1], F32)
                eps_t = Trk()
                sy.op("dve", lambda e: e.memset(epsT[:], EPS), writes=[eps_t])
                cnt = {"main": 0, "aux": 0, "buf": 0, "rp": 0}
                pending = []

                def w_src(g):
                    return w_in[l, :, g * 512:(g + 1) * 512].rearrange("(k p) n -> p k n", p=P)

                def epi_fm(kind, dest, ps, pst, t0, n, rpi):
                    i = cnt["buf"] % NR
                    cnt["buf"] += 1
                    O, Ot = ob[i], ob_t[i]
                    latent = t0 < S
                    dst = qkT[dest * P:(dest + 1) * P, t0:t0 + n]
                    if DEBUG.get("inproj_mode") == "plain_all":
                        kind = "plain"
                    if DEBUG.get("inproj_mode") == "no_rope" and kind == "rope64":
                        kind = "plain"
                    if DEBUG.get("inproj_mode") == "no_nrope" and kind.startswith("nrope"):
                        kind = "plain"
                    if kind == "plain" or (kind == "rope64" and not latent):
                        sy.op("act", lambda e: e.copy(O[:, 0:n], ps[:, 0:n]), reads=[pst], writes=[Ot])
                        sy.store("sp", [Ot], dst, O[:, 0:n])
                        return
                    X, Xt_ = xf[i], xf_t[i]
                    XB, XBt = xb[i], xb_t[i]
                    if kind == "rope64":
                        sy.op("act", lambda e: e.copy(XB[:, 0:n], ps[:, 0:n]), reads=[pst], writes=[XBt])
                        tsel, pmi = 0, 0
                        xsrc, xsrc_t = ps, pst
                    else:
                        s_, s_t = sq[cnt["aux"] % 2], sq_t[cnt["aux"] % 2]
                        r_, r_t = rt[cnt["aux"] % 2], rt_t[cnt["aux"] % 2]
                        pa, pat = ps_h[6 + cnt["aux"] % 2], PS[6 + cnt["aux"] % 2]
                        cnt["aux"] += 1
                        sy.op("act", lambda e: e.activation(s_[:, 0:n], ps[:, 0:n], AF.Square),
                              reads=[pst], writes=[s_t])
                        yield
                        sy.op("pe", lambda e: e.matmul(pa[:, 0:n], lhsT=ones_bf[:], rhs=s_[:, 0:n],
                                                       start=True, stop=True), reads=[s_t, cst], writes=[pat])
                        sy.op("act", lambda e: e.activation(r_[:, 0:n], pa[:, 0:n], AF.Sqrt, bias=epsT[:, 0:1],
                                                            scale=1.0 / P), reads=[pat, eps_t], writes=[r_t])
                        sy.op("dve", lambda e: e.reciprocal(r_[:, 0:n], r_[:, 0:n]), reads=[r_t], writes=[r_t])
                        gcol = 1 if kind == "nrope_q" else 2
                        sy.op("act", lambda e: e.activation(X[:, 0:n], ps[:, 0:n], AF.Identity,
                                                            scale=hgT[:, gcol:gcol + 1]),
                              reads=[pst, small], writes=[Xt_])
                        sy.op("dve", lambda e: e.tensor_tensor(X[:, 0:n], X[:, 0:n], r_[:, 0:n], ALU.mult),
                              reads=[Xt_, r_t], writes=[Xt_])
                        tsel, pmi = 1, 1
                        xsrc, xsrc_t = X, Xt_
                        if not latent:
                            sy.op("act", lambda e: e.copy(O[:, 0:n], X[:, 0:n]), reads=[Xt_], writes=[Ot])
                            sy.store("sp", [Ot], dst, O[:, 0:n])
                            return
                        sy.op("act", lambda e: e.copy(XB[:, 0:n], X[:, 0:n]), reads=[Xt_], writes=[XBt])
                    yield
                    pb_i = 4 + cnt["main"] % 2
                    pb, pbt = ps_h[pb_i], PS[pb_i]
                    sy.op("pe", lambda e: e.matmul(pb[:, 0:n], lhsT=pmb[:, pmi, :], rhs=XB[:, 0:n],
                                                   start=True, stop=True), reads=[XBt, cst], writes=[pbt])
                    R, Rt = rp[rpi], rp_t[rpi]
                    A, At = t1[i], t1_t[i]
                    B, Bt = t2[i], t2_t[i]
                    sy.op("dve", lambda e: e.tensor_tensor(A[:, 0:n], xsrc[:, 0:n], R[:, 0, 0:n], ALU.mult),
                          reads=[xsrc_t, Rt], writes=[At])
                    sy.op("dve", lambda e: e.tensor_tensor(B[:, 0:n], pb[:, 0:n], R[:, 1, 0:n], ALU.mult),
                          reads=[pbt, Rt], writes=[Bt])
                    sy.op("dve", lambda e: e.tensor_tensor(O[:, 0:n], A[:, 0:n], B[:, 0:n], ALU.add),
                          reads=[At, Bt], writes=[Ot])
                    sy.store("sp", [Ot], dst, O[:, 0:n])

                NG = 9
                sy.load("pool", [wt_t[0]], wt[0][:], w_src(0))
                for g in range(NG):
                    if g + 1 < NG:
                        sy.load("pool", [wt_t[(g + 1) % 2]], wt[(g + 1) % 2][:], w_src(g + 1))
                    W, Wt = wt[g % 2], wt_t[g % 2]
                    plans = [inproj_plan(4 * g + j) for j in range(4)]
                    kinds = set(p_[0] for p_ in plans)
                    need_rope = None
                    if "rope64" in kinds:
                        need_rope = 0
                    elif "nrope_q" in kinds or "nrope_k" in kinds:
                        need_rope = 1
                    for (t0, n) in blocks:
                        rpi = None
                        if need_rope is not None and t0 < S:
                            rpi = cnt["rp"] % 2
                            cnt["rp"] += 1
                            sy.load("sp", [rp_t[rpi]], rp[rpi][:, :, 0:n],
                                    rope_in[2 * need_rope:2 * need_rope + 2, :, t0:t0 + n].rearrange("a p t -> p a t"))
                        for j, (kind, dest) in enumerate(plans):
                            if kind == "v":
                                continue
                            pi = cnt["main"] % 4
                            cnt["main"] += 1
                            ps, pst = ps_h[pi], PS[pi]
                            for k in range(KC):
                                sy.op("pe", lambda e, ps=ps, W=W, j=j, k=k: e.matmul(
                                    ps[:, 0:n], lhsT=W[:, k, j * P:(j + 1) * P],
                                    rhs=XT[:, k, t0 - x0:t0 - x0 + n], start=(k == 0), stop=(k == KC - 1)),
                                    reads=[Wt, XT_t[(k, t0)]], writes=[pst], signal=(k == KC - 1))
                            run_pending(pending)
                            gen = epi_fm(kind, dest, ps, pst, t0, n, rpi)
                            pending.append(gen)
                        vj = [j for j, p_ in enumerate(plans) if p_[0] == "v"]
                        if vj:
                            c0, c1 = vj[0] * P, (vj[-1] + 1) * P
                            ncol = c1 - c0
                            for tt in range(n // P):
                                pi = cnt["main"] % 4
                                cnt["main"] += 1
                                ps, pst = ps_h[pi], PS[pi]
                                ta = t0 - x0 + tt * P
                                for k in range(KC):
                                    sy.op("pe", lambda e, ps=ps, W=W, k=k, ta=ta: e.matmul(
                                        ps[:, 0:ncol], lhsT=XT[:, k, ta:ta + P], rhs=W[:, k, c0:c1],
                                        start=(k == 0), stop=(k == KC - 1)),
                                        reads=[Wt, XT_t[(k, t0)]], writes=[pst], signal=(k == KC - 1))
                                run_pending(pending)
                                i = cnt["buf"] % NR
                                cnt["buf"] += 1
                                O, Ot = ob[i], ob_t[i]
                                sy.op("act", lambda e, O=O, ps=ps: e.copy(O[:, 0:ncol], ps[:, 0:ncol]),
                                      reads=[pst], writes=[Ot])
                                chunk = (t0 + tt * P) // P
                                heads = [plans[j][1] for j in vj]
                                h0 = heads[0]
                                sy.store("sp", [Ot],
                                         vS[h0:h0 + len(heads), :, chunk * P:(chunk + 1) * P].rearrange("h p d -> p h d"),
                                         O[:, 0:ncol].rearrange("p (h d) -> p h d", d=P))
                drain(pending)
                sy.barrier()
                sy.release(wt_t + rp_t + ob_t)

        def stage_attention(l, blocks, XT, XT_t, with_ctx):
            x0 = blocks[0][0]
            with ExitStack() as st:
                KT = [sb(st, f"KT{i}", [P, T], BF16) for i in range(2)]
                KT_t = [Trk() for _ in range(2)]
                VT = [sb(st, f"VT{i}", [P, NCH, P], BF16) for i in range(2)]
                VT_t = [Trk() for _ in range(2)]
                QT = [sb(st, f"QT{i}", [P, 2304], BF16) for i in range(2)]
                QT_t = [Trk() for _ in range(2)]
                NE = 4
                E = [sb(st, f"E{i}", [P, 512], BF16) for i in range(NE)]
                E_t = [Trk() for _ in range(NE)]
                SBm = [sb(st, f"SBm{i}", [P, 512], F32) for i in range(2)]
                SB_t = [Trk() for _ in range(2)]
                NB = [sb(st, f"NB{i}", [P, 512], F32) for i in range(3)]
                NB_t = [Trk() for _ in range(3)]
                RD = [sb(st, f"RD{i}", [P, 512], F32) for i in range(2)]
                RD_t = [Trk() for _ in range(2)]
                O1 = sb(st, "O1", [P, 512], F32)
                O1_t = Trk()
                O2 = sb(st, "O2", [P, 512], F32)
                O2_t = Trk()
                OD = sb(st, "OD", [P, 512], F32)
                OD_t = Trk()
                SQ = sb(st, "aSQ", [P, 512], BF16)
                SQ_t = Trk()
                RT = sb(st, "aRT", [P, 512], F32)
                RT_t = Trk()
                epsT = sb(st, "aeps", [P, 1], F32)
                eps_t = Trk()
                sy.op("dve", lambda e: e.memset(epsT[:], EPS), writes=[eps_t])
                cnt = {"s": 0, "e": 0, "acc": 0, "kv": 0, "q": 0, "nb": 0, "sb": 0, "rd": 0}
                ntok = sum(n for _, n in blocks)
                qblocks = [(t0, n) for (t0, n) in blocks if (t0 < S or with_ctx)]

                def load_kv(kchunk, vhead):
                    i = cnt["kv"] % 2
                    cnt["kv"] += 1
                    sy.load("sp", [KT_t[i]], KT[i][:], qkT[kchunk * P:(kchunk + 1) * P, :])
                    sy.load("sp", [VT_t[i]], VT[i][:], vS[vhead].rearrange("p (c d) -> p c d", d=P))
                    return i

                def load_q(qchunk):
                    i = cnt["q"] % 2
                    cnt["q"] += 1
                    sy.load("sp", [QT_t[i]], QT[i][:, 0:ntok], qkT[qchunk * P:(qchunk + 1) * P, x0:x0 + ntok])
                    return i

                def core(kvi, qi, psl, t0, n, scale, na=None):
                    a = cnt["acc"] % 2
                    cnt["acc"] += 1
                    po, pot = ps_h[3 + a], PS[3 + a]
                    pd, pdt = ps_h[5 + a], PS[5 + a]
                    K_, K_t, V_, V_t = KT[kvi], KT_t[kvi], VT[kvi], VT_t[kvi]
                    Q_, Q_t = QT[qi], QT_t[qi]
                    if t0 >= S:
                        chunks = [32, 33]
                    elif na is not None:
                        j = t0 // 512
                        chunks = [c for c in range(4 * j - 2, 4 * j + 6) if 0 <= c < 32] + [32, 33]
                    else:
                        chunks = list(range(NCH))
                    q0 = t0 - x0

                    def emit_s(c):
                        si = cnt["s"] % 3
                        cnt["s"] += 1
                        ps, pst = ps_h[si], PS[si]
                        sy.op("pe", lambda e: e.matmul(ps[:, 0:n], lhsT=K_[psl, c * P:(c + 1) * P],
                                                       rhs=Q_[psl, q0:q0 + n], start=True, stop=True),
                              reads=[K_t, Q_t], writes=[pst])
                        return ps, pst

                    nxt = emit_s(chunks[0])
                    for ci, c in enumerate(chunks):
                        ps, pst = nxt
                        if ci + 1 < len(chunks):
                            nxt = emit_s(chunks[ci + 1])
                        ei = cnt["e"] % NE
                        cnt["e"] += 1
                        E_, E_t_ = E[ei], E_t[ei]
                        if na is not None and c < 32:
                            h, var = na
                            j = t0 // 512
                            i_off = c - (4 * j - 2)
                            bi = cnt["nb"] % 3
                            cnt["nb"] += 1
                            sy.load("sp", [NB_t[bi]], NB[bi][:], nab_in[l, h, var, i_off])
                            sb_i = cnt["sb"] % 2
                            cnt["sb"] += 1
                            sy.op("dve", lambda e: e.scalar_tensor_tensor(
                                SBm[sb_i][:, 0:n], ps[:, 0:n], scale, NB[bi][:, 0:n], ALU.mult, ALU.add),
                                reads=[pst, NB_t[bi]], writes=[SB_t[sb_i]])
                            sy.op("act", lambda e: e.activation(E_[:, 0:n], SBm[sb_i][:, 0:n], AF.Exp),
                                  reads=[SB_t[sb_i]], writes=[E_t_])
                        else:
                            sy.op("act", lambda e: e.activation(E_[:, 0:n], ps[:, 0:n], AF.Exp, scale=scale),
                                  reads=[pst], writes=[E_t_])
                        first, last = ci == 0, ci == len(chunks) - 1
                        sy.op("pe", lambda e: e.matmul(po[:, 0:n], lhsT=V_[:, c, :], rhs=E_[:, 0:n],
                                                       start=first, stop=last),
                              reads=[V_t, E_t_], writes=[pot], signal=False)
                        sy.op("pe", lambda e: e.matmul(pd[:, 0:n], lhsT=ones_bf[:], rhs=E_[:, 0:n],
                                                       start=first, stop=last),
                              reads=[E_t_, cst], writes=[pdt], signal=True)
                    return po, pot, pd, pdt

                def recip(pd, pdt, n):
                    i = cnt["rd"] % 2
                    cnt["rd"] += 1
                    sy.op("dve", lambda e: e.reciprocal(RD[i][:, 0:n], pd[:, 0:n]), reads=[pdt], writes=[RD_t[i]])
                    return RD[i], RD_t[i]

                def finish_plain(xchunk, po, pot, pd, pdt, t0, n):
                    R, Rt = recip(pd, pdt, n)
                    sy.op("dve", lambda e: e.tensor_tensor(XT[:, xchunk, t0 - x0:t0 - x0 + n], po[:, 0:n],
                                                           R[:, 0:n], ALU.mult),
                          reads=[pot, Rt], writes=[XT_t[(xchunk, t0)]])

                sc_a = 64 ** -0.5
                for h in range(4):
                    kvi = load_kv(4 + h, h)
                    qi = load_q(h)
                    for (t0, n) in qblocks:
                        po, pot, pd, pdt = core(kvi, qi, slice(0, 64), t0, n, sc_a)
                        R, Rt = recip(pd, pdt, n)
                        sy.op("dve", lambda e: e.tensor_tensor(O1[:, 0:n], po[:, 0:n], R[:, 0:n], ALU.mult),
                              reads=[pot, Rt], writes=[O1_t])
                        po, pot, pd, pdt = core(kvi, qi, slice(64, 128), t0, n, sc_a)
                        R, Rt = recip(pd, pdt, n)
                        sy.op("dve", lambda e: e.tensor_tensor(O2[:, 0:n], po[:, 0:n], R[:, 0:n], ALU.mult),
                              reads=[pot, Rt], writes=[O2_t])
                        sy.op("dve", lambda e: e.scalar_tensor_tensor(OD[:, 0:n], O2[:, 0:n], nlam[:, 0:1],
                                                                      O1[:, 0:n], ALU.mult, ALU.add),
                              reads=[O1_t, O2_t, small], writes=[OD_t])
                        sy.op("act", lambda e: e.activation(SQ[:, 0:n], OD[:, 0:n], AF.Square),
                              reads=[OD_t], writes=[SQ_t])
                        sy.op("pe", lambda e: e.matmul(ps_h[7][:, 0:n], lhsT=ones_bf[:], rhs=SQ[:, 0:n],
                                                       start=True, stop=True), reads=[SQ_t, cst], writes=[PS[7]])
                        sy.op("act", lambda e: e.activation(RT[:, 0:n], ps_h[7][:, 0:n], AF.Sqrt,
                                                            bias=epsT[:, 0:1], scale=1.0 / P),
                              reads=[PS[7], eps_t], writes=[RT_t])
                        sy.op("dve", lambda e: e.reciprocal(RT[:, 0:n], RT[:, 0:n]), reads=[RT_t], writes=[RT_t])
                        sy.op("dve", lambda e: e.scalar_tensor_tensor(
                            XT[:, h, t0 - x0:t0 - x0 + n], OD[:, 0:n], gdT[:, 0:1], RT[:, 0:n], ALU.mult, ALU.mult),
                            reads=[OD_t, RT_t, small], writes=[XT_t[(h, t0)]])
                sc_b = 128 ** -0.5
                for kvh in range(2):
                    kvi = load_kv(16 + kvh, 4 + kvh)
                    for gq in range(4):
                        hq = kvh * 4 + gq
                        qi = load_q(8 + hq)
                        for (t0, n) in qblocks:
                            po, pot, pd, pdt = core(kvi, qi, slice(0, P), t0, n, sc_b)
                            finish_plain(4 + hq, po, pot, pd, pdt, t0, n)
                for h in range(4):
                    kvi = load_kv(22 + h, 6 + h)
                    qi = load_q(18 + h)
                    for (t0, n) in qblocks:
                        if t0 < S:
                            j = t0 // 512
                            var = 0 if j == 0 else (2 if j == 7 else 1)
                            po, pot, pd, pdt = core(kvi, qi, slice(0, P), t0, n, sc_b, na=(h, var))
                        else:
                            po, pot, pd, pdt = core(kvi, qi, slice(0, P), t0, n, sc_b)
                        finish_plain(12 + h, po, pot, pd, pdt, t0, n)
                sy.barrier()
                sy.release(KT_t + VT_t + QT_t + NB_t)

        def stage_gemm_fm(wsrc_fn, ngroups, gcols, kchunks, blocks, XT, XT_t, x0, dstT, wname, epi="copy",
                          wsrc2_fn=None):
            nj = gcols // P
            with ExitStack() as st:
                nwb = 2
                wt = [sb(st, f"{wname}{i}", [P, kchunks, gcols], BF16) for i in range(nwb)]
                wt_t = [Trk() for _ in range(nwb)]
                if wsrc2_fn is not None:
                    wu = [sb(st, f"{wname}u{i}", [P, kchunks, gcols], BF16) for i in range(nwb)]
                    wu_t = [Trk() for _ in range(nwb)]
                NO = 4
                odt = BF16 if epi == "swiglu" else F32
                ob = [sb(st, f"{wname}o{i}", [P, 512], odt) for i in range(NO)]
                ob_t = [Trk() for _ in range(NO)]
                if epi == "swiglu":
                    sg = [sb(st, f"{wname}s{i}", [P, 512], F32) for i in range(NO)]
                    sg_t = [Trk() for _ in range(NO)]
                cnt = {"ps": 0, "o": 0}

                def issue_w(g):
                    sy.load("pool", [wt_t[g % nwb]], wt[g % nwb][:], wsrc_fn(g))
                    if wsrc2_fn is not None:
                        sy.load("pool", [wu_t[g % nwb]], wu[g % nwb][:], wsrc2_fn(g))

                issue_w(0)
                for g in range(ngroups):
                    if g + 1 < ngroups:
                        issue_w(g + 1)
                    W, Wt = wt[g % nwb], wt_t[g % nwb]
                    for (t0, n) in blocks:
                        for j in range(nj):
                            def mm(Wx, Wxt):
                                pi = cnt["ps"] % 8
                                cnt["ps"] += 1
                                ps, pst = ps_h[pi], PS[pi]
                                for k in range(kchunks):
                                    sy.op("pe", lambda e, k=k: e.matmul(
                                        ps[:, 0:n], lhsT=Wx[:, k, j * P:(j + 1) * P],
                                        rhs=XT[:, k, t0 - x0:t0 - x0 + n], start=(k == 0), stop=(k == kchunks - 1)),
                                        reads=[Wxt, XT_t[(k, t0)]], writes=[pst], signal=(k == kchunks - 1))
                                return ps, pst
                            ps, pst = mm(W, Wt)
                            oi = cnt["o"] % NO
                            cnt["o"] += 1
                            O, Ot = ob[oi], ob_t[oi]
                            row0 = g * gcols + j * P
                            if epi == "copy":
                                if oi % 2 == 0:
                                    sy.op("act", lambda e: e.copy(O[:, 0:n], ps[:, 0:n]), reads=[pst], writes=[Ot])
                                else:
                                    sy.op("dve", lambda e: e.tensor_copy(O[:, 0:n], ps[:, 0:n]), reads=[pst], writes=[Ot])
                            else:
                                ps2, pst2 = mm(wu[g % nwb], wu_t[g % nwb])
                                G, Gt = sg[oi], sg_t[oi]
                                sy.op("act", lambda e: e.activation(G[:, 0:n], ps[:, 0:n], AF.Silu),
                                      reads=[pst], writes=[Gt])
                                sy.op("dve", lambda e: e.tensor_tensor(O[:, 0:n], G[:, 0:n], ps2[:, 0:n], ALU.mult),
                                      reads=[Gt, pst2], writes=[Ot])
                            sy.store("sp", [Ot], dstT[row0:row0 + P, t0:t0 + n], O[:, 0:n])
                sy.barrier()
                sy.release(wt_t + ob_t + (wu_t if wsrc2_fn is not None else []))

        def stage_down(l, with_ctx):
            passes = [[0, 1], [2, 3], [4, 5], [6, 7] + ([8] if with_ctx else [])]
            for pblocks in passes:
                blocks = [BLOCKS[i] for i in pblocks]
                x0 = blocks[0][0]
                ntok = sum(n for _, n in blocks)
                with ExitStack() as st:
                    HX = sb(st, "HX", [P, FC, 1280], BF16)
                    HX_t = {}
                    ld = Trk()
                    for k in range(FC):
                        for (t0, n) in blocks:
                            HX_t[(k, t0)] = ld
                    lds = [Trk() for _ in range(4)]
                    for q in range(4):
                        for k in range(11 * q, 11 * q + 11):
                            for (t0, n) in blocks:
                                HX_t[(k, t0)] = lds[q]
                        sy.load("sp", [lds[q]], HX[:, 11 * q:11 * q + 11, 0:ntok],
                                hidT[11 * q * P:(11 * q + 11) * P, x0:x0 + ntok].rearrange("(k p) t -> p k t", p=P))
                    stage_gemm_fm(
                        lambda g: w_down[l, :, g * 256:(g + 1) * 256].rearrange("(k p) n -> p k n", p=P),
                        8, 256, FC, blocks, HX, HX_t, x0, yT, "wd")
                    sy.release(lds)

        def stage_transpose_out():
            with ExitStack() as st:
                hb = [sb(st, f"ohb{i}", [P, KC, 512], F32) for i in range(2)]
                hb_t = [Trk() for _ in range(2)]
                ox = [sb(st, f"oox{i}", [P, 4, D], F32) for i in range(2)]
                ox_t = [[Trk() for _ in range(16)] for _ in range(2)]
                pi = 0
                for bi, (t0, n) in enumerate(BLOCKS[:8]):
                    H, Ht = hb[bi % 2], hb_t[bi % 2]
                    OX, OXt = ox[bi % 2], ox_t[bi % 2]
                    sy.load("sp", [Ht], H[:], hT[:, t0:t0 + n].rearrange("(k p) t -> p k t", p=P))
                    for j in range(4):
                        for kg in range(4):
                            ps, pst = ps_h[pi % 8], PS[pi % 8]
                            pi += 1
                            for kk in range(4):
                                k = kg * 4 + kk
                                sy.op("pe", lambda e, ps=ps, kk=kk, k=k, j=j, H=H: e.transpose(
                                    ps[:, kk * P:(kk + 1) * P], H[:, k, j * P:(j + 1) * P], ident[:]),
                                    reads=[Ht, cst], writes=[pst], signal=(kk == 3))
                            tr = OXt[j * 4 + kg]
                            if (j * 4 + kg) % 2 == 0:
                                sy.op("act", lambda e, ps=ps, OX=OX, j=j, kg=kg: e.copy(
                                    OX[:, j, kg * 512:(kg + 1) * 512], ps[:, :]), reads=[pst], writes=[tr])
                            else:
                                sy.op("dve", lambda e, ps=ps, OX=OX, j=j, kg=kg: e.tensor_copy(
                                    OX[:, j, kg * 512:(kg + 1) * 512], ps[:, :]), reads=[pst], writes=[tr])
                    sy.store("sp", OXt, out[t0:t0 + n, :].rearrange("(j p) f -> p j f", p=P), OX[:])
                sy.barrier()
                sy.release(hb_t + [t for l_ in ox_t for t in l_])

        def dump(name, src_ap, shape, dtype):
            if name not in dbg:
                return
            rows, cols = shape
            with ExitStack() as st:
                tl = sb(st, "dbgt", [P, cols], dtype)
                tt = Trk()
                for r0 in range(0, rows, P):
                    sy.load("sp", [tt], tl[:], src_ap[r0:r0 + P, :])
                    sy.store("sp", [tt], dbg[name][r0:r0 + P, :], tl[:])
                sy.barrier()
                sy.release([tt])

        def main():
            stage_transpose_in()
            dump("d_hT0", hT, (D, T), F32)
            if stop_after == "x0":
                return
            for l in range(DEPTH):
                last = l == DEPTH - 1
                with_ctx = not last
                stage_mod(l)
                if l == 0 and "d_mod" in dbg:
                    tdm = Trk()
                    sy.store("sp", [small], dbg["d_mod"][:, 0:96], modb[:])
                    sy.store("sp", [small], dbg["d_mod"][:, 96:192], modc[:])
                    for ii, nm_ in enumerate(("a1", "gt1g", "a2", "gt2g")):
                        for jj in range(2):
                            o_ = 192 + (ii * 2 + jj) * 16
                            sy.store("sp", [small], dbg["d_mod"][:, o_:o_ + 16], der[nm_][jj][:])
                    with nc.allow_non_contiguous_dma(reason="debug"):
                        sy.store("sp", [small], dbg["d_mod"][:, 320:321], nlam[:])
                        sy.store("sp", [small], dbg["d_mod"][:, 321:322], gdT[:])
                    sy.barrier()
                if stop_after == "mod":
                    return
                for rb in RANGES:
                    blocks = [BLOCKS[i] for i in rb]
                    x0 = blocks[0][0]
                    with ExitStack() as st:
                        XT = sb(st, "XT", [P, KC, 2304], BF16)
                        XT_t = {(k, t0): Trk() for k in range(KC) for (t0, n) in blocks}
                        stage_norm(l, blocks, XT, XT_t, None, 1)
                        if stop_after == "norm1":
                            if "d_xt" in dbg:
                                for k_ in range(KC):
                                    sy.store("sp", [small], dbg["d_xt"][k_ * P:(k_ + 1) * P, :], XT[:, k_, 0:2048])
                                sy.barrier()
                            return
                        stage_inproj(l, blocks, XT, XT_t)
                if l == 0:
                    dump("d_qkT", qkT, (26 * P, T), BF16)
                    dump("d_vS", vS.rearrange("h p x -> (h p) x"), (10 * P, NCH * P), BF16)
                if stop_after == "inproj":
                    return
                for rb in RANGES:
                    blocks = [BLOCKS[i] for i in rb if (i < 8 or with_ctx)]
                    x0 = blocks[0][0]
                    with ExitStack() as st:
                        XT = sb(st, "XT", [P, KC, 2304], BF16)
                        XT_t = {(k, t0): Trk() for k in range(KC) for (t0, n) in blocks}
                        stage_attention(l, blocks, XT, XT_t, with_ctx)
                        stage_gemm_fm(
                            lambda g: w_out[l, :, g * 512:(g + 1) * 512].rearrange("(k p) n -> p k n", p=P),
                            4, 512, KC, blocks, XT, XT_t, x0, yT, "wo")
                        if l == 0 and stop_after == "outproj":
                            continue
                        stage_norm(l, blocks, XT, XT_t, "gt1g", 2)
                        stage_gemm_fm(
                            lambda g: w_gate[l, :, g * 256:(g + 1) * 256].rearrange("(k p) n -> p k n", p=P),
                            22, 256, KC, blocks, XT, XT_t, x0, hidT, "wg", epi="swiglu",
                            wsrc2_fn=lambda g: w_up[l, :, g * 256:(g + 1) * 256].rearrange("(k p) n -> p k n", p=P))
                if l == 0 and stop_after == "outproj":
                    dump("d_yT", yT, (D, T), F32)
                    return
                stage_down(l, with_ctx)
                blocks = [BLOCKS[i] for i in range(9) if (i < 8 or with_ctx)]
                stage_norm(l, blocks, None, None, "gt2g", None)
                if l == 0:
                    dump("d_hT1", hT, (D, T), F32)
                    if stop_after == "layer0":
                        return
            stage_transpose_out()

        main()
        sy.barrier()
    return nc, sy


_CACHE = {}


def _host_tables():
    if "tabs" not in _CACHE:
        (C64, S64, PM64), (C128, S128, PM128) = _rope_tables()
        rope = np.stack([C64, S64, C128, S128]).astype(np.float32)
        pmm = np.stack([PM64, PM128]).astype(np.float32)
        idx, val = _na_index_tables()
        _CACHE["tabs"] = (rope, pmm, idx, val)
    return _CACHE["tabs"]


def kernel(x, c, ctx, c_ctx, w_mod, b_mod, g_pre1, g_post1, g_pre2, g_post2,
           w_in, w_out, lam_q1, lam_k1, lam_q2, lam_k2, g_diff, g_qn, g_kn, rpb,
           w_gate, w_up, w_down):
    f = lambda a: np.ascontiguousarray(np.asarray(a, dtype=np.float32))
    x, c, ctx, c_ctx = f(x), f(c), f(ctx), f(c_ctx)
    rope, pmm, idx, val = _host_tables()
    rpb = f(rpb)
    rpb_flat = rpb.reshape(DEPTH, 4, 15 * 31)
    nab = np.where(val[None, None], rpb_flat[:, :, idx], np.float32(-30000.0)).astype(np.float32)
    shared = {
        "w_mod": f(w_mod), "b_mod": f(b_mod),
        "gains": np.stack([f(g_pre1), f(g_post1), f(g_pre2), f(g_post2)]),
        "w_in": f(w_in), "w_out": f(w_out),
        "lam": np.stack([f(lam_q1), f(lam_k1), f(lam_q2), f(lam_k2)]),
        "hg": np.stack([f(g_diff), f(g_qn), f(g_kn)]),
        "w_gate": f(w_gate), "w_up": f(w_up), "w_down": f(w_down),
        "rope": rope, "pm": pmm, "nab": np.ascontiguousarray(nab),
        "ident": np.eye(P, dtype=np.float32),
    }
    nc, _ = build_program()
    in_maps = []
    for b in range(N_CORES):
        m = dict(shared)
        m["x"] = x[b]
        m["ctx"] = ctx[b]
        m["cc"] = np.stack([c[b], c_ctx])
        in_maps.append(m)
    res = run_bass_kernel_spmd(nc, in_maps, core_ids=list(range(N_CORES)))
    _CACHE["last_results"] = res
    return np.stack([np.asarray(res.results[b]["out"], dtype=np.float32) for b in range(N_CORES)])
```

```python
import math
import numpy as np
import concourse.bass as bass
import concourse.mybir as mybir
from concourse.bass_utils import run_bass_kernel_spmd
from contextlib import ExitStack

F32 = mybir.dt.float32
BF16 = mybir.dt.bfloat16
AF = mybir.ActivationFunctionType
ALU = mybir.AluOpType

P = 128
D = 2048
KC = D // P
S = 4096
CTX = 256
T = S + CTX
DEPTH = 2
GRID_W = 64
DFF = 5632
FC = DFF // P
INC = 4608
EPS = 1e-6
NCH = T // P
N_CORES = 4

BLOCKS = [(i * 512, 512) for i in range(8)] + [(4096, 256)]
RANGES = [[0, 1, 2, 3], [4, 5, 6, 7, 8]]

DEBUG = {}


class Trk:
    __slots__ = ("w", "r", "dsem", "excl")

    def __init__(self, excl=False):
        self.w = None
        self.r = {}
        self.dsem = None
        self.excl = excl


class Sync:
    CE = ("pe", "act", "dve", "pool")

    def __init__(self, nc, n_dma_sems=80):
        self.nc = nc
        self.eng = {"pe": nc.tensor, "act": nc.scalar, "dve": nc.vector, "pool": nc.gpsimd, "sp": nc.sync}
        self.sem = {e: nc.alloc_semaphore(f"prog_{e}") for e in self.CE}
        self.cnt = {e: 0 for e in self.CE}
        self.known = {e: {} for e in self.eng}
        self.free_dsems = [nc.alloc_semaphore(f"dsem{i}") for i in range(n_dma_sems)]
        self.dcnt = {}
        self.dsem_by_num = {}
        self.n_wait = 0
        self.n_ins = 0

    def _wait(self, e, waits):
        eng = self.eng[e]
        kn = self.known[e]
        for num, (sem, val) in waits.items():
            if kn.get(num, 0) >= val:
                continue
            eng.wait_ge(sem, val)
            kn[num] = val
            self.n_wait += 1

    @staticmethod
    def _need(waits, tk):
        if tk is None:
            return
        sem, val = tk
        cur = waits.get(sem.num)
        if cur is None or cur[1] < val:
            waits[sem.num] = (sem, val)

    def _collect(self, e, reads, writes):
        waits = {}
        own = self.sem[e].num if e in self.sem else None
        for t in reads:
            self._need(waits, t.w)
            if t.excl:
                for num, tk in t.r.items():
                    if num != own:
                        self._need(waits, tk)
        for t in writes:
            self._need(waits, t.w)
            for tk in t.r.values():
                self._need(waits, tk)
        if e == "pe":
            waits.pop(self.sem["pe"].num, None)
        return waits

    @staticmethod
    def _mark(tk, reads, writes):
        sem, val = tk
        for t in reads:
            cur = t.r.get(sem.num)
            if cur is None or cur[1] < val:
                t.r[sem.num] = tk
        for t in writes:
            t.w = tk
            t.r = {}

    def op(self, e, fn, reads=(), writes=(), signal=True):
        self._wait(e, self._collect(e, reads, writes))
        ins = fn(self.eng[e])
        self.n_ins += 1
        if signal:
            self.cnt[e] += 1
            ins.then_inc(self.sem[e], 1)
            tk = (self.sem[e], self.cnt[e])
        else:
            tk = (self.sem[e], self.cnt[e] + 1)
        self._mark(tk, reads, writes)
        return ins

    def _dsem(self, t):
        if t.dsem is None:
            t.dsem = self.free_dsems.pop()
            self.dsem_by_num[t.dsem.num] = t.dsem
            self.dcnt.setdefault(t.dsem.num, 0)
        return t.dsem

    def release(self, trks):
        for t in trks:
            if t.dsem is not None:
                self.free_dsems.append(t.dsem)
                t.dsem = None

    def load(self, q, trks, out_ap, in_ap, **kw):
        self._wait(q, self._collect(q, (), trks))
        sem = self._dsem(trks[0])
        ins = self.eng[q].dma_start(out=out_ap, in_=in_ap, **kw)
        self.dcnt[sem.num] += 1
        ins.then_inc(sem, 16)
        self.n_ins += 1
        self._mark((sem, 16 * self.dcnt[sem.num]), (), trks)

    def store(self, q, trks, out_ap, in_ap, **kw):
        self._wait(q, self._collect(q, trks, ()))
        sem = self._dsem(trks[0])
        ins = self.eng[q].dma_start(out=out_ap, in_=in_ap, **kw)
        self.dcnt[sem.num] += 1
        ins.then_inc(sem, 16)
        self.n_ins += 1
        self._mark((sem, 16 * self.dcnt[sem.num]), trks, ())

    def barrier(self):
        waits = {}
        for e in self.CE:
            if self.cnt[e] > 0:
                waits[self.sem[e].num] = (self.sem[e], self.cnt[e])
        for num, c in self.dcnt.items():
            if c > 0:
                waits[num] = (self.dsem_by_num[num], 16 * c)
        for e in self.eng:
            self._wait(e, dict(waits))


def _rope_tables():
    t = np.arange(S)
    row = (t // GRID_W).astype(np.float32)
    col = (t % GRID_W).astype(np.float32)

    def tab(d):
        half = d // 2
        quarter = half // 2
        inv = (np.float32(10000.0) ** (-np.arange(quarter, dtype=np.float32) / np.float32(quarter))).astype(np.float32)
        C = np.zeros((P, S), np.float32)
        Sg = np.zeros((P, S), np.float32)
        PM = np.zeros((P, P), np.float32)
        for p in range(P):
            m = p % d
            pos = row if (m // half) == 0 else col
            j = m % half
            i = j % quarter
            second = j // quarter
            ang = (pos * inv[i]).astype(np.float32)
            C[p] = np.cos(ang)
            Sg[p] = np.sin(ang) * (-1.0 if second == 0 else 1.0)
            partner = p + quarter if second == 0 else p - quarter
            PM[partner, p] = 1.0
        return C, Sg, PM

    return tab(64), tab(128)


def _na_index_tables():
    idx = np.zeros((3, 8, P, 512), np.int64)
    val = np.zeros((3, 8, P, 512), bool)
    for v, j in enumerate((0, 3, 7)):
        for i in range(8):
            kc = 4 * j - 2 + i
            kk = np.arange(P)
            krow = 2 * kc + kk // 64
            kcol = kk % 64
            qq = np.arange(512)
            qrow = 8 * j + qq // 64
            qcol = qq % 64
            r0 = np.clip(qrow - 4, 0, 64 - 8)
            c0 = np.clip(qcol - 8, 0, 64 - 16)
            rok = (krow[:, None] >= r0[None, :]) & (krow[:, None] < r0[None, :] + 8)
            cok = (kcol[:, None] >= c0[None, :]) & (kcol[:, None] < c0[None, :] + 16)
            roff = krow[:, None] - qrow[None, :] + 7
            coff = np.clip(kcol[:, None] - qcol[None, :], -15, 15) + 15
            ok = rok & cok & (krow[:, None] >= 0) & (krow[:, None] < 64)
            idx[v, i] = np.where(ok, roff * 31 + coff, 0)
            val[v, i] = ok
    return idx, val


def build_program():
    nc = bass.Bass("TRN2", target_bir_lowering=False)
    dt = nc.dram_tensor

    def din(name, shape, dtype=F32):
        return dt(name, list(shape), dtype, kind="ExternalInput").ap()

    x_in = din("x", [S, D])
    ctx_in = din("ctx", [CTX, D])
    cc_in = din("cc", [2, D])
    w_mod = din("w_mod", [DEPTH, D, 6 * D])
    b_mod = din("b_mod", [DEPTH, 6 * D])
    gains = din("gains", [4, DEPTH, D])
    w_in = din("w_in", [DEPTH, D, INC])
    w_out = din("w_out", [DEPTH, D, D])
    lam_in = din("lam", [4, DEPTH, 64])
    hg_in = din("hg", [3, DEPTH, P])
    w_gate = din("w_gate", [DEPTH, D, DFF])
    w_up = din("w_up", [DEPTH, D, DFF])
    w_down = din("w_down", [DEPTH, DFF, D])
    rope_in = din("rope", [4, P, S])
    pm_in = din("pm", [2, P, P])
    nab_in = din("nab", [DEPTH, 4, 3, 8, P, 512])
    ident_in = din("ident", [P, P])
    out = dt("out", [S, D], F32, kind="ExternalOutput").ap()

    hT = dt("hT", [D, T], F32, kind="Internal").ap()
    yT = dt("yT", [D, T], F32, kind="Internal").ap()
    qkT = dt("qkT", [26 * P, T], BF16, kind="Internal").ap()
    vS = dt("vS", [10, P, NCH * P], BF16, kind="Internal").ap()
    hidT = dt("hidT", [DFF, T], BF16, kind="Internal").ap()

    dbg = {}
    for name, shape, dtype in DEBUG.get("outs", []):
        dbg[name] = dt(name, list(shape), dtype, kind="ExternalOutput").ap()

    sy = Sync(nc)
    stop_after = DEBUG.get("stop_after")

    with ExitStack() as top:
        uid = [0]

        def sb(stack, name, shape, dtype):
            uid[0] += 1
            return stack.enter_context(nc.sbuf_tensor(f"s{uid[0]}_{name}", list(shape), dtype))

        ps_h = [top.enter_context(nc.psum_tensor(f"ps{i}", [P, 512], F32)) for i in range(8)]
        PS = [Trk(excl=True) for _ in range(8)]

        ident = sb(top, "ident", [P, P], F32)
        ones_bf = sb(top, "ones_bf", [P, P], BF16)
        ones_f = sb(top, "ones_f", [P, P], F32)
        pm = sb(top, "pm", [P, 2, P], F32)
        cst = Trk()
        sy.load("sp", [cst], ident[:], ident_in[:, :])
        sy.load("sp", [cst], pm[:], pm_in.rearrange("a p q -> p a q"))
        sy.op("dve", lambda e: e.memset(ones_bf[:], 1.0), writes=[cst])
        sy.op("dve", lambda e: e.memset(ones_f[:], 1.0), writes=[cst])
        pmb = sb(top, "pmb", [P, 2, P], BF16)
        sy.op("dve", lambda e: e.tensor_copy(pmb[:], pm[:]), reads=[cst], writes=[cst])
        modb = sb(top, "modb", [P, 96], F32)
        modc = sb(top, "modc", [P, 96], F32)
        der = {n: [sb(top, f"{n}{i}", [P, KC], F32) for i in range(2)]
               for n in ("a1", "gt1g", "a2", "gt2g")}
        gT = sb(top, "gT", [P, 4, KC], F32)
        hgT = sb(top, "hgT", [P, 3], F32)
        gdT = sb(top, "gdT", [P, 1], F32)
        nlam = sb(top, "nlam", [P, 1], F32)
        small = Trk()

        sy.barrier()

        def stage_transpose_in():
            with ExitStack() as st:
                xin = [sb(st, f"xin{i}", [P, 4, D], F32) for i in range(2)]
                xin_t = [Trk() for _ in range(2)]
                hb = [sb(st, f"hbo{i}", [P, KC, 512], F32) for i in range(2)]
                hb_t = [[Trk() for _ in range(KC)] for _ in range(2)]
                pi = 0
                for bi, (t0, n) in enumerate(BLOCKS):
                    nt = n // P
                    X, Xt = xin[bi % 2], xin_t[bi % 2]
                    H, Ht = hb[bi % 2], hb_t[bi % 2]
                    if t0 < S:
                        src = x_in[t0:t0 + n, :]
                    else:
                        src = ctx_in[:, :]
                    sy.load("sp", [Xt], X[:, 0:nt, :], src.rearrange("(j p) f -> p j f", p=P))
                    for k in range(KC):
                        ps, pst = ps_h[pi % 8], PS[pi % 8]
                        pi += 1
                        for j in range(nt):
                            sy.op("pe", lambda e, j=j, k=k, ps=ps, X=X: e.transpose(
                                ps[:, j * P:(j + 1) * P], X[:, j, k * P:(k + 1) * P], ident[:]),
                                reads=[Xt, cst], writes=[pst], signal=(j == nt - 1))
                        eng = "act" if k % 2 == 0 else "dve"
                        if eng == "act":
                            sy.op("act", lambda e, ps=ps, H=H, k=k, n=n: e.copy(H[:, k, 0:n], ps[:, 0:n]),
                                  reads=[pst], writes=[Ht[k]])
                        else:
                            sy.op("dve", lambda e, ps=ps, H=H, k=k, n=n: e.tensor_copy(H[:, k, 0:n], ps[:, 0:n]),
                                  reads=[pst], writes=[Ht[k]])
                    sy.store("sp", Ht, hT[:, t0:t0 + n].rearrange("(k p) t -> p k t", p=P), H[:, :, 0:n])
                sy.barrier()
                sy.release(xin_t + [t for l in hb_t for t in l])

        def stage_mod(l):
            with ExitStack() as st:
                cT = sb(st, "cT", [P, 2, KC], F32)
                scf = sb(st, "scf", [P, KC, 2], F32)
                scb = sb(st, "scb", [P, KC, 2], BF16)
                bmT = sb(st, "bmT", [P, 96], F32)
                lamc = sb(st, "lamc", [64, 4], F32)
                prod = sb(st, "prod", [64, 2], F32)
                ex = sb(st, "ex", [P, 2], F32)
                tmp = sb(st, "tmpm", [P, KC], F32)
                wm = [sb(st, f"wm{i}", [P, KC, 512], BF16) for i in range(2)]
                wm_t = [Trk() for _ in range(2)]
                tk = Trk()
                with nc.allow_non_contiguous_dma(reason="tiny transposed vector loads"):
                    sy.load("sp", [tk], cT[:], cc_in.rearrange("a (k p) -> p a k", p=P))
                    sy.load("sp", [tk], bmT[:], b_mod[l].rearrange("(c p) -> p c", p=P))
                    for a_ in range(4):
                        sy.load("sp", [small], gT[:, a_, :], gains[a_, l, :].rearrange("(k p) -> p k", p=P))
                    sy.load("sp", [small], hgT[:], hg_in[:, l, :].rearrange("a p -> p a"))
                    sy.load("sp", [tk], lamc[:], lam_in[:, l, :].rearrange("a p -> p a"))
                for a in range(2):
                    sy.op("act", lambda e, a=a: e.activation(scf[:, :, a], cT[:, a, :], AF.Silu),
                          reads=[tk], writes=[tk])
                sy.op("dve", lambda e: e.tensor_copy(scb[:], scf[:]), reads=[tk], writes=[tk])
                lam_init = 0.8 - 0.6 * math.exp(-0.3 * l)
                sy.op("dve", lambda e: e.tensor_tensor(prod[:, 0:1], lamc[:, 0:1], lamc[:, 1:2], ALU.mult),
                      reads=[tk], writes=[tk])
                sy.op("dve", lambda e: e.tensor_tensor(prod[:, 1:2], lamc[:, 2:3], lamc[:, 3:4], ALU.mult),
                      reads=[tk], writes=[tk])
                sy.op("pe", lambda e: e.matmul(ps_h[7][:, 0:2], lhsT=ones_f[0:64, :], rhs=prod[:, :],
                                               start=True, stop=True), reads=[tk, cst], writes=[PS[7]])
                sy.op("act", lambda e: e.activation(ex[:], ps_h[7][:, 0:2], AF.Exp), reads=[PS[7]], writes=[tk])
                sy.op("dve", lambda e: e.tensor_tensor(nlam[:], ex[:, 1:2], ex[:, 0:1], ALU.subtract),
                      reads=[tk], writes=[small])
                sy.op("dve", lambda e: e.tensor_scalar(nlam[:], nlam[:], -lam_init, None, ALU.add),
                      reads=[small], writes=[small])
                sy.op("dve", lambda e: e.tensor_scalar(gdT[:], hgT[:, 0:1], 1.0 - lam_init, None, ALU.mult),
                      reads=[small], writes=[small])
                NG = 24
                sy.load("pool", [wm_t[0]], wm[0][:], w_mod[l, :, 0:512].rearrange("(k p) n -> p k n", p=P))
                for g in range(NG):
                    if g + 1 < NG:
                        sy.load("pool", [wm_t[(g + 1) % 2]], wm[(g + 1) % 2][:],
                                w_mod[l, :, (g + 1) * 512:(g + 2) * 512].rearrange("(k p) n -> p k n", p=P))
                    W, Wt = wm[g % 2], wm_t[g % 2]
                    for j in range(4):
                        c = 4 * g + j
                        for k in range(KC):
                            sy.op("pe", lambda e, W=W, j=j, k=k, c=c: e.matmul(
                                ps_h[6][:, 2 * c:2 * c + 2], lhsT=W[:, k, j * P:(j + 1) * P], rhs=scb[:, k, :],
                                start=(k == 0), stop=(k == KC - 1)),
                                reads=[Wt, tk], writes=[PS[6]], signal=(k == KC - 1))
                psv = ps_h[6][:, 0:192].rearrange("p (c a) -> p c a", a=2)
                sy.op("dve", lambda e: e.tensor_tensor(modb[:], psv[:, :, 0], bmT[:], ALU.add),
                      reads=[PS[6], tk], writes=[small])
                sy.op("dve", lambda e: e.tensor_tensor(modc[:], psv[:, :, 1], bmT[:], ALU.add),
                      reads=[PS[6], tk], writes=[small])
                for i, mod in enumerate((modb, modc)):
                    for nm, sc_off, gidx in (("a1", 16, 0), ("a2", 64, 2)):
                        sy.op("dve", lambda e, mod=mod, sc_off=sc_off: e.tensor_scalar(
                            tmp[:], mod[:, sc_off:sc_off + 16], 1.0, None, ALU.add), reads=[small, tk], writes=[tk])
                        sy.op("dve", lambda e, nm=nm, i=i, gidx=gidx: e.tensor_tensor(
                            der[nm][i][:], tmp[:], gT[:, gidx, :], ALU.mult), reads=[tk, small], writes=[small])
                    for nm, gt_off, gidx in (("gt1g", 32, 1), ("gt2g", 80, 3)):
                        sy.op("dve", lambda e, nm=nm, i=i, gidx=gidx, mod=mod, gt_off=gt_off: e.tensor_tensor(
                            der[nm][i][:], mod[:, gt_off:gt_off + 16], gT[:, gidx, :], ALU.mult),
                            reads=[small], writes=[small])
                sy.barrier()
                sy.release([tk] + wm_t)

        def ssq_rstd(st_tiles, src_fn, srck_trks, n, d_count, ps_idx):
            sq, sq_t = st_tiles["sq"], st_tiles["sq_t"]
            ps, pst = ps_h[ps_idx], PS[ps_idx]
            for k in range(KC):
                s_, s_t = sq[k % len(sq)], sq_t[k % len(sq)]
                if k % 2 == 0:
                    sy.op("act", lambda e, s_=s_, k=k: e.activation(s_[:, 0:n], src_fn(k), AF.Square),
                          reads=[srck_trks[k]], writes=[s_t])
                else:
                    sy.op("dve", lambda e, s_=s_, k=k: e.tensor_tensor(s_[:, 0:n], src_fn(k), src_fn(k), ALU.mult),
                          reads=[srck_trks[k]], writes=[s_t])
                sy.op("pe", lambda e, s_=s_, k=k: e.matmul(ps[:, 0:n], lhsT=ones_bf[:], rhs=s_[:, 0:n],
                                                          start=(k == 0), stop=(k == KC - 1)),
                      reads=[s_t, cst], writes=[pst], signal=True)
            rt, rt_t = st_tiles["rt"], st_tiles["rt_t"]
            sy.op("act", lambda e: e.activation(rt[:, 0:n], ps[:, 0:n], AF.Sqrt, bias=st_tiles["eps"][:, 0:1],
                                                scale=1.0 / d_count),
                  reads=[pst, st_tiles["eps_t"]], writes=[rt_t])
            sy.op("dve", lambda e: e.reciprocal(rt[:, 0:n], rt[:, 0:n]), reads=[rt_t], writes=[rt_t])
            return rt, rt_t

        def stage_norm(l, blocks, XT, XT_t, resid, modn):
            with ExitStack() as st:
                NB = 2
                Hs = [sb(st, f"nH{i}", [P, KC, 256], F32) for i in range(NB)]
                H_t = [[Trk() for _ in range(KC)] for _ in range(NB)]
                Ys = [sb(st, f"nY{i}", [P, KC, 256], F32) for i in range(NB)] if resid else None
                Y_t = [[Trk() for _ in range(KC)] for _ in range(NB)]
                tiles = {
                    "sq": [sb(st, f"nsq{i}", [P, 512], BF16) for i in range(4)],
                    "sq_t": [Trk() for _ in range(4)],
                    "rt": None, "rt_t": None,
                    "eps": sb(st, "neps", [P, 1], F32),
                }
                eps_t = Trk()
                sy.op("dve", lambda e: e.memset(tiles["eps"][:], EPS), writes=[eps_t])
                tiles["eps_t"] = eps_t
                rts = [sb(st, f"nrt{i}", [P, 512], F32) for i in range(4)]
                rt_ts = [Trk() for _ in range(4)]
                tmps = [sb(st, f"ntmp{i}", [P, 512], F32) for i in range(4)]
                tmp_ts = [Trk() for _ in range(4)]
                ri = 0
                ti = 0
                x0 = blocks[0][0] if XT is not None else 0
                subs = []
                for (bt0, bn) in blocks:
                    for o in range(0, bn, 256):
                        subs.append((bt0 + o, 256, bt0))
                for bi, (t0, n, kt0) in enumerate(subs):
                    is_ctx = 1 if t0 >= S else 0
                    H, Ht = Hs[bi % NB], H_t[bi % NB]
                    sy.load("sp", Ht, H[:, :, 0:n], hT[:, t0:t0 + n].rearrange("(k p) t -> p k t", p=P))
                    if resid:
                        Y, Yt = Ys[bi % NB], Y_t[bi % NB]
                        sy.load("sp", Yt, Y[:, :, 0:n], yT[:, t0:t0 + n].rearrange("(k p) t -> p k t", p=P))
                        tiles["rt"], tiles["rt_t"] = rts[ri % 4], rt_ts[ri % 4]
                        ri += 1
                        r1, r1t = ssq_rstd(tiles, lambda k, Y=Y: Y[:, k, 0:n], Yt, n, D, 6)
                        gtg = der[resid][is_ctx]
                        for k in range(KC):
                            tm, tmt = tmps[ti % 4], tmp_ts[ti % 4]
                            ti += 1
                            sy.op("dve", lambda e, tm=tm, Y=Y, k=k, r1=r1: e.scalar_tensor_tensor(
                                tm[:, 0:n], Y[:, k, 0:n], gtg[:, k:k + 1], r1[:, 0:n], ALU.mult, ALU.mult),
                                reads=[Yt[k], r1t, small], writes=[tmt])
                            sy.op("pool", lambda e, tm=tm, H=H, k=k: e.tensor_tensor(
                                H[:, k, 0:n], tm[:, 0:n], H[:, k, 0:n], ALU.add),
                                reads=[tmt, Ht[k]], writes=[Ht[k]])
                    if modn:
                        tiles["rt"], tiles["rt_t"] = rts[ri % 4], rt_ts[ri % 4]
                        ri += 1
                        r2, r2t = ssq_rstd(tiles, lambda k, H=H: H[:, k, 0:n], Ht, n, D, 7)
                        a = der["a1" if modn == 1 else "a2"][is_ctx]
                        mod = modc if is_ctx else modb
                        sh_off = 0 if modn == 1 else 48
                        for k in range(KC):
                            tm, tmt = tmps[ti % 4], tmp_ts[ti % 4]
                            ti += 1
                            sy.op("dve", lambda e, tm=tm, H=H, k=k, r2=r2: e.tensor_tensor(
                                tm[:, 0:n], H[:, k, 0:n], r2[:, 0:n], ALU.mult),
                                reads=[Ht[k], r2t], writes=[tmt])
                            sy.op("act", lambda e, tm=tm, k=k, a=a, mod=mod: e.activation(
                                XT[:, k, t0 - x0:t0 - x0 + n], tm[:, 0:n], AF.Identity,
                                bias=mod[:, sh_off + k:sh_off + k + 1], scale=a[:, k:k + 1]),
                                reads=[tmt, small], writes=[XT_t[(k, kt0)]])
                    if resid:
                        sy.store("sp", Ht, hT[:, t0:t0 + n].rearrange("(k p) t -> p k t", p=P), H[:, :, 0:n])
                sy.barrier()
                sy.release([t for l_ in H_t for t in l_] + [t for l_ in Y_t for t in l_])

        def run_pending(pending):
            alive = []
            for g in pending:
                try:
                    next(g)
                    alive.append(g)
                except StopIteration:
                    pass
            pending[:] = alive

        def drain(pending):
            while pending:
                run_pending(pending)

        def inproj_plan(c):
            if c < 4:
                return ("rope64", c)
            if c < 8:
                return ("rope64", c)
            if c < 12:
                return ("v", c - 8)
            if c < 20:
                return ("nrope_q", 8 + (c - 12))
            if c < 22:
                return ("nrope_k", 16 + (c - 20))
            if c < 24:
                return ("v", 4 + (c - 22))
            if c < 28:
                return ("plain", 18 + (c - 24))
            if c < 32:
                return ("plain", 22 + (c - 28))
            return ("v", 6 + (c - 32))

        def stage_inproj(l, blocks, XT, XT_t):
            x0 = blocks[0][0]
            with ExitStack() as st:
                wt = [sb(st, f"wi{i}", [P, KC, 512], BF16) for i in range(2)]
                wt_t = [Trk() for _ in range(2)]
                rp = [sb(st, f"rp{i}", [P, 2, 512], F32) for i in range(2)]
                rp_t = [Trk() for _ in range(2)]
                NR = 6
                xf = [sb(st, f"xf{i}", [P, 512], F32) for i in range(NR)]
                xf_t = [Trk() for _ in range(NR)]
                xb = [sb(st, f"xb{i}", [P, 512], BF16) for i in range(NR)]
                xb_t = [Trk() for _ in range(NR)]
                sq = [sb(st, f"isq{i}", [P, 512], BF16) for i in range(2)]
                sq_t = [Trk() for _ in range(2)]
                rt = [sb(st, f"irt{i}", [P, 512], F32) for i in range(2)]
                rt_t = [Trk() for _ in range(2)]
                t1 = [sb(st, f"it1{i}", [P, 512], F32) for i in range(NR)]
                t1_t = [Trk() for _ in range(NR)]
                t2 = [sb(st, f"it2{i}", [P, 512], F32) for i in range(NR)]
                t2_t = [Trk() for _ in range(NR)]
                ob = [sb(st, f"iob{i}", [P, 512], BF16) for i in range(NR)]
                ob_t = [Trk() for _ in range(NR)]
                epsT = sb(st, "ieps", [P, 1], F32)
                eps_t = Trk()
                sy.op("dve", lambda e: e.memset(epsT[:], EPS), writes=[eps_t])
                cnt = {"main": 0, "aux": 0, "buf": 0, "rp": 0}
                pending = []

                def w_src(g):
                    return w_in[l, :, g * 512:(g + 1) * 512].rearrange("(k p) n -> p k n", p=P)

                def epi_fm(kind, dest, ps, pst, t0, n, rpi):
                    i = cnt["buf"] % NR
                    cnt["buf"] += 1
                    O, Ot = ob[i], ob_t[i]
                    latent = t0 < S
                    dst = qkT[dest * P:(dest + 1) * P, t0:t0 + n]
                    if DEBUG.get("inproj_mode") == "plain_all":
                        kind = "plain"
                    if DEBUG.get("inproj_mode") == "no_rope" and kind == "rope64":
                        kind = "plain"
                    if DEBUG.get("inproj_mode") == "no_nrope" and kind.startswith("nrope"):
                        kind = "plain"
                    if kind == "plain" or (kind == "rope64" and not latent):
                        sy.op("act", lambda e: e.copy(O[:, 0:n], ps[:, 0:n]), reads=[pst], writes=[Ot])
                        sy.store("sp", [Ot], dst, O[:, 0:n])
                        return
                    X, Xt_ = xf[i], xf_t[i]
                    XB, XBt = xb[i], xb_t[i]
                    if kind == "rope64":
                        sy.op("act", lambda e: e.copy(XB[:, 0:n], ps[:, 0:n]), reads=[pst], writes=[XBt])
                        tsel, pmi = 0, 0
                        xsrc, xsrc_t = ps, pst
                    else:
                        s_, s_t = sq[cnt["aux"] % 2], sq_t[cnt["aux"] % 2]
                        r_, r_t = rt[cnt["aux"] % 2], rt_t[cnt["aux"] % 2]
                        pa, pat = ps_h[6 + cnt["aux"] % 2], PS[6 + cnt["aux"] % 2]
                        cnt["aux"] += 1
                        sy.op("act", lambda e: e.activation(s_[:, 0:n], ps[:, 0:n], AF.Square),
                              reads=[pst], writes=[s_t])
                        yield
                        sy.op("pe", lambda e: e.matmul(pa[:, 0:n], lhsT=ones_bf[:], rhs=s_[:, 0:n],
                                                       start=True, stop=True), reads=[s_t, cst], writes=[pat])
                        sy.op("act", lambda e: e.activation(r_[:, 0:n], pa[:, 0:n], AF.Sqrt, bias=epsT[:, 0:1],
                                                            scale=1.0 / P), reads=[pat, eps_t], writes=[r_t])
                        sy.op("dve", lambda e: e.reciprocal(r_[:, 0:n], r_[:, 0:n]), reads=[r_t], writes=[r_t])
                        gcol = 1 if kind == "nrope_q" else 2
                        sy.op("act", lambda e: e.activation(X[:, 0:n], ps[:, 0:n], AF.Identity,
                                                            scale=hgT[:, gcol:gcol + 1]),
                              reads=[pst, small], writes=[Xt_])
                        sy.op("dve", lambda e: e.tensor_tensor(X[:, 0:n], X[:, 0:n], r_[:, 0:n], ALU.mult),
                              reads=[Xt_, r_t], writes=[Xt_])
                        tsel, pmi = 1, 1
                        xsrc, xsrc_t = X, Xt_
                        if not latent:
                            sy.op("act", lambda e: e.copy(O[:, 0:n], X[:, 0:n]), reads=[Xt_], writes=[Ot])
                            sy.store("sp", [Ot], dst, O[:, 0:n])
                            return
                        sy.op("act", lambda e: e.copy(XB[:, 0:n], X[:, 0:n]), reads=[Xt_], writes=[XBt])
                    yield
                    pb_i = 4 + cnt["main"] % 2
                    pb, pbt = ps_h[pb_i], PS[pb_i]
                    sy.op("pe", lambda e: e.matmul(pb[:, 0:n], lhsT=pmb[:, pmi, :], rhs=XB[:, 0:n],
                                                   start=True, stop=True), reads=[XBt, cst], writes=[pbt])
                    R, Rt = rp[rpi], rp_t[rpi]
                    A, At = t1[i], t1_t[i]
                    B, Bt = t2[i], t2_t[i]
                    sy.op("dve", lambda e: e.tensor_tensor(A[:, 0:n], xsrc[:, 0:n], R[:, 0, 0:n], ALU.mult),
                          reads=[xsrc_t, Rt], writes=[At])
                    sy.op("dve", lambda e: e.tensor_tensor(B[:, 0:n], pb[:, 0:n], R[:, 1, 0:n], ALU.mult),
                          reads=[pbt, Rt], writes=[Bt])
                    sy.op("dve", lambda e: e.tensor_tensor(O[:, 0:n], A[:, 0:n], B[:, 0:n], ALU.add),
                          reads=[At, Bt], writes=[Ot])
                    sy.store("sp", [Ot], dst, O[:, 0:n])

                NG = 9
                sy.load("pool", [wt_t[0]], wt[0][:], w_src(0))
                for g in range(NG):
                    if g + 1 < NG:
                        sy.load("pool", [wt_t[(g + 1) % 2]], wt[(g + 1) % 2][:], w_src(g + 1))
                    W, Wt = wt[g % 2], wt_t[g % 2]
                    plans = [inproj_plan(4 * g + j) for j in range(4)]
                    kinds = set(p_[0] for p_ in plans)
                    need_rope = None
                    if "rope64" in kinds:
                        need_rope = 0
                    elif "nrope_q" in kinds or "nrope_k" in kinds:
                        need_rope = 1
                    for (t0, n) in blocks:
                        rpi = None
                        if need_rope is not None and t0 < S:
                            rpi = cnt["rp"] % 2
                            cnt["rp"] += 1
                            sy.load("sp", [rp_t[rpi]], rp[rpi][:, :, 0:n],
                                    rope_in[2 * need_rope:2 * need_rope + 2, :, t0:t0 + n].rearrange("a p t -> p a t"))
                        for j, (kind, dest) in enumerate(plans):
                            if kind == "v":
                                continue
                            pi = cnt["main"] % 4
                            cnt["main"] += 1
                            ps, pst = ps_h[pi], PS[pi]
                            for k in range(KC):
                                sy.op("pe", lambda e, ps=ps, W=W, j=j, k=k: e.matmul(
                                    ps[:, 0:n], lhsT=W[:, k, j * P:(j + 1) * P],
                                    rhs=XT[:, k, t0 - x0:t0 - x0 + n], start=(k == 0), stop=(k == KC - 1)),
                                    reads=[Wt, XT_t[(k, t0)]], writes=[pst], signal=(k == KC - 1))
                            run_pending(pending)
                            gen = epi_fm(kind, dest, ps, pst, t0, n, rpi)
                            pending.append(gen)
                        vj = [j for j, p_ in enumerate(plans) if p_[0] == "v"]
                        if vj:
                            c0, c1 = vj[0] * P, (vj[-1] + 1) * P
                            ncol = c1 - c0
                            for tt in range(n // P):
                                pi = cnt["main"] % 4
                                cnt["main"] += 1
                                ps, pst = ps_h[pi], PS[pi]
                                ta = t0 - x0 + tt * P
                                for k in range(KC):
                                    sy.op("pe", lambda e, ps=ps, W=W, k=k, ta=ta: e.matmul(
                                        ps[:, 0:ncol], lhsT=XT[:, k, ta:ta + P], rhs=W[:, k, c0:c1],
                                        start=(k == 0), stop=(k == KC - 1)),
                                        reads=[Wt, XT_t[(k, t0)]], writes=[pst], signal=(k == KC - 1))
                                run_pending(pending)
                                i = cnt["buf"] % NR
                                cnt["buf"] += 1
                                O, Ot = ob[i], ob_t[i]
                                sy.op("act", lambda e, O=O, ps=ps: e.copy(O[:, 0:ncol], ps[:, 0:ncol]),
                                      reads=[pst], writes=[Ot])
                                chunk = (t0 + tt * P) // P
                                heads = [plans[j][1] for j in vj]
                                h0 = heads[0]
                                sy.store("sp", [Ot],
                                         vS[h0:h0 + len(heads), :, chunk * P:(chunk + 1) * P].rearrange("h p d -> p h d"),
                                         O[:, 0:ncol].rearrange("p (h d) -> p h d", d=P))
                drain(pending)
                sy.barrier()
                sy.release(wt_t + rp_t + ob_t)

        def stage_attention(l, blocks, XT, XT_t, with_ctx):
            x0 = blocks[0][0]
            with ExitStack() as st:
                KT = [sb(st, f"KT{i}", [P, T], BF16) for i in range(2)]
                KT_t = [Trk() for _ in range(2)]
                VT = [sb(st, f"VT{i}", [P, NCH, P], BF16) for i in range(2)]
                VT_t = [Trk() for _ in range(2)]
                QT = [sb(st, f"QT{i}", [P, 2304], BF16) for i in range(2)]
                QT_t = [Trk() for _ in range(2)]
                QZ = [[sb(st, f"QZ{i}_{m}", [P, 2304], BF16) for m in range(2)] for i in range(2)]
                QZ_t = [[Trk() for m in range(2)] for i in range(2)]
                for i in range(2):
                    sy.op("dve", lambda e, i=i: e.memset(QZ[i][0][64:128, :], 0.0), writes=[QZ_t[i][0]])
                    sy.op("dve", lambda e, i=i: e.memset(QZ[i][1][0:64, :], 0.0), writes=[QZ_t[i][1]])
                NE = 4
                E = [sb(st, f"E{i}", [P, 512], BF16) for i in range(NE)]
                E_t = [Trk() for _ in range(NE)]
                SBm = [sb(st, f"SBm{i}", [P, 512], F32) for i in range(2)]
                SB_t = [Trk() for _ in range(2)]
                NB = [sb(st, f"NB{i}", [P, 512], F32) for i in range(3)]
                NB_t = [Trk() for _ in range(3)]
                RD = [sb(st, f"RD{i}", [P, 512], F32) for i in range(2)]
                RD_t = [Trk() for _ in range(2)]
                O1 = sb(st, "O1", [P, 512], F32)
                O1_t = Trk()
                O2 = sb(st, "O2", [P, 512], F32)
                O2_t = Trk()
                OD = sb(st, "OD", [P, 512], F32)
                OD_t = Trk()
                SQ = sb(st, "aSQ", [P, 512], BF16)
                SQ_t = Trk()
                RT = sb(st, "aRT", [P, 512], F32)
                RT_t = Trk()
                epsT = sb(st, "aeps", [P, 1], F32)
                eps_t = Trk()
                sy.op("dve", lambda e: e.memset(epsT[:], EPS), writes=[eps_t])
                cnt = {"s": 0, "e": 0, "acc": 0, "kv": 0, "q": 0, "nb": 0, "sb": 0, "rd": 0}
                ntok = sum(n for _, n in blocks)
                qblocks = [(t0, n) for (t0, n) in blocks if (t0 < S or with_ctx)]

                def load_kv(kchunk, vhead):
                    i = cnt["kv"] % 2
                    cnt["kv"] += 1
                    sy.load("sp", [KT_t[i]], KT[i][:], qkT[kchunk * P:(kchunk + 1) * P, :])
                    sy.load("sp", [VT_t[i]], VT[i][:], vS[vhead].rearrange("p (c d) -> p c d", d=P))
                    return i

                def load_q(qchunk):
                    i = cnt["q"] % 2
                    cnt["q"] += 1
                    sy.load("sp", [QT_t[i]], QT[i][:, 0:ntok], qkT[qchunk * P:(qchunk + 1) * P, x0:x0 + ntok])
                    return i

                def load_qz(qchunk):
                    i = cnt["q"] % 2
                    cnt["q"] += 1
                    sy.load("sp", [QZ_t[i][0]], QZ[i][0][0:64, 0:ntok], qkT[qchunk * P:qchunk * P + 64, x0:x0 + ntok])
                    sy.load("sp", [QZ_t[i][1]], QZ[i][1][64:128, 0:ntok],
                            qkT[qchunk * P + 64:(qchunk + 1) * P, x0:x0 + ntok])
                    return i

                def core(kvi, Q_, Q_t, t0, n, scale, na=None):
                    a = cnt["acc"] % 2
                    cnt["acc"] += 1
                    po, pot = ps_h[3 + a], PS[3 + a]
                    pd, pdt = ps_h[5 + a], PS[5 + a]
                    K_, K_t, V_, V_t = KT[kvi], KT_t[kvi], VT[kvi], VT_t[kvi]
                    if t0 >= S:
                        chunks = [32, 33]
                    elif na is not None:
                        j = t0 // 512
                        chunks = [c for c in range(4 * j - 2, 4 * j + 6) if 0 <= c < 32] + [32, 33]
                    else:
                        chunks = list(range(NCH))
                    q0 = t0 - x0

                    def emit_s(c):
                        si = cnt["s"] % 3
                        cnt["s"] += 1
                        ps, pst = ps_h[si], PS[si]
                        sy.op("pe", lambda e: e.matmul(ps[:, 0:n], lhsT=K_[:, c * P:(c + 1) * P],
                                                       rhs=Q_[:, q0:q0 + n], start=True, stop=True),
                              reads=[K_t, Q_t], writes=[pst])
                        return ps, pst

                    nxt = emit_s(chunks[0])
                    for ci, c in enumerate(chunks):
                        ps, pst = nxt
                        if ci + 1 < len(chunks):
                            nxt = emit_s(chunks[ci + 1])
                        ei = cnt["e"] % NE
                        cnt["e"] += 1
                        E_, E_t_ = E[ei], E_t[ei]
                        if na is not None and c < 32:
                            h, var = na
                            j = t0 // 512
                            i_off = c - (4 * j - 2)
                            bi = cnt["nb"] % 3
                            cnt["nb"] += 1
                            sy.load("sp", [NB_t[bi]], NB[bi][:], nab_in[l, h, var, i_off])
                            sb_i = cnt["sb"] % 2
                            cnt["sb"] += 1
                            sy.op("dve", lambda e: e.scalar_tensor_tensor(
                                SBm[sb_i][:, 0:n], ps[:, 0:n], scale, NB[bi][:, 0:n], ALU.mult, ALU.add),
                                reads=[pst, NB_t[bi]], writes=[SB_t[sb_i]])
                            sy.op("act", lambda e: e.activation(E_[:, 0:n], SBm[sb_i][:, 0:n], AF.Exp),
                                  reads=[SB_t[sb_i]], writes=[E_t_])
                        else:
                            sy.op("act", lambda e: e.activation(E_[:, 0:n], ps[:, 0:n], AF.Exp, scale=scale),
                                  reads=[pst], writes=[E_t_])
                        first, last = ci == 0, ci == len(chunks) - 1
                        sy.op("pe", lambda e: e.matmul(po[:, 0:n], lhsT=V_[:, c, :], rhs=E_[:, 0:n],
                                                       start=first, stop=last),
                              reads=[V_t, E_t_], writes=[pot], signal=False)
                        sy.op("pe", lambda e: e.matmul(pd[:, 0:n], lhsT=ones_bf[:], rhs=E_[:, 0:n],
                                                       start=first, stop=last),
                              reads=[E_t_, cst], writes=[pdt], signal=True)
                    return po, pot, pd, pdt

                def recip(pd, pdt, n):
                    i = cnt["rd"] % 2
                    cnt["rd"] += 1
                    sy.op("dve", lambda e: e.reciprocal(RD[i][:, 0:n], pd[:, 0:n]), reads=[pdt], writes=[RD_t[i]])
                    return RD[i], RD_t[i]

                def finish_plain(xchunk, po, pot, pd, pdt, t0, n):
                    R, Rt = recip(pd, pdt, n)
                    sy.op("dve", lambda e: e.tensor_tensor(XT[:, xchunk, t0 - x0:t0 - x0 + n], po[:, 0:n],
                                                           R[:, 0:n], ALU.mult),
                          reads=[pot, Rt], writes=[XT_t[(xchunk, t0)]])

                sc_a = 64 ** -0.5
                for h in range(4):
                    kvi = load_kv(4 + h, h)
                    qi = load_qz(h)
                    for (t0, n) in qblocks:
                        po, pot, pd, pdt = core(kvi, QZ[qi][0], QZ_t[qi][0], t0, n, sc_a)
                        R, Rt = recip(pd, pdt, n)
                        sy.op("dve", lambda e: e.tensor_tensor(O1[:, 0:n], po[:, 0:n], R[:, 0:n], ALU.mult),
                              reads=[pot, Rt], writes=[O1_t])
                        po, pot, pd, pdt = core(kvi, QZ[qi][1], QZ_t[qi][1], t0, n, sc_a)
                        R, Rt = recip(pd, pdt, n)
                        sy.op("dve", lambda e: e.tensor_tensor(O2[:, 0:n], po[:, 0:n], R[:, 0:n], ALU.mult),
                              reads=[pot, Rt], writes=[O2_t])
                        sy.op("dve", lambda e: e.scalar_tensor_tensor(OD[:, 0:n], O2[:, 0:n], nlam[:, 0:1],
                                                                      O1[:, 0:n], ALU.mult, ALU.add),
                              reads=[O1_t, O2_t, small], writes=[OD_t])
                        sy.op("act", lambda e: e.activation(SQ[:, 0:n], OD[:, 0:n], AF.Square),
                              reads=[OD_t], writes=[SQ_t])
                        sy.op("pe", lambda e: e.matmul(ps_h[7][:, 0:n], lhsT=ones_bf[:], rhs=SQ[:, 0:n],
                                                       start=True, stop=True), reads=[SQ_t, cst], writes=[PS[7]])
                        sy.op("act", lambda e: e.activation(RT[:, 0:n], ps_h[7][:, 0:n], AF.Sqrt,
                                                            bias=epsT[:, 0:1], scale=1.0 / P),
                              reads=[PS[7], eps_t], writes=[RT_t])
                        sy.op("dve", lambda e: e.reciprocal(RT[:, 0:n], RT[:, 0:n]), reads=[RT_t], writes=[RT_t])
                        sy.op("dve", lambda e: e.scalar_tensor_tensor(
                            XT[:, h, t0 - x0:t0 - x0 + n], OD[:, 0:n], gdT[:, 0:1], RT[:, 0:n], ALU.mult, ALU.mult),
                            reads=[OD_t, RT_t, small], writes=[XT_t[(h, t0)]])
                sc_b = 128 ** -0.5
                for kvh in range(2):
                    kvi = load_kv(16 + kvh, 4 + kvh)
                    for gq in range(4):
                        hq = kvh * 4 + gq
                        qi = load_q(8 + hq)
                        for (t0, n) in qblocks:
                            po, pot, pd, pdt = core(kvi, QT[qi], QT_t[qi], t0, n, sc_b)
                            finish_plain(4 + hq, po, pot, pd, pdt, t0, n)
                for h in range(4):
                    kvi = load_kv(22 + h, 6 + h)
                    qi = load_q(18 + h)
                    for (t0, n) in qblocks:
                        if t0 < S:
                            j = t0 // 512
                            var = 0 if j == 0 else (2 if j == 7 else 1)
                            po, pot, pd, pdt = core(kvi, QT[qi], QT_t[qi], t0, n, sc_b, na=(h, var))
                        else:
                            po, pot, pd, pdt = core(kvi, QT[qi], QT_t[qi], t0, n, sc_b)
                        finish_plain(12 + h, po, pot, pd, pdt, t0, n)
                sy.barrier()
                sy.release(KT_t + VT_t + QT_t + NB_t + [t for l_ in QZ_t for t in l_])

        def stage_gemm_fm(wsrc_fn, ngroups, gcols, kchunks, blocks, XT, XT_t, x0, dstT, wname, epi="copy",
                          wsrc2_fn=None):
            nj = gcols // P
            with ExitStack() as st:
                nwb = 2
                wt = [sb(st, f"{wname}{i}", [P, kchunks, gcols], BF16) for i in range(nwb)]
                wt_t = [Trk() for _ in range(nwb)]
                if wsrc2_fn is not None:
                    wu = [sb(st, f"{wname}u{i}", [P, kchunks, gcols], BF16) for i in range(nwb)]
                    wu_t = [Trk() for _ in range(nwb)]
                NO = 4
                odt = BF16 if epi == "swiglu" else F32
                ob = [sb(st, f"{wname}o{i}", [P, 512], odt) for i in range(NO)]
                ob_t = [Trk() for _ in range(NO)]
                if epi == "swiglu":
                    sg = [sb(st, f"{wname}s{i}", [P, 512], F32) for i in range(NO)]
                    sg_t = [Trk() for _ in range(NO)]
                cnt = {"ps": 0, "o": 0}

                def issue_w(g):
                    sy.load("pool", [wt_t[g % nwb]], wt[g % nwb][:], wsrc_fn(g))
                    if wsrc2_fn is not None:
                        sy.load("pool", [wu_t[g % nwb]], wu[g % nwb][:], wsrc2_fn(g))

                issue_w(0)
                for g in range(ngroups):
                    if g + 1 < ngroups:
                        issue_w(g + 1)
                    W, Wt = wt[g % nwb], wt_t[g % nwb]
                    for (t0, n) in blocks:
                        for j in range(nj):
                            def mm(Wx, Wxt):
                                pi = cnt["ps"] % 8
                                cnt["ps"] += 1
                                ps, pst = ps_h[pi], PS[pi]
                                for k in range(kchunks):
                                    sy.op("pe", lambda e, k=k: e.matmul(
                                        ps[:, 0:n], lhsT=Wx[:, k, j * P:(j + 1) * P],
                                        rhs=XT[:, k, t0 - x0:t0 - x0 + n], start=(k == 0), stop=(k == kchunks - 1)),
                                        reads=[Wxt, XT_t[(k, t0)]], writes=[pst], signal=(k == kchunks - 1))
                                return ps, pst
                            ps, pst = mm(W, Wt)
                            oi = cnt["o"] % NO
                            cnt["o"] += 1
                            O, Ot = ob[oi], ob_t[oi]
                            row0 = g * gcols + j * P
                            if epi == "copy":
                                if oi % 2 == 0:
                                    sy.op("act", lambda e: e.copy(O[:, 0:n], ps[:, 0:n]), reads=[pst], writes=[Ot])
                                else:
                                    sy.op("dve", lambda e: e.tensor_copy(O[:, 0:n], ps[:, 0:n]), reads=[pst], writes=[Ot])
                            else:
                                ps2, pst2 = mm(wu[g % nwb], wu_t[g % nwb])
                                G, Gt = sg[oi], sg_t[oi]
                                sy.op("act", lambda e: e.activation(G[:, 0:n], ps[:, 0:n], AF.Silu),
                                      reads=[pst], writes=[Gt])
                                sy.op("dve", lambda e: e.tensor_tensor(O[:, 0:n], G[:, 0:n], ps2[:, 0:n], ALU.mult),
                                      reads=[Gt, pst2], writes=[Ot])
                            sy.store("sp", [Ot], dstT[row0:row0 + P, t0:t0 + n], O[:, 0:n])
                sy.barrier()
                sy.release(wt_t + ob_t + (wu_t if wsrc2_fn is not None else []))

        def stage_down(l, with_ctx):
            passes = [[0, 1, 2], [3, 4, 5], [6, 7] + ([8] if with_ctx else [])]
            for pblocks in passes:
                blocks = [BLOCKS[i] for i in pblocks]
                x0 = blocks[0][0]
                ntok = sum(n for _, n in blocks)
                with ExitStack() as st:
                    HX = sb(st, "HX", [P, FC, 1536], BF16)
                    HX_t = {}
                    ld = Trk()
                    for k in range(FC):
                        for (t0, n) in blocks:
                            HX_t[(k, t0)] = ld
                    lds = [Trk() for _ in range(4)]
                    for q in range(4):
                        for k in range(11 * q, 11 * q + 11):
                            for (t0, n) in blocks:
                                HX_t[(k, t0)] = lds[q]
                        sy.load("sp", [lds[q]], HX[:, 11 * q:11 * q + 11, 0:ntok],
                                hidT[11 * q * P:(11 * q + 11) * P, x0:x0 + ntok].rearrange("(k p) t -> p k t", p=P))
                    stage_gemm_fm(
                        lambda g: w_down[l, :, g * 256:(g + 1) * 256].rearrange("(k p) n -> p k n", p=P),
                        8, 256, FC, blocks, HX, HX_t, x0, yT, "wd")
                    sy.release(lds)

        def stage_transpose_out():
            with ExitStack() as st:
                hb = [sb(st, f"ohb{i}", [P, KC, 512], F32) for i in range(2)]
                hb_t = [Trk() for _ in range(2)]
                ox = [sb(st, f"oox{i}", [P, 4, D], F32) for i in range(2)]
                ox_t = [[Trk() for _ in range(16)] for _ in range(2)]
                pi = 0
                for bi, (t0, n) in enumerate(BLOCKS[:8]):
                    H, Ht = hb[bi % 2], hb_t[bi % 2]
                    OX, OXt = ox[bi % 2], ox_t[bi % 2]
                    sy.load("sp", [Ht], H[:], hT[:, t0:t0 + n].rearrange("(k p) t -> p k t", p=P))
                    for j in range(4):
                        for kg in range(4):
                            ps, pst = ps_h[pi % 8], PS[pi % 8]
                            pi += 1
                            for kk in range(4):
                                k = kg * 4 + kk
                                sy.op("pe", lambda e, ps=ps, kk=kk, k=k, j=j, H=H: e.transpose(
                                    ps[:, kk * P:(kk + 1) * P], H[:, k, j * P:(j + 1) * P], ident[:]),
                                    reads=[Ht, cst], writes=[pst], signal=(kk == 3))
                            tr = OXt[j * 4 + kg]
                            if (j * 4 + kg) % 2 == 0:
                                sy.op("act", lambda e, ps=ps, OX=OX, j=j, kg=kg: e.copy(
                                    OX[:, j, kg * 512:(kg + 1) * 512], ps[:, :]), reads=[pst], writes=[tr])
                            else:
                                sy.op("dve", lambda e, ps=ps, OX=OX, j=j, kg=kg: e.tensor_copy(
                                    OX[:, j, kg * 512:(kg + 1) * 512], ps[:, :]), reads=[pst], writes=[tr])
                    sy.store("sp", OXt, out[t0:t0 + n, :].rearrange("(j p) f -> p j f", p=P), OX[:])
                sy.barrier()
                sy.release(hb_t + [t for l_ in ox_t for t in l_])

        def dump(name, src_ap, shape, dtype):
            if name not in dbg:
                return
            rows, cols = shape
            with ExitStack() as st:
                tl = sb(st, "dbgt", [P, cols], dtype)
                tt = Trk()
                for r0 in range(0, rows, P):
                    sy.load("sp", [tt], tl[:], src_ap[r0:r0 + P, :])
                    sy.store("sp", [tt], dbg[name][r0:r0 + P, :], tl[:])
                sy.barrier()
                sy.release([tt])

        def main():
            stage_transpose_in()
            dump("d_hT0", hT, (D, T), F32)
            if stop_after == "x0":
                return
            for l in range(DEPTH):
                last = l == DEPTH - 1
                with_ctx = not last
                stage_mod(l)
                if l == 0 and "d_mod" in dbg:
                    tdm = Trk()
                    sy.store("sp", [small], dbg["d_mod"][:, 0:96], modb[:])
                    sy.store("sp", [small], dbg["d_mod"][:, 96:192], modc[:])
                    for ii, nm_ in enumerate(("a1", "gt1g", "a2", "gt2g")):
                        for jj in range(2):
                            o_ = 192 + (ii * 2 + jj) * 16
                            sy.store("sp", [small], dbg["d_mod"][:, o_:o_ + 16], der[nm_][jj][:])
                    with nc.allow_non_contiguous_dma(reason="debug"):
                        sy.store("sp", [small], dbg["d_mod"][:, 320:321], nlam[:])
                        sy.store("sp", [small], dbg["d_mod"][:, 321:322], gdT[:])
                    sy.barrier()
                if stop_after == "mod":
                    return
                for rb in RANGES:
                    blocks = [BLOCKS[i] for i in rb]
                    x0 = blocks[0][0]
                    with ExitStack() as st:
                        XT = sb(st, "XT", [P, KC, 2304], BF16)
                        XT_t = {(k, t0): Trk() for k in range(KC) for (t0, n) in blocks}
                        stage_norm(l, blocks, XT, XT_t, None, 1)
                        if stop_after == "norm1":
                            if "d_xt" in dbg:
                                for k_ in range(KC):
                                    sy.store("sp", [small], dbg["d_xt"][k_ * P:(k_ + 1) * P, :], XT[:, k_, 0:2048])
                                sy.barrier()
                            return
                        stage_inproj(l, blocks, XT, XT_t)
                if l == 0:
                    dump("d_qkT", qkT, (26 * P, T), BF16)
                    dump("d_vS", vS.rearrange("h p x -> (h p) x"), (10 * P, NCH * P), BF16)
                if stop_after == "inproj":
                    return
                for rb in RANGES:
                    blocks = [BLOCKS[i] for i in rb if (i < 8 or with_ctx)]
                    x0 = blocks[0][0]
                    with ExitStack() as st:
                        XT = sb(st, "XT", [P, KC, 2304], BF16)
                        XT_t = {(k, t0): Trk() for k in range(KC) for (t0, n) in blocks}
                        stage_attention(l, blocks, XT, XT_t, with_ctx)
                        stage_gemm_fm(
                            lambda g: w_out[l, :, g * 512:(g + 1) * 512].rearrange("(k p) n -> p k n", p=P),
                            4, 512, KC, blocks, XT, XT_t, x0, yT, "wo")
                        if l == 0 and stop_after == "outproj":
                            continue
                        stage_norm(l, blocks, XT, XT_t, "gt1g", 2)
                        stage_gemm_fm(
                            lambda g: w_gate[l, :, g * 256:(g + 1) * 256].rearrange("(k p) n -> p k n", p=P),
                            22, 256, KC, blocks, XT, XT_t, x0, hidT, "wg", epi="swiglu",
                            wsrc2_fn=lambda g: w_up[l, :, g * 256:(g + 1) * 256].rearrange("(k p) n -> p k n", p=P))
                if l == 0 and stop_after == "outproj":
                    dump("d_yT", yT, (D, T), F32)
                    return
                stage_down(l, with_ctx)
                blocks = [BLOCKS[i] for i in range(9) if (i < 8 or with_ctx)]
                stage_norm(l, blocks, None, None, "gt2g", None)
                if l == 0:
                    dump("d_hT1", hT, (D, T), F32)
                    if stop_after == "layer0":
                        return
            stage_transpose_out()

        main()
        sy.barrier()
    return nc, sy


_CACHE = {}


def _host_tables():
    if "tabs" not in _CACHE:
        (C64, S64, PM64), (C128, S128, PM128) = _rope_tables()
        rope = np.stack([C64, S64, C128, S128]).astype(np.float32)
        pmm = np.stack([PM64, PM128]).astype(np.float32)
        idx, val = _na_index_tables()
        _CACHE["tabs"] = (rope, pmm, idx, val)
    return _CACHE["tabs"]


def kernel(x, c, ctx, c_ctx, w_mod, b_mod, g_pre1, g_post1, g_pre2, g_post2,
           w_in, w_out, lam_q1, lam_k1, lam_q2, lam_k2, g_diff, g_qn, g_kn, rpb,
           w_gate, w_up, w_down):
    f = lambda a: np.ascontiguousarray(np.asarray(a, dtype=np.float32))
    x, c, ctx, c_ctx = f(x), f(c), f(ctx), f(c_ctx)
    rope, pmm, idx, val = _host_tables()
    rpb = f(rpb)
    rpb_flat = rpb.reshape(DEPTH, 4, 15 * 31)
    nab = np.where(val[None, None], rpb_flat[:, :, idx], np.float32(-30000.0)).astype(np.float32)
    shared = {
        "w_mod": f(w_mod), "b_mod": f(b_mod),
        "gains": np.stack([f(g_pre1), f(g_post1), f(g_pre2), f(g_post2)]),
        "w_in": f(w_in), "w_out": f(w_out),
        "lam": np.stack([f(lam_q1), f(lam_k1), f(lam_q2), f(lam_k2)]),
        "hg": np.stack([f(g_diff), f(g_qn), f(g_kn)]),
        "w_gate": f(w_gate), "w_up": f(w_up), "w_down": f(w_down),
        "rope": rope, "pm": pmm, "nab": np.ascontiguousarray(nab),
        "ident": np.eye(P, dtype=np.float32),
    }
    nc, _ = build_program()
    in_maps = []
    for b in range(N_CORES):
        m = dict(shared)
        m["x"] = x[b]
        m["ctx"] = ctx[b]
        m["cc"] = np.stack([c[b], c_ctx])
        in_maps.append(m)
    res = run_bass_kernel_spmd(nc, in_maps, core_ids=list(range(N_CORES)))
    _CACHE["last_results"] = res
    return np.stack([np.asarray(res.results[b]["out"], dtype=np.float32) for b in range(N_CORES)])
```

```python
import math
import numpy as np
import concourse.bass as bass
import concourse.mybir as mybir
from concourse.bass_utils import run_bass_kernel_spmd
from contextlib import ExitStack

F32 = mybir.dt.float32
BF16 = mybir.dt.bfloat16
AF = mybir.ActivationFunctionType
ALU = mybir.AluOpType

P = 128
D = 2048
KC = D // P
S = 4096
CTX = 256
T = S + CTX
DEPTH = 2
GRID_W = 64
DFF = 5632
FC = DFF // P
INC = 4608
EPS = 1e-6
NCH = T // P
N_CORES = 4

BLOCKS = [(i * 512, 512) for i in range(8)] + [(4096, 256)]
RANGES = [[0, 1, 2, 3], [4, 5, 6, 7, 8]]

DEBUG = {}


class Trk:
    __slots__ = ("w", "r", "dsem", "excl")

    def __init__(self, excl=False):
        self.w = None
        self.r = {}
        self.dsem = None
        self.excl = excl


class Sync:
    CE = ("pe", "act", "dve", "pool")

    def __init__(self, nc, n_dma_sems=80):
        self.nc = nc
        self.eng = {"pe": nc.tensor, "act": nc.scalar, "dve": nc.vector, "pool": nc.gpsimd, "sp": nc.sync}
        self.sem = {e: nc.alloc_semaphore(f"prog_{e}") for e in self.CE}
        self.cnt = {e: 0 for e in self.CE}
        self.known = {e: {} for e in self.eng}
        self.free_dsems = [nc.alloc_semaphore(f"dsem{i}") for i in range(n_dma_sems)]
        self.dcnt = {}
        self.dsem_by_num = {}
        self.n_wait = 0
        self.n_ins = 0

    def _wait(self, e, waits):
        eng = self.eng[e]
        kn = self.known[e]
        for num, (sem, val) in waits.items():
            if kn.get(num, 0) >= val:
                continue
            eng.wait_ge(sem, val)
            kn[num] = val
            self.n_wait += 1

    @staticmethod
    def _need(waits, tk):
        if tk is None:
            return
        sem, val = tk
        cur = waits.get(sem.num)
        if cur is None or cur[1] < val:
            waits[sem.num] = (sem, val)

    def _collect(self, e, reads, writes):
        waits = {}
        own = self.sem[e].num if e in self.sem else None
        for t in reads:
            self._need(waits, t.w)
            if t.excl:
                for num, tk in t.r.items():
                    if num != own:
                        self._need(waits, tk)
        for t in writes:
            self._need(waits, t.w)
            for tk in t.r.values():
                self._need(waits, tk)
        if e == "pe":
            waits.pop(self.sem["pe"].num, None)
        return waits

    @staticmethod
    def _mark(tk, reads, writes):
        sem, val = tk
        for t in reads:
            cur = t.r.get(sem.num)
            if cur is None or cur[1] < val:
                t.r[sem.num] = tk
        for t in writes:
            t.w = tk
            t.r = {}

    def op(self, e, fn, reads=(), writes=(), signal=True):
        self._wait(e, self._collect(e, reads, writes))
        ins = fn(self.eng[e])
        self.n_ins += 1
        if signal:
            self.cnt[e] += 1
            ins.then_inc(self.sem[e], 1)
            tk = (self.sem[e], self.cnt[e])
        else:
            tk = (self.sem[e], self.cnt[e] + 1)
        self._mark(tk, reads, writes)
        return ins

    def _dsem(self, t):
        if t.dsem is None:
            t.dsem = self.free_dsems.pop()
            self.dsem_by_num[t.dsem.num] = t.dsem
            self.dcnt.setdefault(t.dsem.num, 0)
        return t.dsem

    def release(self, trks):
        for t in trks:
            if t.dsem is not None:
                self.free_dsems.append(t.dsem)
                t.dsem = None

    def load(self, q, trks, out_ap, in_ap, **kw):
        self._wait(q, self._collect(q, (), trks))
        sem = self._dsem(trks[0])
        ins = self.eng[q].dma_start(out=out_ap, in_=in_ap, **kw)
        self.dcnt[sem.num] += 1
        ins.then_inc(sem, 16)
        self.n_ins += 1
        self._mark((sem, 16 * self.dcnt[sem.num]), (), trks)

    def store(self, q, trks, out_ap, in_ap, **kw):
        self._wait(q, self._collect(q, trks, ()))
        sem = self._dsem(trks[0])
        ins = self.eng[q].dma_start(out=out_ap, in_=in_ap, **kw)
        self.dcnt[sem.num] += 1
        ins.then_inc(sem, 16)
        self.n_ins += 1
        self._mark((sem, 16 * self.dcnt[sem.num]), trks, ())

    def barrier(self):
        waits = {}
        for e in self.CE:
            if self.cnt[e] > 0:
                waits[self.sem[e].num] = (self.sem[e], self.cnt[e])
        for num, c in self.dcnt.items():
            if c > 0:
                waits[num] = (self.dsem_by_num[num], 16 * c)
        for e in self.eng:
            self._wait(e, dict(waits))


def _rope_tables():
    t = np.arange(S)
    row = (t // GRID_W).astype(np.float32)
    col = (t % GRID_W).astype(np.float32)

    def tab(d):
        half = d // 2
        quarter = half // 2
        inv = (np.float32(10000.0) ** (-np.arange(quarter, dtype=np.float32) / np.float32(quarter))).astype(np.float32)
        C = np.zeros((P, S), np.float32)
        Sg = np.zeros((P, S), np.float32)
        PM = np.zeros((P, P), np.float32)
        for p in range(P):
            m = p % d
            pos = row if (m // half) == 0 else col
            j = m % half
            i = j % quarter
            second = j // quarter
            ang = (pos * inv[i]).astype(np.float32)
            C[p] = np.cos(ang)
            Sg[p] = np.sin(ang) * (-1.0 if second == 0 else 1.0)
            partner = p + quarter if second == 0 else p - quarter
            PM[partner, p] = 1.0
        return C, Sg, PM

    return tab(64), tab(128)


def _na_index_tables():
    idx = np.zeros((3, 8, P, 512), np.int64)
    val = np.zeros((3, 8, P, 512), bool)
    for v, j in enumerate((0, 3, 7)):
        for i in range(8):
            kc = 4 * j - 2 + i
            kk = np.arange(P)
            krow = 2 * kc + kk // 64
            kcol = kk % 64
            qq = np.arange(512)
            qrow = 8 * j + qq // 64
            qcol = qq % 64
            r0 = np.clip(qrow - 4, 0, 64 - 8)
            c0 = np.clip(qcol - 8, 0, 64 - 16)
            rok = (krow[:, None] >= r0[None, :]) & (krow[:, None] < r0[None, :] + 8)
            cok = (kcol[:, None] >= c0[None, :]) & (kcol[:, None] < c0[None, :] + 16)
            roff = krow[:, None] - qrow[None, :] + 7
            coff = np.clip(kcol[:, None] - qcol[None, :], -15, 15) + 15
            ok = rok & cok & (krow[:, None] >= 0) & (krow[:, None] < 64)
            idx[v, i] = np.where(ok, roff * 31 + coff, 0)
            val[v, i] = ok
    return idx, val


def build_program():
    nc = bass.Bass("TRN2", target_bir_lowering=False)
    dt = nc.dram_tensor

    def din(name, shape, dtype=F32):
        return dt(name, list(shape), dtype, kind="ExternalInput").ap()

    x_in = din("x", [S, D])
    ctx_in = din("ctx", [CTX, D])
    cc_in = din("cc", [2, D])
    w_mod = din("w_mod", [DEPTH, D, 6 * D])
    b_mod = din("b_mod", [DEPTH, 6 * D])
    gains = din("gains", [4, DEPTH, D])
    w_in = din("w_in", [DEPTH, D, INC])
    w_out = din("w_out", [DEPTH, D, D])
    lam_in = din("lam", [4, DEPTH, 64])
    hg_in = din("hg", [3, DEPTH, P])
    w_gate = din("w_gate", [DEPTH, D, DFF])
    w_up = din("w_up", [DEPTH, D, DFF])
    w_down = din("w_down", [DEPTH, DFF, D])
    rope_in = din("rope", [4, P, S])
    pm_in = din("pm", [2, P, P])
    nab_in = din("nab", [DEPTH, 4, 3, 8, P, 512])
    ident_in = din("ident", [P, P])
    out = dt("out", [S, D], F32, kind="ExternalOutput").ap()

    hT = dt("hT", [D, T], F32, kind="Internal").ap()
    yT = dt("yT", [D, T], F32, kind="Internal").ap()
    qkT = dt("qkT", [26 * P, T], BF16, kind="Internal").ap()
    vS = dt("vS", [10, P, NCH * P], BF16, kind="Internal").ap()
    hidT = dt("hidT", [DFF, T], BF16, kind="Internal").ap()

    dbg = {}
    for name, shape, dtype in DEBUG.get("outs", []):
        dbg[name] = dt(name, list(shape), dtype, kind="ExternalOutput").ap()

    sy = Sync(nc)
    stop_after = DEBUG.get("stop_after")

    with ExitStack() as top:
        uid = [0]

        def sb(stack, name, shape, dtype):
            uid[0] += 1
            return stack.enter_context(nc.sbuf_tensor(f"s{uid[0]}_{name}", list(shape), dtype))

        ps_h = [top.enter_context(nc.psum_tensor(f"ps{i}", [P, 512], F32)) for i in range(8)]
        PS = [Trk(excl=True) for _ in range(8)]

        ident = sb(top, "ident", [P, P], F32)
        ones_bf = sb(top, "ones_bf", [P, P], BF16)
        ones_f = sb(top, "ones_f", [P, P], F32)
        pm = sb(top, "pm", [P, 2, P], F32)
        cst = Trk()
        sy.load("sp", [cst], ident[:], ident_in[:, :])
        sy.load("sp", [cst], pm[:], pm_in.rearrange("a p q -> p a q"))
        sy.op("dve", lambda e: e.memset(ones_bf[:], 1.0), writes=[cst])
        sy.op("dve", lambda e: e.memset(ones_f[:], 1.0), writes=[cst])
        pmb = sb(top, "pmb", [P, 2, P], BF16)
        sy.op("dve", lambda e: e.tensor_copy(pmb[:], pm[:]), reads=[cst], writes=[cst])
        modb = sb(top, "modb", [P, 96], F32)
        modc = sb(top, "modc", [P, 96], F32)
        der = {n: [sb(top, f"{n}{i}", [P, KC], F32) for i in range(2)]
               for n in ("a1", "gt1g", "a2", "gt2g")}
        gT = sb(top, "gT", [P, 4, KC], F32)
        hgT = sb(top, "hgT", [P, 3], F32)
        gdT = sb(top, "gdT", [P, 1], F32)
        nlam = sb(top, "nlam", [P, 1], F32)
        small = Trk()

        sy.barrier()

        def stage_transpose_in():
            with ExitStack() as st:
                xin = [sb(st, f"xin{i}", [P, 4, D], F32) for i in range(2)]
                xin_t = [Trk() for _ in range(2)]
                hb = [sb(st, f"hbo{i}", [P, KC, 512], F32) for i in range(2)]
                hb_t = [[Trk() for _ in range(KC)] for _ in range(2)]
                pi = 0
                for bi, (t0, n) in enumerate(BLOCKS):
                    nt = n // P
                    X, Xt = xin[bi % 2], xin_t[bi % 2]
                    H, Ht = hb[bi % 2], hb_t[bi % 2]
                    if t0 < S:
                        src = x_in[t0:t0 + n, :]
                    else:
                        src = ctx_in[:, :]
                    sy.load("sp", [Xt], X[:, 0:nt, :], src.rearrange("(j p) f -> p j f", p=P))
                    for k in range(KC):
                        ps, pst = ps_h[pi % 8], PS[pi % 8]
                        pi += 1
                        for j in range(nt):
                            sy.op("pe", lambda e, j=j, k=k, ps=ps, X=X: e.transpose(
                                ps[:, j * P:(j + 1) * P], X[:, j, k * P:(k + 1) * P], ident[:]),
                                reads=[Xt, cst], writes=[pst], signal=(j == nt - 1))
                        eng = "act" if k % 2 == 0 else "dve"
                        if eng == "act":
                            sy.op("act", lambda e, ps=ps, H=H, k=k, n=n: e.copy(H[:, k, 0:n], ps[:, 0:n]),
                                  reads=[pst], writes=[Ht[k]])
                        else:
                            sy.op("dve", lambda e, ps=ps, H=H, k=k, n=n: e.tensor_copy(H[:, k, 0:n], ps[:, 0:n]),
                                  reads=[pst], writes=[Ht[k]])
                    sy.store("sp", Ht, hT[:, t0:t0 + n].rearrange("(k p) t -> p k t", p=P), H[:, :, 0:n])
                sy.barrier()
                sy.release(xin_t + [t for l in hb_t for t in l])

        def stage_mod(l):
            with ExitStack() as st:
                cT = sb(st, "cT", [P, 2, KC], F32)
                scf = sb(st, "scf", [P, KC, 2], F32)
                scb = sb(st, "scb", [P, KC, 2], BF16)
                bmT = sb(st, "bmT", [P, 96], F32)
                lamc = sb(st, "lamc", [64, 4], F32)
                prod = sb(st, "prod", [64, 2], F32)
                ex = sb(st, "ex", [P, 2], F32)
                tmp = sb(st, "tmpm", [P, KC], F32)
                wm = [sb(st, f"wm{i}", [P, KC, 512], BF16) for i in range(2)]
                wm_t = [Trk() for _ in range(2)]
                tk = Trk()
                with nc.allow_non_contiguous_dma(reason="tiny transposed vector loads"):
                    sy.load("sp", [tk], cT[:], cc_in.rearrange("a (k p) -> p a k", p=P))
                    sy.load("sp", [tk], bmT[:], b_mod[l].rearrange("(c p) -> p c", p=P))
                    for a_ in range(4):
                        sy.load("sp", [small], gT[:, a_, :], gains[a_, l, :].rearrange("(k p) -> p k", p=P))
                    sy.load("sp", [small], hgT[:], hg_in[:, l, :].rearrange("a p -> p a"))
                    sy.load("sp", [tk], lamc[:], lam_in[:, l, :].rearrange("a p -> p a"))
                for a in range(2):
                    sy.op("act", lambda e, a=a: e.activation(scf[:, :, a], cT[:, a, :], AF.Silu),
                          reads=[tk], writes=[tk])
                sy.op("dve", lambda e: e.tensor_copy(scb[:], scf[:]), reads=[tk], writes=[tk])
                lam_init = 0.8 - 0.6 * math.exp(-0.3 * l)
                sy.op("dve", lambda e: e.tensor_tensor(prod[:, 0:1], lamc[:, 0:1], lamc[:, 1:2], ALU.mult),
                      reads=[tk], writes=[tk])
                sy.op("dve", lambda e: e.tensor_tensor(prod[:, 1:2], lamc[:, 2:3], lamc[:, 3:4], ALU.mult),
                      reads=[tk], writes=[tk])
                sy.op("pe", lambda e: e.matmul(ps_h[7][:, 0:2], lhsT=ones_f[0:64, :], rhs=prod[:, :],
                                               start=True, stop=True), reads=[tk, cst], writes=[PS[7]])
                sy.op("act", lambda e: e.activation(ex[:], ps_h[7][:, 0:2], AF.Exp), reads=[PS[7]], writes=[tk])
                sy.op("dve", lambda e: e.tensor_tensor(nlam[:], ex[:, 1:2], ex[:, 0:1], ALU.subtract),
                      reads=[tk], writes=[small])
                sy.op("dve", lambda e: e.tensor_scalar(nlam[:], nlam[:], -lam_init, None, ALU.add),
                      reads=[small], writes=[small])
                sy.op("dve", lambda e: e.tensor_scalar(gdT[:], hgT[:, 0:1], 1.0 - lam_init, None, ALU.mult),
                      reads=[small], writes=[small])
                NG = 24
                sy.load("pool", [wm_t[0]], wm[0][:], w_mod[l, :, 0:512].rearrange("(k p) n -> p k n", p=P))
                for g in range(NG):
                    if g + 1 < NG:
                        sy.load("pool", [wm_t[(g + 1) % 2]], wm[(g + 1) % 2][:],
                                w_mod[l, :, (g + 1) * 512:(g + 2) * 512].rearrange("(k p) n -> p k n", p=P))
                    W, Wt = wm[g % 2], wm_t[g % 2]
                    for j in range(4):
                        c = 4 * g + j
                        for k in range(KC):
                            sy.op("pe", lambda e, W=W, j=j, k=k, c=c: e.matmul(
                                ps_h[6][:, 2 * c:2 * c + 2], lhsT=W[:, k, j * P:(j + 1) * P], rhs=scb[:, k, :],
                                start=(k == 0), stop=(k == KC - 1)),
                                reads=[Wt, tk], writes=[PS[6]], signal=(k == KC - 1))
                psv = ps_h[6][:, 0:192].rearrange("p (c a) -> p c a", a=2)
                sy.op("dve", lambda e: e.tensor_tensor(modb[:], psv[:, :, 0], bmT[:], ALU.add),
                      reads=[PS[6], tk], writes=[small])
                sy.op("dve", lambda e: e.tensor_tensor(modc[:], psv[:, :, 1], bmT[:], ALU.add),
                      reads=[PS[6], tk], writes=[small])
                for i, mod in enumerate((modb, modc)):
                    for nm, sc_off, gidx in (("a1", 16, 0), ("a2", 64, 2)):
                        sy.op("dve", lambda e, mod=mod, sc_off=sc_off: e.tensor_scalar(
                            tmp[:], mod[:, sc_off:sc_off + 16], 1.0, None, ALU.add), reads=[small, tk], writes=[tk])
                        sy.op("dve", lambda e, nm=nm, i=i, gidx=gidx: e.tensor_tensor(
                            der[nm][i][:], tmp[:], gT[:, gidx, :], ALU.mult), reads=[tk, small], writes=[small])
                    for nm, gt_off, gidx in (("gt1g", 32, 1), ("gt2g", 80, 3)):
                        sy.op("dve", lambda e, nm=nm, i=i, gidx=gidx, mod=mod, gt_off=gt_off: e.tensor_tensor(
                            der[nm][i][:], mod[:, gt_off:gt_off + 16], gT[:, gidx, :], ALU.mult),
                            reads=[small], writes=[small])
                sy.barrier()
                sy.release([tk] + wm_t)

        def ssq_rstd(st_tiles, src_fn, srck_trks, n, d_count, ps_idx):
            sq, sq_t = st_tiles["sq"], st_tiles["sq_t"]
            ps, pst = ps_h[ps_idx], PS[ps_idx]
            for k in range(KC):
                s_, s_t = sq[k % len(sq)], sq_t[k % len(sq)]
                if k % 2 == 0:
                    sy.op("act", lambda e, s_=s_, k=k: e.activation(s_[:, 0:n], src_fn(k), AF.Square),
                          reads=[srck_trks[k]], writes=[s_t])
                else:
                    sy.op("dve", lambda e, s_=s_, k=k: e.tensor_tensor(s_[:, 0:n], src_fn(k), src_fn(k), ALU.mult),
                          reads=[srck_trks[k]], writes=[s_t])
                sy.op("pe", lambda e, s_=s_, k=k: e.matmul(ps[:, 0:n], lhsT=ones_bf[:], rhs=s_[:, 0:n],
                                                          start=(k == 0), stop=(k == KC - 1)),
                      reads=[s_t, cst], writes=[pst], signal=True)
            rt, rt_t = st_tiles["rt"], st_tiles["rt_t"]
            sy.op("act", lambda e: e.activation(rt[:, 0:n], ps[:, 0:n], AF.Sqrt, bias=st_tiles["eps"][:, 0:1],
                                                scale=1.0 / d_count),
                  reads=[pst, st_tiles["eps_t"]], writes=[rt_t])
            sy.op("dve", lambda e: e.reciprocal(rt[:, 0:n], rt[:, 0:n]), reads=[rt_t], writes=[rt_t])
            return rt, rt_t

        def stage_norm(l, blocks, XT, XT_t, resid, modn):
            with ExitStack() as st:
                NB = 2
                Hs = [sb(st, f"nH{i}", [P, KC, 256], F32) for i in range(NB)]
                H_t = [[Trk() for _ in range(KC)] for _ in range(NB)]
                Ys = [sb(st, f"nY{i}", [P, KC, 256], F32) for i in range(NB)] if resid else None
                Y_t = [[Trk() for _ in range(KC)] for _ in range(NB)]
                tiles = {
                    "sq": [sb(st, f"nsq{i}", [P, 512], BF16) for i in range(4)],
                    "sq_t": [Trk() for _ in range(4)],
                    "rt": None, "rt_t": None,
                    "eps": sb(st, "neps", [P, 1], F32),
                }
                eps_t = Trk()
                sy.op("dve", lambda e: e.memset(tiles["eps"][:], EPS), writes=[eps_t])
                tiles["eps_t"] = eps_t
                rts = [sb(st, f"nrt{i}", [P, 512], F32) for i in range(4)]
                rt_ts = [Trk() for _ in range(4)]
                tmps = [sb(st, f"ntmp{i}", [P, 512], F32) for i in range(4)]
                tmp_ts = [Trk() for _ in range(4)]
                ri = 0
                ti = 0
                x0 = blocks[0][0] if XT is not None else 0
                subs = []
                for (bt0, bn) in blocks:
                    for o in range(0, bn, 256):
                        subs.append((bt0 + o, 256, bt0))
                for bi, (t0, n, kt0) in enumerate(subs):
                    is_ctx = 1 if t0 >= S else 0
                    H, Ht = Hs[bi % NB], H_t[bi % NB]
                    sy.load("sp", Ht, H[:, :, 0:n], hT[:, t0:t0 + n].rearrange("(k p) t -> p k t", p=P))
                    if resid:
                        Y, Yt = Ys[bi % NB], Y_t[bi % NB]
                        sy.load("sp", Yt, Y[:, :, 0:n], yT[:, t0:t0 + n].rearrange("(k p) t -> p k t", p=P))
                        tiles["rt"], tiles["rt_t"] = rts[ri % 4], rt_ts[ri % 4]
                        ri += 1
                        r1, r1t = ssq_rstd(tiles, lambda k, Y=Y: Y[:, k, 0:n], Yt, n, D, 6)
                        gtg = der[resid][is_ctx]
                        for k in range(KC):
                            tm, tmt = tmps[ti % 4], tmp_ts[ti % 4]
                            ti += 1
                            sy.op("dve", lambda e, tm=tm, Y=Y, k=k, r1=r1: e.scalar_tensor_tensor(
                                tm[:, 0:n], Y[:, k, 0:n], gtg[:, k:k + 1], r1[:, 0:n], ALU.mult, ALU.mult),
                                reads=[Yt[k], r1t, small], writes=[tmt])
                            sy.op("pool", lambda e, tm=tm, H=H, k=k: e.tensor_tensor(
                                H[:, k, 0:n], tm[:, 0:n], H[:, k, 0:n], ALU.add),
                                reads=[tmt, Ht[k]], writes=[Ht[k]])
                    if modn:
                        tiles["rt"], tiles["rt_t"] = rts[ri % 4], rt_ts[ri % 4]
                        ri += 1
                        r2, r2t = ssq_rstd(tiles, lambda k, H=H: H[:, k, 0:n], Ht, n, D, 7)
                        a = der["a1" if modn == 1 else "a2"][is_ctx]
                        mod = modc if is_ctx else modb
                        sh_off = 0 if modn == 1 else 48
                        for k in range(KC):
                            tm, tmt = tmps[ti % 4], tmp_ts[ti % 4]
                            ti += 1
                            sy.op("dve", lambda e, tm=tm, H=H, k=k, r2=r2: e.tensor_tensor(
                                tm[:, 0:n], H[:, k, 0:n], r2[:, 0:n], ALU.mult),
                                reads=[Ht[k], r2t], writes=[tmt])
                            sy.op("act", lambda e, tm=tm, k=k, a=a, mod=mod: e.activation(
                                XT[:, k, t0 - x0:t0 - x0 + n], tm[:, 0:n], AF.Identity,
                                bias=mod[:, sh_off + k:sh_off + k + 1], scale=a[:, k:k + 1]),
                                reads=[tmt, small], writes=[XT_t[(k, kt0)]])
                    if resid:
                        sy.store("sp", Ht, hT[:, t0:t0 + n].rearrange("(k p) t -> p k t", p=P), H[:, :, 0:n])
                sy.barrier()
                sy.release([t for l_ in H_t for t in l_] + [t for l_ in Y_t for t in l_])

        def run_pending(pending):
            alive = []
            for g in pending:
                try:
                    next(g)
                    alive.append(g)
                except StopIteration:
                    pass
            pending[:] = alive

        def drain(pending):
            while pending:
                run_pending(pending)

        def inproj_plan(c):
            if c < 4:
                return ("rope64", c)
            if c < 8:
                return ("rope64", c)
            if c < 12:
                return ("v", c - 8)
            if c < 20:
                return ("nrope_q", 8 + (c - 12))
            if c < 22:
                return ("nrope_k", 16 + (c - 20))
            if c < 24:
                return ("v", 4 + (c - 22))
            if c < 28:
                return ("plain", 18 + (c - 24))
            if c < 32:
                return ("plain", 22 + (c - 28))
            return ("v", 6 + (c - 32))

        def stage_inproj(l, blocks, XT, XT_t):
            x0 = blocks[0][0]
            with ExitStack() as st:
                wt = [sb(st, f"wi{i}", [P, KC, 512], BF16) for i in range(2)]
                wt_t = [Trk() for _ in range(2)]
                rp = [sb(st, f"rp{i}", [P, 2, 512], F32) for i in range(2)]
                rp_t = [Trk() for _ in range(2)]
                NR = 6
                xf = [sb(st, f"xf{i}", [P, 512], F32) for i in range(NR)]
                xf_t = [Trk() for _ in range(NR)]
                xb = [sb(st, f"xb{i}", [P, 512], BF16) for i in range(NR)]
                xb_t = [Trk() for _ in range(NR)]
                sq = [sb(st, f"isq{i}", [P, 512], BF16) for i in range(2)]
                sq_t = [Trk() for _ in range(2)]
                rt = [sb(st, f"irt{i}", [P, 512], F32) for i in range(2)]
                rt_t = [Trk() for _ in range(2)]
                t1 = [sb(st, f"it1{i}", [P, 512], F32) for i in range(NR)]
                t1_t = [Trk() for _ in range(NR)]
                t2 = [sb(st, f"it2{i}", [P, 512], F32) for i in range(NR)]
                t2_t = [Trk() for _ in range(NR)]
                ob = [sb(st, f"iob{i}", [P, 512], BF16) for i in range(NR)]
                ob_t = [Trk() for _ in range(NR)]
                epsT = sb(st, "ieps", [P, 1], F32)
                eps_t = Trk()
                sy.op("dve", lambda e: e.memset(epsT[:], EPS), writes=[eps_t])
                cnt = {"main": 0, "aux": 0, "buf": 0, "rp": 0}
                pending = []

                def w_src(g):
                    return w_in[l, :, g * 512:(g + 1) * 512].rearrange("(k p) n -> p k n", p=P)

                def epi_fm(kind, dest, ps, pst, t0, n, rpi):
                    i = cnt["buf"] % NR
                    cnt["buf"] += 1
                    O, Ot = ob[i], ob_t[i]
                    latent = t0 < S
                    dst = qkT[dest * P:(dest + 1) * P, t0:t0 + n]
                    if DEBUG.get("inproj_mode") == "plain_all":
                        kind = "plain"
                    if DEBUG.get("inproj_mode") == "no_rope" and kind == "rope64":
                        kind = "plain"
                    if DEBUG.get("inproj_mode") == "no_nrope" and kind.startswith("nrope"):
                        kind = "plain"
                    if kind == "plain" or (kind == "rope64" and not latent):
                        sy.op("act", lambda e: e.copy(O[:, 0:n], ps[:, 0:n]), reads=[pst], writes=[Ot])
                        sy.store("sp", [Ot], dst, O[:, 0:n])
                        return
                    X, Xt_ = xf[i], xf_t[i]
                    XB, XBt = xb[i], xb_t[i]
                    if kind == "rope64":
                        sy.op("act", lambda e: e.copy(XB[:, 0:n], ps[:, 0:n]), reads=[pst], writes=[XBt])
                        tsel, pmi = 0, 0
                        xsrc, xsrc_t = ps, pst
                    else:
                        s_, s_t = sq[cnt["aux"] % 2], sq_t[cnt["aux"] % 2]
                        r_, r_t = rt[cnt["aux"] % 2], rt_t[cnt["aux"] % 2]
                        pa, pat = ps_h[6 + cnt["aux"] % 2], PS[6 + cnt["aux"] % 2]
                        cnt["aux"] += 1
                        sy.op("act", lambda e: e.activation(s_[:, 0:n], ps[:, 0:n], AF.Square),
                              reads=[pst], writes=[s_t])
                        yield
                        sy.op("pe", lambda e: e.matmul(pa[:, 0:n], lhsT=ones_bf[:], rhs=s_[:, 0:n],
                                                       start=True, stop=True), reads=[s_t, cst], writes=[pat])
                        sy.op("act", lambda e: e.activation(r_[:, 0:n], pa[:, 0:n], AF.Sqrt, bias=epsT[:, 0:1],
                                                            scale=1.0 / P), reads=[pat, eps_t], writes=[r_t])
                        sy.op("dve", lambda e: e.reciprocal(r_[:, 0:n], r_[:, 0:n]), reads=[r_t], writes=[r_t])
                        gcol = 1 if kind == "nrope_q" else 2
                        sy.op("act", lambda e: e.activation(X[:, 0:n], ps[:, 0:n], AF.Identity,
                                                            scale=hgT[:, gcol:gcol + 1]),
                              reads=[pst, small], writes=[Xt_])
                        sy.op("dve", lambda e: e.tensor_tensor(X[:, 0:n], X[:, 0:n], r_[:, 0:n], ALU.mult),
                              reads=[Xt_, r_t], writes=[Xt_])
                        tsel, pmi = 1, 1
                        xsrc, xsrc_t = X, Xt_
                        if not latent:
                            sy.op("act", lambda e: e.copy(O[:, 0:n], X[:, 0:n]), reads=[Xt_], writes=[Ot])
                            sy.store("sp", [Ot], dst, O[:, 0:n])
                            return
                        sy.op("act", lambda e: e.copy(XB[:, 0:n], X[:, 0:n]), reads=[Xt_], writes=[XBt])
                    yield
                    pb_i = 4 + cnt["main"] % 2
                    pb, pbt = ps_h[pb_i], PS[pb_i]
                    sy.op("pe", lambda e: e.matmul(pb[:, 0:n], lhsT=pmb[:, pmi, :], rhs=XB[:, 0:n],
                                                   start=True, stop=True), reads=[XBt, cst], writes=[pbt])
                    R, Rt = rp[rpi], rp_t[rpi]
                    A, At = t1[i], t1_t[i]
                    B, Bt = t2[i], t2_t[i]
                    sy.op("dve", lambda e: e.tensor_tensor(A[:, 0:n], xsrc[:, 0:n], R[:, 0, 0:n], ALU.mult),
                          reads=[xsrc_t, Rt], writes=[At])
                    sy.op("dve", lambda e: e.tensor_tensor(B[:, 0:n], pb[:, 0:n], R[:, 1, 0:n], ALU.mult),
                          reads=[pbt, Rt], writes=[Bt])
                    sy.op("dve", lambda e: e.tensor_tensor(O[:, 0:n], A[:, 0:n], B[:, 0:n], ALU.add),
                          reads=[At, Bt], writes=[Ot])
                    sy.store("sp", [Ot], dst, O[:, 0:n])

                NG = 9
                sy.load("pool", [wt_t[0]], wt[0][:], w_src(0))
                for g in range(NG):
                    if g + 1 < NG:
                        sy.load("pool", [wt_t[(g + 1) % 2]], wt[(g + 1) % 2][:], w_src(g + 1))
                    W, Wt = wt[g % 2], wt_t[g % 2]
                    plans = [inproj_plan(4 * g + j) for j in range(4)]
                    kinds = set(p_[0] for p_ in plans)
                    need_rope = None
                    if "rope64" in kinds:
                        need_rope = 0
                    elif "nrope_q" in kinds or "nrope_k" in kinds:
                        need_rope = 1
                    for (t0, n) in blocks:
                        rpi = None
                        if need_rope is not None and t0 < S:
                            rpi = cnt["rp"] % 2
                            cnt["rp"] += 1
                            sy.load("sp", [rp_t[rpi]], rp[rpi][:, :, 0:n],
                                    rope_in[2 * need_rope:2 * need_rope + 2, :, t0:t0 + n].rearrange("a p t -> p a t"))
                        for j, (kind, dest) in enumerate(plans):
                            if kind == "v":
                                continue
                            pi = cnt["main"] % 4
                            cnt["main"] += 1
                            ps, pst = ps_h[pi], PS[pi]
                            for k in range(KC):
                                sy.op("pe", lambda e, ps=ps, W=W, j=j, k=k: e.matmul(
                                    ps[:, 0:n], lhsT=W[:, k, j * P:(j + 1) * P],
                                    rhs=XT[:, k, t0 - x0:t0 - x0 + n], start=(k == 0), stop=(k == KC - 1)),
                                    reads=[Wt, XT_t[(k, t0)]], writes=[pst], signal=(k == KC - 1))
                            run_pending(pending)
                            gen = epi_fm(kind, dest, ps, pst, t0, n, rpi)
                            pending.append(gen)
                        vj = [j for j, p_ in enumerate(plans) if p_[0] == "v"]
                        if vj:
                            c0, c1 = vj[0] * P, (vj[-1] + 1) * P
                            ncol = c1 - c0
                            for tt in range(n // P):
                                pi = cnt["main"] % 4
                                cnt["main"] += 1
                                ps, pst = ps_h[pi], PS[pi]
                                ta = t0 - x0 + tt * P
                                for k in range(KC):
                                    sy.op("pe", lambda e, ps=ps, W=W, k=k, ta=ta: e.matmul(
                                        ps[:, 0:ncol], lhsT=XT[:, k, ta:ta + P], rhs=W[:, k, c0:c1],
                                        start=(k == 0), stop=(k == KC - 1)),
                                        reads=[Wt, XT_t[(k, t0)]], writes=[pst], signal=(k == KC - 1))
                                run_pending(pending)
                                i = cnt["buf"] % NR
                                cnt["buf"] += 1
                                O, Ot = ob[i], ob_t[i]
                                sy.op("act", lambda e, O=O, ps=ps: e.copy(O[:, 0:ncol], ps[:, 0:ncol]),
                                      reads=[pst], writes=[Ot])
                                chunk = (t0 + tt * P) // P
                                heads = [plans[j][1] for j in vj]
                                h0 = heads[0]
                                sy.store("sp", [Ot],
                                         vS[h0:h0 + len(heads), :, chunk * P:(chunk + 1) * P].rearrange("h p d -> p h d"),
                                         O[:, 0:ncol].rearrange("p (h d) -> p h d", d=P))
                drain(pending)
                sy.barrier()
                sy.release(wt_t + rp_t + ob_t)

        def stage_attention(l, blocks, XT, XT_t, with_ctx):
            x0 = blocks[0][0]
            with ExitStack() as st:
                KT = [sb(st, f"KT{i}", [P, T], BF16) for i in range(2)]
                KT_t = [Trk() for _ in range(2)]
                VT = [sb(st, f"VT{i}", [P, NCH, P], BF16) for i in range(2)]
                VT_t = [Trk() for _ in range(2)]
                QT = [sb(st, f"QT{i}", [P, 2304], BF16) for i in range(2)]
                QT_t = [Trk() for _ in range(2)]
                QZ = [[sb(st, f"QZ{i}_{m}", [P, 2304], BF16) for m in range(2)] for i in range(2)]
                QZ_t = [[Trk() for m in range(2)] for i in range(2)]
                for i in range(2):
                    sy.op("dve", lambda e, i=i: e.memset(QZ[i][0][64:128, :], 0.0), writes=[QZ_t[i][0]])
                    sy.op("dve", lambda e, i=i: e.memset(QZ[i][1][0:64, :], 0.0), writes=[QZ_t[i][1]])
                NE = 4
                E = [sb(st, f"E{i}", [P, 512], BF16) for i in range(NE)]
                E_t = [Trk() for _ in range(NE)]
                SBm = [sb(st, f"SBm{i}", [P, 512], F32) for i in range(2)]
                SB_t = [Trk() for _ in range(2)]
                NB = [sb(st, f"NB{i}", [P, 512], F32) for i in range(3)]
                NB_t = [Trk() for _ in range(3)]
                RD = [sb(st, f"RD{i}", [P, 512], F32) for i in range(2)]
                RD_t = [Trk() for _ in range(2)]
                O1 = sb(st, "O1", [P, 512], F32)
                O1_t = Trk()
                O2 = sb(st, "O2", [P, 512], F32)
                O2_t = Trk()
                OD = sb(st, "OD", [P, 512], F32)
                OD_t = Trk()
                SQ = sb(st, "aSQ", [P, 512], BF16)
                SQ_t = Trk()
                RT = sb(st, "aRT", [P, 512], F32)
                RT_t = Trk()
                epsT = sb(st, "aeps", [P, 1], F32)
                eps_t = Trk()
                sy.op("dve", lambda e: e.memset(epsT[:], EPS), writes=[eps_t])
                cnt = {"s": 0, "e": 0, "acc": 0, "kv": 0, "q": 0, "nb": 0, "sb": 0, "rd": 0}
                ntok = sum(n for _, n in blocks)
                qblocks = [(t0, n) for (t0, n) in blocks if (t0 < S or with_ctx)]

                def load_kv(kchunk, vhead):
                    i = cnt["kv"] % 2
                    cnt["kv"] += 1
                    sy.load("sp", [KT_t[i]], KT[i][:], qkT[kchunk * P:(kchunk + 1) * P, :])
                    sy.load("sp", [VT_t[i]], VT[i][:], vS[vhead].rearrange("p (c d) -> p c d", d=P))
                    return i

                def load_q(qchunk):
                    i = cnt["q"] % 2
                    cnt["q"] += 1
                    sy.load("sp", [QT_t[i]], QT[i][:, 0:ntok], qkT[qchunk * P:(qchunk + 1) * P, x0:x0 + ntok])
                    return i

                def load_qz(qchunk):
                    i = cnt["q"] % 2
                    cnt["q"] += 1
                    sy.load("sp", [QZ_t[i][0]], QZ[i][0][0:64, 0:ntok], qkT[qchunk * P:qchunk * P + 64, x0:x0 + ntok])
                    sy.load("sp", [QZ_t[i][1]], QZ[i][1][64:128, 0:ntok],
                            qkT[qchunk * P + 64:(qchunk + 1) * P, x0:x0 + ntok])
                    return i

                def core(kvi, Q_, Q_t, t0, n, scale, na=None):
                    a = cnt["acc"] % 2
                    cnt["acc"] += 1
                    po, pot = ps_h[3 + a], PS[3 + a]
                    pd, pdt = ps_h[5 + a], PS[5 + a]
                    K_, K_t, V_, V_t = KT[kvi], KT_t[kvi], VT[kvi], VT_t[kvi]
                    if t0 >= S:
                        chunks = [32, 33]
                    elif na is not None:
                        j = t0 // 512
                        chunks = [c for c in range(4 * j - 2, 4 * j + 6) if 0 <= c < 32] + [32, 33]
                    else:
                        chunks = list(range(NCH))
                    q0 = t0 - x0

                    def emit_s(c):
                        si = cnt["s"] % 3
                        cnt["s"] += 1
                        ps, pst = ps_h[si], PS[si]
                        sy.op("pe", lambda e: e.matmul(ps[:, 0:n], lhsT=K_[:, c * P:(c + 1) * P],
                                                       rhs=Q_[:, q0:q0 + n], start=True, stop=True),
                              reads=[K_t, Q_t], writes=[pst])
                        return ps, pst

                    nxt = emit_s(chunks[0])
                    for ci, c in enumerate(chunks):
                        ps, pst = nxt
                        if ci + 1 < len(chunks):
                            nxt = emit_s(chunks[ci + 1])
                        ei = cnt["e"] % NE
                        cnt["e"] += 1
                        E_, E_t_ = E[ei], E_t[ei]
                        if na is not None and c < 32:
                            h, var = na
                            j = t0 // 512
                            i_off = c - (4 * j - 2)
                            bi = cnt["nb"] % 3
                            cnt["nb"] += 1
                            sy.load("sp", [NB_t[bi]], NB[bi][:], nab_in[l, h, var, i_off])
                            sb_i = cnt["sb"] % 2
                            cnt["sb"] += 1
                            sy.op("dve", lambda e: e.scalar_tensor_tensor(
                                SBm[sb_i][:, 0:n], ps[:, 0:n], scale, NB[bi][:, 0:n], ALU.mult, ALU.add),
                                reads=[pst, NB_t[bi]], writes=[SB_t[sb_i]])
                            sy.op("act", lambda e: e.activation(E_[:, 0:n], SBm[sb_i][:, 0:n], AF.Exp),
                                  reads=[SB_t[sb_i]], writes=[E_t_])
                        else:
                            sy.op("act", lambda e: e.activation(E_[:, 0:n], ps[:, 0:n], AF.Exp, scale=scale),
                                  reads=[pst], writes=[E_t_])
                        first, last = ci == 0, ci == len(chunks) - 1
                        sy.op("pe", lambda e: e.matmul(po[:, 0:n], lhsT=V_[:, c, :], rhs=E_[:, 0:n],
                                                       start=first, stop=last),
                              reads=[V_t, E_t_], writes=[pot], signal=False)
                        sy.op("pe", lambda e: e.matmul(pd[:, 0:n], lhsT=ones_bf[:], rhs=E_[:, 0:n],
                                                       start=first, stop=last),
                              reads=[E_t_, cst], writes=[pdt], signal=True)
                    return po, pot, pd, pdt

                def recip(pd, pdt, n):
                    i = cnt["rd"] % 2
                    cnt["rd"] += 1
                    sy.op("dve", lambda e: e.reciprocal(RD[i][:, 0:n], pd[:, 0:n]), reads=[pdt], writes=[RD_t[i]])
                    return RD[i], RD_t[i]

                def finish_plain(xchunk, po, pot, pd, pdt, t0, n):
                    R, Rt = recip(pd, pdt, n)
                    sy.op("dve", lambda e: e.tensor_tensor(XT[:, xchunk, t0 - x0:t0 - x0 + n], po[:, 0:n],
                                                           R[:, 0:n], ALU.mult),
                          reads=[pot, Rt], writes=[XT_t[(xchunk, t0)]])

                sc_a = 64 ** -0.5
                for h in range(4):
                    kvi = load_kv(4 + h, h)
                    qi = load_qz(h)
                    for (t0, n) in qblocks:
                        po, pot, pd, pdt = core(kvi, QZ[qi][0], QZ_t[qi][0], t0, n, sc_a)
                        R, Rt = recip(pd, pdt, n)
                        sy.op("dve", lambda e: e.tensor_tensor(O1[:, 0:n], po[:, 0:n], R[:, 0:n], ALU.mult),
                              reads=[pot, Rt], writes=[O1_t])
                        po, pot, pd, pdt = core(kvi, QZ[qi][1], QZ_t[qi][1], t0, n, sc_a)
                        R, Rt = recip(pd, pdt, n)
                        sy.op("dve", lambda e: e.tensor_tensor(O2[:, 0:n], po[:, 0:n], R[:, 0:n], ALU.mult),
                              reads=[pot, Rt], writes=[O2_t])
                        sy.op("dve", lambda e: e.scalar_tensor_tensor(OD[:, 0:n], O2[:, 0:n], nlam[:, 0:1],
                                                                      O1[:, 0:n], ALU.mult, ALU.add),
                              reads=[O1_t, O2_t, small], writes=[OD_t])
                        sy.op("act", lambda e: e.activation(SQ[:, 0:n], OD[:, 0:n], AF.Square),
                              reads=[OD_t], writes=[SQ_t])
                        sy.op("pe", lambda e: e.matmul(ps_h[7][:, 0:n], lhsT=ones_bf[:], rhs=SQ[:, 0:n],
                                                       start=True, stop=True), reads=[SQ_t, cst], writes=[PS[7]])
                        sy.op("act", lambda e: e.activation(RT[:, 0:n], ps_h[7][:, 0:n], AF.Sqrt,
                                                            bias=epsT[:, 0:1], scale=1.0 / P),
                              reads=[PS[7], eps_t], writes=[RT_t])
                        sy.op("dve", lambda e: e.reciprocal(RT[:, 0:n], RT[:, 0:n]), reads=[RT_t], writes=[RT_t])
                        sy.op("dve", lambda e: e.scalar_tensor_tensor(
                            XT[:, h, t0 - x0:t0 - x0 + n], OD[:, 0:n], gdT[:, 0:1], RT[:, 0:n], ALU.mult, ALU.mult),
                            reads=[OD_t, RT_t, small], writes=[XT_t[(h, t0)]])
                sc_b = 128 ** -0.5
                for kvh in range(2):
                    kvi = load_kv(16 + kvh, 4 + kvh)
                    for gq in range(4):
                        hq = kvh * 4 + gq
                        qi = load_q(8 + hq)
                        for (t0, n) in qblocks:
                            po, pot, pd, pdt = core(kvi, QT[qi], QT_t[qi], t0, n, sc_b)
                            finish_plain(4 + hq, po, pot, pd, pdt, t0, n)
                for h in range(4):
                    kvi = load_kv(22 + h, 6 + h)
                    qi = load_q(18 + h)
                    for (t0, n) in qblocks:
                        if t0 < S:
                            j = t0 // 512
                            var = 0 if j == 0 else (2 if j == 7 else 1)
                            po, pot, pd, pdt = core(kvi, QT[qi], QT_t[qi], t0, n, sc_b, na=(h, var))
                        else:
                            po, pot, pd, pdt = core(kvi, QT[qi], QT_t[qi], t0, n, sc_b)
                        finish_plain(12 + h, po, pot, pd, pdt, t0, n)
                sy.barrier()
                sy.release(KT_t + VT_t + QT_t + NB_t + [t for l_ in QZ_t for t in l_])

        def stage_gemm_fm(wsrc_fn, ngroups, gcols, kchunks, blocks, XT, XT_t, x0, dstT, wname, epi="copy",
                          wsrc2_fn=None):
            nj = gcols // P
            with ExitStack() as st:
                nwb = 2
                wt = [sb(st, f"{wname}{i}", [P, kchunks, gcols], BF16) for i in range(nwb)]
                wt_t = [Trk() for _ in range(nwb)]
                if wsrc2_fn is not None:
                    wu = [sb(st, f"{wname}u{i}", [P, kchunks, gcols], BF16) for i in range(nwb)]
                    wu_t = [Trk() for _ in range(nwb)]
                NO = 4
                odt = BF16 if epi == "swiglu" else F32
                ob = [sb(st, f"{wname}o{i}", [P, 512], odt) for i in range(NO)]
                ob_t = [Trk() for _ in range(NO)]
                if epi == "swiglu":
                    sg = [sb(st, f"{wname}s{i}", [P, 512], F32) for i in range(NO)]
                    sg_t = [Trk() for _ in range(NO)]
                cnt = {"ps": 0, "o": 0}

                def issue_w(g):
                    sy.load("pool", [wt_t[g % nwb]], wt[g % nwb][:], wsrc_fn(g))
                    if wsrc2_fn is not None:
                        sy.load("pool", [wu_t[g % nwb]], wu[g % nwb][:], wsrc2_fn(g))

                issue_w(0)
                for g in range(ngroups):
                    if g + 1 < ngroups:
                        issue_w(g + 1)
                    W, Wt = wt[g % nwb], wt_t[g % nwb]
                    for (t0, n) in blocks:
                        for j in range(nj):
                            def mm(Wx, Wxt):
                                pi = cnt["ps"] % 8
                                cnt["ps"] += 1
                                ps, pst = ps_h[pi], PS[pi]
                                for k in range(kchunks):
                                    sy.op("pe", lambda e, k=k: e.matmul(
                                        ps[:, 0:n], lhsT=Wx[:, k, j * P:(j + 1) * P],
                                        rhs=XT[:, k, t0 - x0:t0 - x0 + n], start=(k == 0), stop=(k == kchunks - 1)),
                                        reads=[Wxt, XT_t[(k, t0)]], writes=[pst], signal=(k == kchunks - 1))
                                return ps, pst
                            ps, pst = mm(W, Wt)
                            oi = cnt["o"] % NO
                            cnt["o"] += 1
                            O, Ot = ob[oi], ob_t[oi]
                            row0 = g * gcols + j * P
                            if epi == "copy":
                                if oi % 2 == 0:
                                    sy.op("act", lambda e: e.copy(O[:, 0:n], ps[:, 0:n]), reads=[pst], writes=[Ot])
                                else:
                                    sy.op("dve", lambda e: e.tensor_copy(O[:, 0:n], ps[:, 0:n]), reads=[pst], writes=[Ot])
                            else:
                                ps2, pst2 = mm(wu[g % nwb], wu_t[g % nwb])
                                G, Gt = sg[oi], sg_t[oi]
                                sy.op("act", lambda e: e.activation(G[:, 0:n], ps[:, 0:n], AF.Silu),
                                      reads=[pst], writes=[Gt])
                                sy.op("dve", lambda e: e.tensor_tensor(O[:, 0:n], G[:, 0:n], ps2[:, 0:n], ALU.mult),
                                      reads=[Gt, pst2], writes=[Ot])
                            sy.store("sp", [Ot], dstT[row0:row0 + P, t0:t0 + n], O[:, 0:n])
                sy.barrier()
                sy.release(wt_t + ob_t + (wu_t if wsrc2_fn is not None else []))

        def stage_down(l, with_ctx):
            passes = [[0, 1], [2, 3], [4, 5], [6, 7] + ([8] if with_ctx else [])]
            for pblocks in passes:
                blocks = [BLOCKS[i] for i in pblocks]
                x0 = blocks[0][0]
                ntok = sum(n for _, n in blocks)
                with ExitStack() as st:
                    HX = sb(st, "HX", [P, FC, 1280], BF16)
                    HX_t = {}
                    ld = Trk()
                    for k in range(FC):
                        for (t0, n) in blocks:
                            HX_t[(k, t0)] = ld
                    lds = [Trk() for _ in range(4)]
                    for q in range(4):
                        for k in range(11 * q, 11 * q + 11):
                            for (t0, n) in blocks:
                                HX_t[(k, t0)] = lds[q]
                        sy.load("sp", [lds[q]], HX[:, 11 * q:11 * q + 11, 0:ntok],
                                hidT[11 * q * P:(11 * q + 11) * P, x0:x0 + ntok].rearrange("(k p) t -> p k t", p=P))
                    stage_gemm_fm(
                        lambda g: w_down[l, :, g * 256:(g + 1) * 256].rearrange("(k p) n -> p k n", p=P),
                        8, 256, FC, blocks, HX, HX_t, x0, yT, "wd")
                    sy.release(lds)

        def stage_transpose_out():
            with ExitStack() as st:
                hb = [sb(st, f"ohb{i}", [P, KC, 512], F32) for i in range(2)]
                hb_t = [Trk() for _ in range(2)]
                ox = [sb(st, f"oox{i}", [P, 4, D], F32) for i in range(2)]
                ox_t = [[Trk() for _ in range(16)] for _ in range(2)]
                pi = 0
                for bi, (t0, n) in enumerate(BLOCKS[:8]):
                    H, Ht = hb[bi % 2], hb_t[bi % 2]
                    OX, OXt = ox[bi % 2], ox_t[bi % 2]
                    sy.load("sp", [Ht], H[:], hT[:, t0:t0 + n].rearrange("(k p) t -> p k t", p=P))
                    for j in range(4):
                        for kg in range(4):
                            ps, pst = ps_h[pi % 8], PS[pi % 8]
                            pi += 1
                            for kk in range(4):
                                k = kg * 4 + kk
                                sy.op("pe", lambda e, ps=ps, kk=kk, k=k, j=j, H=H: e.transpose(
                                    ps[:, kk * P:(kk + 1) * P], H[:, k, j * P:(j + 1) * P], ident[:]),
                                    reads=[Ht, cst], writes=[pst], signal=(kk == 3))
                            tr = OXt[j * 4 + kg]
                            if (j * 4 + kg) % 2 == 0:
                                sy.op("act", lambda e, ps=ps, OX=OX, j=j, kg=kg: e.copy(
                                    OX[:, j, kg * 512:(kg + 1) * 512], ps[:, :]), reads=[pst], writes=[tr])
                            else:
                                sy.op("dve", lambda e, ps=ps, OX=OX, j=j, kg=kg: e.tensor_copy(
                                    OX[:, j, kg * 512:(kg + 1) * 512], ps[:, :]), reads=[pst], writes=[tr])
                    sy.store("sp", OXt, out[t0:t0 + n, :].rearrange("(j p) f -> p j f", p=P), OX[:])
                sy.barrier()
                sy.release(hb_t + [t for l_ in ox_t for t in l_])

        def dump(name, src_ap, shape, dtype):
            if name not in dbg:
                return
            rows, cols = shape
            with ExitStack() as st:
                tl = sb(st, "dbgt", [P, cols], dtype)
                tt = Trk()
                for r0 in range(0, rows, P):
                    sy.load("sp", [tt], tl[:], src_ap[r0:r0 + P, :])
                    sy.store("sp", [tt], dbg[name][r0:r0 + P, :], tl[:])
                sy.barrier()
                sy.release([tt])

        def main():
            stage_transpose_in()
            dump("d_hT0", hT, (D, T), F32)
            if stop_after == "x0":
                return
            for l in range(DEPTH):
                last = l == DEPTH - 1
                with_ctx = not last
                stage_mod(l)
                if l == 0 and "d_mod" in dbg:
                    tdm = Trk()
                    sy.store("sp", [small], dbg["d_mod"][:, 0:96], modb[:])
                    sy.store("sp", [small], dbg["d_mod"][:, 96:192], modc[:])
                    for ii, nm_ in enumerate(("a1", "gt1g", "a2", "gt2g")):
                        for jj in range(2):
                            o_ = 192 + (ii * 2 + jj) * 16
                            sy.store("sp", [small], dbg["d_mod"][:, o_:o_ + 16], der[nm_][jj][:])
                    with nc.allow_non_contiguous_dma(reason="debug"):
                        sy.store("sp", [small], dbg["d_mod"][:, 320:321], nlam[:])
                        sy.store("sp", [small], dbg["d_mod"][:, 321:322], gdT[:])
                    sy.barrier()
                if stop_after == "mod":
                    return
                for rb in RANGES:
                    blocks = [BLOCKS[i] for i in rb]
                    x0 = blocks[0][0]
                    with ExitStack() as st:
                        XT = sb(st, "XT", [P, KC, 2304], BF16)
                        XT_t = {(k, t0): Trk() for k in range(KC) for (t0, n) in blocks}
                        stage_norm(l, blocks, XT, XT_t, None, 1)
                        if stop_after == "norm1":
                            if "d_xt" in dbg:
                                for k_ in range(KC):
                                    sy.store("sp", [small], dbg["d_xt"][k_ * P:(k_ + 1) * P, :], XT[:, k_, 0:2048])
                                sy.barrier()
                            return
                        stage_inproj(l, blocks, XT, XT_t)
                if l == 0:
                    dump("d_qkT", qkT, (26 * P, T), BF16)
                    dump("d_vS", vS.rearrange("h p x -> (h p) x"), (10 * P, NCH * P), BF16)
                if stop_after == "inproj":
                    return
                for rb in RANGES:
                    blocks = [BLOCKS[i] for i in rb if (i < 8 or with_ctx)]
                    x0 = blocks[0][0]
                    with ExitStack() as st:
                        XT = sb(st, "XT", [P, KC, 2304], BF16)
                        XT_t = {(k, t0): Trk() for k in range(KC) for (t0, n) in blocks}
                        stage_attention(l, blocks, XT, XT_t, with_ctx)
                        stage_gemm_fm(
                            lambda g: w_out[l, :, g * 512:(g + 1) * 512].rearrange("(k p) n -> p k n", p=P),
                            4, 512, KC, blocks, XT, XT_t, x0, yT, "wo")
                        if l == 0 and stop_after == "outproj":
                            continue
                        stage_norm(l, blocks, XT, XT_t, "gt1g", 2)
                        stage_gemm_fm(
                            lambda g: w_gate[l, :, g * 256:(g + 1) * 256].rearrange("(k p) n -> p k n", p=P),
                            22, 256, KC, blocks, XT, XT_t, x0, hidT, "wg", epi="swiglu",
                            wsrc2_fn=lambda g: w_up[l, :, g * 256:(g + 1) * 256].rearrange("(k p) n -> p k n", p=P))
                if l == 0 and stop_after == "outproj":
                    dump("d_yT", yT, (D, T), F32)
                    return
                stage_down(l, with_ctx)
                blocks = [BLOCKS[i] for i in range(9) if (i < 8 or with_ctx)]
                stage_norm(l, blocks, None, None, "gt2g", None)
                if l == 0:
                    dump("d_hT1", hT, (D, T), F32)
                    if stop_after == "layer0":
                        return
            stage_transpose_out()

        main()
        sy.barrier()
    return nc, sy


_CACHE = {}


def _host_tables():
    if "tabs" not in _CACHE:
        (C64, S64, PM64), (C128, S128, PM128) = _rope_tables()
        rope = np.stack([C64, S64, C128, S128]).astype(np.float32)
        pmm = np.stack([PM64, PM128]).astype(np.float32)
        idx, val = _na_index_tables()
        _CACHE["tabs"] = (rope, pmm, idx, val)
    return _CACHE["tabs"]


def kernel(x, c, ctx, c_ctx, w_mod, b_mod, g_pre1, g_post1, g_pre2, g_post2,
           w_in, w_out, lam_q1, lam_k1, lam_q2, lam_k2, g_diff, g_qn, g_kn, rpb,
           w_gate, w_up, w_down):
    f = lambda a: np.ascontiguousarray(np.asarray(a, dtype=np.float32))
    x, c, ctx, c_ctx = f(x), f(c), f(ctx), f(c_ctx)
    rope, pmm, idx, val = _host_tables()
    rpb = f(rpb)
    rpb_flat = rpb.reshape(DEPTH, 4, 15 * 31)
    nab = np.where(val[None, None], rpb_flat[:, :, idx], np.float32(-30000.0)).astype(np.float32)
    shared = {
        "w_mod": f(w_mod), "b_mod": f(b_mod),
        "gains": np.stack([f(g_pre1), f(g_post1), f(g_pre2), f(g_post2)]),
        "w_in": f(w_in), "w_out": f(w_out),
        "lam": np.stack([f(lam_q1), f(lam_k1), f(lam_q2), f(lam_k2)]),
        "hg": np.stack([f(g_diff), f(g_qn), f(g_kn)]),
        "w_gate": f(w_gate), "w_up": f(w_up), "w_down": f(w_down),
        "rope": rope, "pm": pmm, "nab": np.ascontiguousarray(nab),
        "ident": np.eye(P, dtype=np.float32),
    }
    nc, _ = build_program()
    in_maps = []
    for b in range(N_CORES):
        m = dict(shared)
        m["x"] = x[b]
        m["ctx"] = ctx[b]
        m["cc"] = np.stack([c[b], c_ctx])
        in_maps.append(m)
    res = run_bass_kernel_spmd(nc, in_maps, core_ids=list(range(N_CORES)))
    _CACHE["last_results"] = res
    return np.stack([np.asarray(res.results[b]["out"], dtype=np.float32) for b in range(N_CORES)])
```

```python
import math
import numpy as np
import concourse.bass as bass
import concourse.mybir as mybir
from concourse.bass_utils import run_bass_kernel_spmd
from contextlib import ExitStack

F32 = mybir.dt.float32
BF16 = mybir.dt.bfloat16
AF = mybir.ActivationFunctionType
ALU = mybir.AluOpType

P = 128
D = 2048
KC = D // P
S = 4096
CTX = 256
T = S + CTX
DEPTH = 2
GRID_W = 64
DFF = 5632
FC = DFF // P
INC = 4608
EPS = 1e-6
NCH = T // P
N_CORES = 4

BLOCKS = [(i * 512, 512) for i in range(8)] + [(4096, 256)]
RANGES = [[0, 1, 2, 3], [4, 5, 6, 7, 8]]

DEBUG = {}


class Trk:
    __slots__ = ("w", "r", "dsem", "excl")

    def __init__(self, excl=False):
        self.w = None
        self.r = {}
        self.dsem = None
        self.excl = excl


class Sync:
    CE = ("pe", "act", "dve", "pool")

    def __init__(self, nc, n_dma_sems=80):
        self.nc = nc
        self.eng = {"pe": nc.tensor, "act": nc.scalar, "dve": nc.vector, "pool": nc.gpsimd, "sp": nc.sync}
        self.sem = {e: nc.alloc_semaphore(f"prog_{e}") for e in self.CE}
        self.cnt = {e: 0 for e in self.CE}
        self.known = {e: {} for e in self.eng}
        self.free_dsems = [nc.alloc_semaphore(f"dsem{i}") for i in range(n_dma_sems)]
        self.dcnt = {}
        self.dsem_by_num = {}
        self.n_wait = 0
        self.n_ins = 0

    def _wait(self, e, waits):
        eng = self.eng[e]
        kn = self.known[e]
        for num, (sem, val) in waits.items():
            if kn.get(num, 0) >= val:
                continue
            eng.wait_ge(sem, val)
            kn[num] = val
            self.n_wait += 1

    @staticmethod
    def _need(waits, tk):
        if tk is None:
            return
        sem, val = tk
        cur = waits.get(sem.num)
        if cur is None or cur[1] < val:
            waits[sem.num] = (sem, val)

    def _collect(self, e, reads, writes):
        waits = {}
        own = self.sem[e].num if e in self.sem else None
        for t in reads:
            self._need(waits, t.w)
            if t.excl:
                for num, tk in t.r.items():
                    if num != own:
                        self._need(waits, tk)
        for t in writes:
            self._need(waits, t.w)
            for tk in t.r.values():
                self._need(waits, tk)
        if e == "pe":
            waits.pop(self.sem["pe"].num, None)
        return waits

    @staticmethod
    def _mark(tk, reads, writes):
        sem, val = tk
        for t in reads:
            cur = t.r.get(sem.num)
            if cur is None or cur[1] < val:
                t.r[sem.num] = tk
        for t in writes:
            t.w = tk
            t.r = {}

    def op(self, e, fn, reads=(), writes=(), signal=True):
        self._wait(e, self._collect(e, reads, writes))
        ins = fn(self.eng[e])
        self.n_ins += 1
        if signal:
            self.cnt[e] += 1
            ins.then_inc(self.sem[e], 1)
            tk = (self.sem[e], self.cnt[e])
        else:
            tk = (self.sem[e], self.cnt[e] + 1)
        self._mark(tk, reads, writes)
        return ins

    def _dsem(self, t):
        if t.dsem is None:
            t.dsem = self.free_dsems.pop()
            self.dsem_by_num[t.dsem.num] = t.dsem
            self.dcnt.setdefault(t.dsem.num, 0)
        return t.dsem

    def release(self, trks):
        for t in trks:
            if t.dsem is not None:
                self.free_dsems.append(t.dsem)
                t.dsem = None

    def load(self, q, trks, out_ap, in_ap, **kw):
        self._wait(q, self._collect(q, (), trks))
        sem = self._dsem(trks[0])
        ins = self.eng[q].dma_start(out=out_ap, in_=in_ap, **kw)
        self.dcnt[sem.num] += 1
        ins.then_inc(sem, 16)
        self.n_ins += 1
        self._mark((sem, 16 * self.dcnt[sem.num]), (), trks)

    def store(self, q, trks, out_ap, in_ap, **kw):
        self._wait(q, self._collect(q, trks, ()))
        sem = self._dsem(trks[0])
        ins = self.eng[q].dma_start(out=out_ap, in_=in_ap, **kw)
        self.dcnt[sem.num] += 1
        ins.then_inc(sem, 16)
        self.n_ins += 1
        self._mark((sem, 16 * self.dcnt[sem.num]), trks, ())

    def barrier(self):
        waits = {}
        for e in self.CE:
            if self.cnt[e] > 0:
                waits[self.sem[e].num] = (self.sem[e], self.cnt[e])
        for num, c in self.dcnt.items():
            if c > 0:
                waits[num] = (self.dsem_by_num[num], 16 * c)
        for e in self.eng:
            self._wait(e, dict(waits))


def _rope_tables():
    t = np.arange(S)
    row = (t // GRID_W).astype(np.float32)
    col = (t % GRID_W).astype(np.float32)

    def tab(d):
        half = d // 2
        quarter = half // 2
        inv = (np.float32(10000.0) ** (-np.arange(quarter, dtype=np.float32) / np.float32(quarter))).astype(np.float32)
        C = np.zeros((P, S), np.float32)
        Sg = np.zeros((P, S), np.float32)
        PM = np.zeros((P, P), np.float32)
        for p in range(P):
            m = p % d
            pos = row if (m // half) == 0 else col
            j = m % half
            i = j % quarter
            second = j // quarter
            ang = (pos * inv[i]).astype(np.float32)
            C[p] = np.cos(ang)
            Sg[p] = np.sin(ang) * (-1.0 if second == 0 else 1.0)
            partner = p + quarter if second == 0 else p - quarter
            PM[partner, p] = 1.0
        return C, Sg, PM

    return tab(64), tab(128)


def _na_index_tables():
    idx = np.zeros((3, 8, P, 512), np.int64)
    val = np.zeros((3, 8, P, 512), bool)
    for v, j in enumerate((0, 3, 7)):
        for i in range(8):
            kc = 4 * j - 2 + i
            kk = np.arange(P)
            krow = 2 * kc + kk // 64
            kcol = kk % 64
            qq = np.arange(512)
            qrow = 8 * j + qq // 64
            qcol = qq % 64
            r0 = np.clip(qrow - 4, 0, 64 - 8)
            c0 = np.clip(qcol - 8, 0, 64 - 16)
            rok = (krow[:, None] >= r0[None, :]) & (krow[:, None] < r0[None, :] + 8)
            cok = (kcol[:, None] >= c0[None, :]) & (kcol[:, None] < c0[None, :] + 16)
            roff = krow[:, None] - qrow[None, :] + 7
            coff = np.clip(kcol[:, None] - qcol[None, :], -15, 15) + 15
            ok = rok & cok & (krow[:, None] >= 0) & (krow[:, None] < 64)
            idx[v, i] = np.where(ok, roff * 31 + coff, 0)
            val[v, i] = ok
    return idx, val


def build_program():
    nc = bass.Bass("TRN2", target_bir_lowering=False)
    dt = nc.dram_tensor

    def din(name, shape, dtype=F32):
        return dt(name, list(shape), dtype, kind="ExternalInput").ap()

    x_in = din("x", [S, D])
    ctx_in = din("ctx", [CTX, D])
    cc_in = din("cc", [2, D])
    w_mod = din("w_mod", [DEPTH, D, 6 * D])
    b_mod = din("b_mod", [DEPTH, 6 * D])
    gains = din("gains", [4, DEPTH, D])
    w_in = din("w_in", [DEPTH, D, INC])
    w_out = din("w_out", [DEPTH, D, D])
    lam_in = din("lam", [4, DEPTH, 64])
    hg_in = din("hg", [3, DEPTH, P])
    w_gate = din("w_gate", [DEPTH, D, DFF])
    w_up = din("w_up", [DEPTH, D, DFF])
    w_down = din("w_down", [DEPTH, DFF, D])
    rope_in = din("rope", [4, P, S])
    pm_in = din("pm", [2, P, P])
    nab_in = din("nab", [DEPTH, 4, 3, 8, P, 512])
    ident_in = din("ident", [P, P])
    out = dt("out", [S, D], F32, kind="ExternalOutput").ap()

    NSUB = T // 256
    hB = dt("hB", [NSUB, P, KC, 256], F32, kind="Internal").ap()
    yB = dt("yB", [NSUB, P, KC, 256], BF16, kind="Internal").ap()
    qkT = dt("qkT", [26 * P, T], BF16, kind="Internal").ap()
    vS = dt("vS", [10, P, NCH * P], BF16, kind="Internal").ap()
    hidT = dt("hidT", [DFF, T], BF16, kind="Internal").ap()

    dbg = {}
    for name, shape, dtype in DEBUG.get("outs", []):
        dbg[name] = dt(name, list(shape), dtype, kind="ExternalOutput").ap()

    sy = Sync(nc)
    stop_after = DEBUG.get("stop_after")

    with ExitStack() as top:
        uid = [0]

        def sb(stack, name, shape, dtype):
            uid[0] += 1
            return stack.enter_context(nc.sbuf_tensor(f"s{uid[0]}_{name}", list(shape), dtype))

        ps_h = [top.enter_context(nc.psum_tensor(f"ps{i}", [P, 512], F32)) for i in range(8)]
        PS = [Trk(excl=True) for _ in range(8)]

        ident = sb(top, "ident", [P, P], F32)
        ones_bf = sb(top, "ones_bf", [P, P], BF16)
        ones_f = sb(top, "ones_f", [P, P], F32)
        pm = sb(top, "pm", [P, 2, P], F32)
        cst = Trk()
        sy.load("sp", [cst], ident[:], ident_in[:, :])
        sy.load("sp", [cst], pm[:], pm_in.rearrange("a p q -> p a q"))
        sy.op("dve", lambda e: e.memset(ones_bf[:], 1.0), writes=[cst])
        sy.op("dve", lambda e: e.memset(ones_f[:], 1.0), writes=[cst])
        pmb = sb(top, "pmb", [P, 2, P], BF16)
        sy.op("dve", lambda e: e.tensor_copy(pmb[:], pm[:]), reads=[cst], writes=[cst])
        modb = sb(top, "modb", [P, 96], F32)
        modc = sb(top, "modc", [P, 96], F32)
        der = {n: [sb(top, f"{n}{i}", [P, KC], F32) for i in range(2)]
               for n in ("a1", "gt1g", "a2", "gt2g")}
        gT = sb(top, "gT", [P, 4, KC], F32)
        hgT = sb(top, "hgT", [P, 3], F32)
        gdT = sb(top, "gdT", [P, 1], F32)
        nlam = sb(top, "nlam", [P, 1], F32)
        small = Trk()

        sy.barrier()

        def stage_transpose_in():
            with ExitStack() as st:
                xin = [sb(st, f"xin{i}", [P, 4, D], F32) for i in range(2)]
                xin_t = [Trk() for _ in range(2)]
                hb = [sb(st, f"hbo{i}", [P, KC, 512], F32) for i in range(2)]
                hb_t = [[Trk() for _ in range(KC)] for _ in range(2)]
                pi = 0
                for bi, (t0, n) in enumerate(BLOCKS):
                    nt = n // P
                    X, Xt = xin[bi % 2], xin_t[bi % 2]
                    H, Ht = hb[bi % 2], hb_t[bi % 2]
                    if t0 < S:
                        src = x_in[t0:t0 + n, :]
                    else:
                        src = ctx_in[:, :]
                    sy.load("sp", [Xt], X[:, 0:nt, :], src.rearrange("(j p) f -> p j f", p=P))
                    for k in range(KC):
                        ps, pst = ps_h[pi % 8], PS[pi % 8]
                        pi += 1
                        for j in range(nt):
                            sy.op("pe", lambda e, j=j, k=k, ps=ps, X=X: e.transpose(
                                ps[:, j * P:(j + 1) * P], X[:, j, k * P:(k + 1) * P], ident[:]),
                                reads=[Xt, cst], writes=[pst], signal=(j == nt - 1))
                        eng = "act" if k % 2 == 0 else "dve"
                        if eng == "act":
                            sy.op("act", lambda e, ps=ps, H=H, k=k, n=n: e.copy(H[:, k, 0:n], ps[:, 0:n]),
                                  reads=[pst], writes=[Ht[k]])
                        else:
                            sy.op("dve", lambda e, ps=ps, H=H, k=k, n=n: e.tensor_copy(H[:, k, 0:n], ps[:, 0:n]),
                                  reads=[pst], writes=[Ht[k]])
                    for o in range(0, n, 256):
                        sy.store("sp", Ht, hB[(t0 + o) // 256], H[:, :, o:o + 256])
                sy.barrier()
                sy.release(xin_t + [t for l in hb_t for t in l])

        def stage_mod(l):
            with ExitStack() as st:
                cT = sb(st, "cT", [P, 2, KC], F32)
                scf = sb(st, "scf", [P, KC, 2], F32)
                scb = sb(st, "scb", [P, KC, 2], BF16)
                bmT = sb(st, "bmT", [P, 96], F32)
                lamc = sb(st, "lamc", [64, 4], F32)
                prod = sb(st, "prod", [64, 2], F32)
                ex = sb(st, "ex", [P, 2], F32)
                tmp = sb(st, "tmpm", [P, KC], F32)
                wm = [sb(st, f"wm{i}", [P, KC, 512], BF16) for i in range(2)]
                wm_t = [Trk() for _ in range(2)]
                tk = Trk()
                with nc.allow_non_contiguous_dma(reason="tiny transposed vector loads"):
                    sy.load("sp", [tk], cT[:], cc_in.rearrange("a (k p) -> p a k", p=P))
                    sy.load("sp", [tk], bmT[:], b_mod[l].rearrange("(c p) -> p c", p=P))
                    for a_ in range(4):
                        sy.load("sp", [small], gT[:, a_, :], gains[a_, l, :].rearrange("(k p) -> p k", p=P))
                    sy.load("sp", [small], hgT[:], hg_in[:, l, :].rearrange("a p -> p a"))
                    sy.load("sp", [tk], lamc[:], lam_in[:, l, :].rearrange("a p -> p a"))
                for a in range(2):
                    sy.op("act", lambda e, a=a: e.activation(scf[:, :, a], cT[:, a, :], AF.Silu),
                          reads=[tk], writes=[tk])
                sy.op("dve", lambda e: e.tensor_copy(scb[:], scf[:]), reads=[tk], writes=[tk])
                lam_init = 0.8 - 0.6 * math.exp(-0.3 * l)
                sy.op("dve", lambda e: e.tensor_tensor(prod[:, 0:1], lamc[:, 0:1], lamc[:, 1:2], ALU.mult),
                      reads=[tk], writes=[tk])
                sy.op("dve", lambda e: e.tensor_tensor(prod[:, 1:2], lamc[:, 2:3], lamc[:, 3:4], ALU.mult),
                      reads=[tk], writes=[tk])
                sy.op("pe", lambda e: e.matmul(ps_h[7][:, 0:2], lhsT=ones_f[0:64, :], rhs=prod[:, :],
                                               start=True, stop=True), reads=[tk, cst], writes=[PS[7]])
                sy.op("act", lambda e: e.activation(ex[:], ps_h[7][:, 0:2], AF.Exp), reads=[PS[7]], writes=[tk])
                sy.op("dve", lambda e: e.tensor_tensor(nlam[:], ex[:, 1:2], ex[:, 0:1], ALU.subtract),
                      reads=[tk], writes=[small])
                sy.op("dve", lambda e: e.tensor_scalar(nlam[:], nlam[:], -lam_init, None, ALU.add),
                      reads=[small], writes=[small])
                sy.op("dve", lambda e: e.tensor_scalar(gdT[:], hgT[:, 0:1], 1.0 - lam_init, None, ALU.mult),
                      reads=[small], writes=[small])
                NG = 24
                sy.load("pool", [wm_t[0]], wm[0][:], w_mod[l, :, 0:512].rearrange("(k p) n -> p k n", p=P))
                for g in range(NG):
                    if g + 1 < NG:
                        sy.load("pool", [wm_t[(g + 1) % 2]], wm[(g + 1) % 2][:],
                                w_mod[l, :, (g + 1) * 512:(g + 2) * 512].rearrange("(k p) n -> p k n", p=P))
                    W, Wt = wm[g % 2], wm_t[g % 2]
                    for j in range(4):
                        c = 4 * g + j
                        for k in range(KC):
                            sy.op("pe", lambda e, W=W, j=j, k=k, c=c: e.matmul(
                                ps_h[6][:, 2 * c:2 * c + 2], lhsT=W[:, k, j * P:(j + 1) * P], rhs=scb[:, k, :],
                                start=(k == 0), stop=(k == KC - 1)),
                                reads=[Wt, tk], writes=[PS[6]], signal=(k == KC - 1))
                psv = ps_h[6][:, 0:192].rearrange("p (c a) -> p c a", a=2)
                sy.op("dve", lambda e: e.tensor_tensor(modb[:], psv[:, :, 0], bmT[:], ALU.add),
                      reads=[PS[6], tk], writes=[small])
                sy.op("dve", lambda e: e.tensor_tensor(modc[:], psv[:, :, 1], bmT[:], ALU.add),
                      reads=[PS[6], tk], writes=[small])
                for i, mod in enumerate((modb, modc)):
                    for nm, sc_off, gidx in (("a1", 16, 0), ("a2", 64, 2)):
                        sy.op("dve", lambda e, mod=mod, sc_off=sc_off: e.tensor_scalar(
                            tmp[:], mod[:, sc_off:sc_off + 16], 1.0, None, ALU.add), reads=[small, tk], writes=[tk])
                        sy.op("dve", lambda e, nm=nm, i=i, gidx=gidx: e.tensor_tensor(
                            der[nm][i][:], tmp[:], gT[:, gidx, :], ALU.mult), reads=[tk, small], writes=[small])
                    for nm, gt_off, gidx in (("gt1g", 32, 1), ("gt2g", 80, 3)):
                        sy.op("dve", lambda e, nm=nm, i=i, gidx=gidx, mod=mod, gt_off=gt_off: e.tensor_tensor(
                            der[nm][i][:], mod[:, gt_off:gt_off + 16], gT[:, gidx, :], ALU.mult),
                            reads=[small], writes=[small])
                sy.barrier()
                sy.release([tk] + wm_t)

        def ssq_rstd(st_tiles, src_fn, srck_trks, n, d_count, ps_idx):
            sq, sq_t = st_tiles["sq"], st_tiles["sq_t"]
            ps, pst = ps_h[ps_idx], PS[ps_idx]
            for k in range(KC):
                s_, s_t = sq[k % len(sq)], sq_t[k % len(sq)]
                if k % 2 == 0:
                    sy.op("act", lambda e, s_=s_, k=k: e.activation(s_[:, 0:n], src_fn(k), AF.Square),
                          reads=[srck_trks[k]], writes=[s_t])
                else:
                    sy.op("dve", lambda e, s_=s_, k=k: e.tensor_tensor(s_[:, 0:n], src_fn(k), src_fn(k), ALU.mult),
                          reads=[srck_trks[k]], writes=[s_t])
                sy.op("pe", lambda e, s_=s_, k=k: e.matmul(ps[:, 0:n], lhsT=ones_bf[:], rhs=s_[:, 0:n],
                                                          start=(k == 0), stop=(k == KC - 1)),
                      reads=[s_t, cst], writes=[pst], signal=True)
            rt, rt_t = st_tiles["rt"], st_tiles["rt_t"]
            sy.op("act", lambda e: e.activation(rt[:, 0:n], ps[:, 0:n], AF.Sqrt, bias=st_tiles["eps"][:, 0:1],
                                                scale=1.0 / d_count),
                  reads=[pst, st_tiles["eps_t"]], writes=[rt_t])
            sy.op("dve", lambda e: e.reciprocal(rt[:, 0:n], rt[:, 0:n]), reads=[rt_t], writes=[rt_t])
            return rt, rt_t

        def stage_norm(l, blocks, XT, XT_t, resid, modn):
            with ExitStack() as st:
                NB = 2
                Hs = [sb(st, f"nH{i}", [P, KC, 256], F32) for i in range(NB)]
                H_t = [[Trk() for _ in range(KC)] for _ in range(NB)]
                Ys = [sb(st, f"nY{i}", [P, KC, 256], BF16) for i in range(NB)] if resid else None
                Y_t = [[Trk() for _ in range(KC)] for _ in range(NB)]
                tiles = {
                    "sq": [sb(st, f"nsq{i}", [P, 512], BF16) for i in range(4)],
                    "sq_t": [Trk() for _ in range(4)],
                    "rt": None, "rt_t": None,
                    "eps": sb(st, "neps", [P, 1], F32),
                }
                eps_t = Trk()
                sy.op("dve", lambda e: e.memset(tiles["eps"][:], EPS), writes=[eps_t])
                tiles["eps_t"] = eps_t
                rts = [sb(st, f"nrt{i}", [P, 512], F32) for i in range(4)]
                rt_ts = [Trk() for _ in range(4)]
                tmps = [sb(st, f"ntmp{i}", [P, 512], F32) for i in range(4)]
                tmp_ts = [Trk() for _ in range(4)]
                ri = 0
                ti = 0
                x0 = blocks[0][0] if XT is not None else 0
                subs = []
                for (bt0, bn) in blocks:
                    for o in range(0, bn, 256):
                        subs.append((bt0 + o, 256, bt0))
                for bi, (t0, n, kt0) in enumerate(subs):
                    is_ctx = 1 if t0 >= S else 0
                    H, Ht = Hs[bi % NB], H_t[bi % NB]
                    sy.load("sp", Ht, H[:, :, 0:n], hB[t0 // 256])
                    if resid:
                        Y, Yt = Ys[bi % NB], Y_t[bi % NB]
                        sy.load("sp", Yt, Y[:, :, 0:n], yB[t0 // 256])
                        tiles["rt"], tiles["rt_t"] = rts[ri % 4], rt_ts[ri % 4]
                        ri += 1
                        r1, r1t = ssq_rstd(tiles, lambda k, Y=Y: Y[:, k, 0:n], Yt, n, D, 6)
                        gtg = der[resid][is_ctx]
                        for k in range(KC):
                            tm, tmt = tmps[ti % 4], tmp_ts[ti % 4]
                            ti += 1
                            sy.op("dve", lambda e, tm=tm, Y=Y, k=k, r1=r1: e.scalar_tensor_tensor(
                                tm[:, 0:n], Y[:, k, 0:n], gtg[:, k:k + 1], r1[:, 0:n], ALU.mult, ALU.mult),
                                reads=[Yt[k], r1t, small], writes=[tmt])
                            sy.op("pool", lambda e, tm=tm, H=H, k=k: e.tensor_tensor(
                                H[:, k, 0:n], tm[:, 0:n], H[:, k, 0:n], ALU.add),
                                reads=[tmt, Ht[k]], writes=[Ht[k]])
                    if modn:
                        tiles["rt"], tiles["rt_t"] = rts[ri % 4], rt_ts[ri % 4]
                        ri += 1
                        r2, r2t = ssq_rstd(tiles, lambda k, H=H: H[:, k, 0:n], Ht, n, D, 7)
                        a = der["a1" if modn == 1 else "a2"][is_ctx]
                        mod = modc if is_ctx else modb
                        sh_off = 0 if modn == 1 else 48
                        for k in range(KC):
                            tm, tmt = tmps[ti % 4], tmp_ts[ti % 4]
                            ti += 1
                            sy.op("dve", lambda e, tm=tm, H=H, k=k, r2=r2: e.tensor_tensor(
                                tm[:, 0:n], H[:, k, 0:n], r2[:, 0:n], ALU.mult),
                                reads=[Ht[k], r2t], writes=[tmt])
                            sy.op("act", lambda e, tm=tm, k=k, a=a, mod=mod: e.activation(
                                XT[:, k, t0 - x0:t0 - x0 + n], tm[:, 0:n], AF.Identity,
                                bias=mod[:, sh_off + k:sh_off + k + 1], scale=a[:, k:k + 1]),
                                reads=[tmt, small], writes=[XT_t[(k, kt0)]])
                    if resid:
                        sy.store("sp", Ht, hB[t0 // 256], H[:, :, 0:n])
                sy.barrier()
                sy.release([t for l_ in H_t for t in l_] + [t for l_ in Y_t for t in l_])

        def run_pending(pending):
            alive = []
            for g in pending:
                try:
                    next(g)
                    alive.append(g)
                except StopIteration:
                    pass
            pending[:] = alive

        def drain(pending):
            while pending:
                run_pending(pending)

        def inproj_plan(c):
            if c < 4:
                return ("rope64", c)
            if c < 8:
                return ("rope64", c)
            if c < 12:
                return ("v", c - 8)
            if c < 20:
                return ("nrope_q", 8 + (c - 12))
            if c < 22:
                return ("nrope_k", 16 + (c - 20))
            if c < 24:
                return ("v", 4 + (c - 22))
            if c < 28:
                return ("plain", 18 + (c - 24))
            if c < 32:
                return ("plain", 22 + (c - 28))
            return ("v", 6 + (c - 32))

        def stage_inproj(l, blocks, XT, XT_t):
            x0 = blocks[0][0]
            with ExitStack() as st:
                wt = [sb(st, f"wi{i}", [P, KC, 512], BF16) for i in range(2)]
                wt_t = [Trk() for _ in range(2)]
                rp = [sb(st, f"rp{i}", [P, 2, 512], F32) for i in range(2)]
                rp_t = [Trk() for _ in range(2)]
                NR = 6
                xf = [sb(st, f"xf{i}", [P, 512], F32) for i in range(NR)]
                xf_t = [Trk() for _ in range(NR)]
                xb = [sb(st, f"xb{i}", [P, 512], BF16) for i in range(NR)]
                xb_t = [Trk() for _ in range(NR)]
                sq = [sb(st, f"isq{i}", [P, 512], BF16) for i in range(2)]
                sq_t = [Trk() for _ in range(2)]
                rt = [sb(st, f"irt{i}", [P, 512], F32) for i in range(2)]
                rt_t = [Trk() for _ in range(2)]
                t1 = [sb(st, f"it1{i}", [P, 512], F32) for i in range(NR)]
                t1_t = [Trk() for _ in range(NR)]
                t2 = [sb(st, f"it2{i}", [P, 512], F32) for i in range(NR)]
                t2_t = [Trk() for _ in range(NR)]
                ob = [sb(st, f"iob{i}", [P, 512], BF16) for i in range(NR)]
                ob_t = [Trk() for _ in range(NR)]
                epsT = sb(st, "ieps", [P, 1], F32)
                eps_t = Trk()
                sy.op("dve", lambda e: e.memset(epsT[:], EPS), writes=[eps_t])
                cnt = {"main": 0, "aux": 0, "buf": 0, "rp": 0}
                pending = []

                def w_src(g):
                    return w_in[l, :, g * 512:(g + 1) * 512].rearrange("(k p) n -> p k n", p=P)

                def epi_fm(kind, dest, ps, pst, t0, n, rpi):
                    i = cnt["buf"] % NR
                    cnt["buf"] += 1
                    O, Ot = ob[i], ob_t[i]
                    latent = t0 < S
                    dst = qkT[dest * P:(dest + 1) * P, t0:t0 + n]
                    if DEBUG.get("inproj_mode") == "plain_all":
                        kind = "plain"
                    if DEBUG.get("inproj_mode") == "no_rope" and kind == "rope64":
                        kind = "plain"
                    if DEBUG.get("inproj_mode") == "no_nrope" and kind.startswith("nrope"):
                        kind = "plain"
                    if kind == "plain" or (kind == "rope64" and not latent):
                        sy.op("act", lambda e: e.copy(O[:, 0:n], ps[:, 0:n]), reads=[pst], writes=[Ot])
                        sy.store("sp", [Ot], dst, O[:, 0:n])
                        return
                    X, Xt_ = xf[i], xf_t[i]
                    XB, XBt = xb[i], xb_t[i]
                    if kind == "rope64":
                        sy.op("act", lambda e: e.copy(XB[:, 0:n], ps[:, 0:n]), reads=[pst], writes=[XBt])
                        tsel, pmi = 0, 0
                        xsrc, xsrc_t = ps, pst
                    else:
                        s_, s_t = sq[cnt["aux"] % 2], sq_t[cnt["aux"] % 2]
                        r_, r_t = rt[cnt["aux"] % 2], rt_t[cnt["aux"] % 2]
                        pa, pat = ps_h[6 + cnt["aux"] % 2], PS[6 + cnt["aux"] % 2]
                        cnt["aux"] += 1
                        sy.op("act", lambda e: e.activation(s_[:, 0:n], ps[:, 0:n], AF.Square),
                              reads=[pst], writes=[s_t])
                        yield
                        sy.op("pe", lambda e: e.matmul(pa[:, 0:n], lhsT=ones_bf[:], rhs=s_[:, 0:n],
                                                       start=True, stop=True), reads=[s_t, cst], writes=[pat])
                        sy.op("act", lambda e: e.activation(r_[:, 0:n], pa[:, 0:n], AF.Sqrt, bias=epsT[:, 0:1],
                                                            scale=1.0 / P), reads=[pat, eps_t], writes=[r_t])
                        sy.op("dve", lambda e: e.reciprocal(r_[:, 0:n], r_[:, 0:n]), reads=[r_t], writes=[r_t])
                        gcol = 1 if kind == "nrope_q" else 2
                        sy.op("act", lambda e: e.activation(X[:, 0:n], ps[:, 0:n], AF.Identity,
                                                            scale=hgT[:, gcol:gcol + 1]),
                              reads=[pst, small], writes=[Xt_])
                        sy.op("dve", lambda e: e.tensor_tensor(X[:, 0:n], X[:, 0:n], r_[:, 0:n], ALU.mult),
                              reads=[Xt_, r_t], writes=[Xt_])
                        tsel, pmi = 1, 1
                        xsrc, xsrc_t = X, Xt_
                        if not latent:
                            sy.op("act", lambda e: e.copy(O[:, 0:n], X[:, 0:n]), reads=[Xt_], writes=[Ot])
                            sy.store("sp", [Ot], dst, O[:, 0:n])
                            return
                        sy.op("act", lambda e: e.copy(XB[:, 0:n], X[:, 0:n]), reads=[Xt_], writes=[XBt])
                    yield
                    pb_i = 4 + cnt["main"] % 2
                    pb, pbt = ps_h[pb_i], PS[pb_i]
                    sy.op("pe", lambda e: e.matmul(pb[:, 0:n], lhsT=pmb[:, pmi, :], rhs=XB[:, 0:n],
                                                   start=True, stop=True), reads=[XBt, cst], writes=[pbt])
                    R, Rt = rp[rpi], rp_t[rpi]
                    A, At = t1[i], t1_t[i]
                    B, Bt = t2[i], t2_t[i]
                    sy.op("dve", lambda e: e.tensor_tensor(A[:, 0:n], xsrc[:, 0:n], R[:, 0, 0:n], ALU.mult),
                          reads=[xsrc_t, Rt], writes=[At])
                    sy.op("dve", lambda e: e.tensor_tensor(B[:, 0:n], pb[:, 0:n], R[:, 1, 0:n], ALU.mult),
                          reads=[pbt, Rt], writes=[Bt])
                    sy.op("dve", lambda e: e.tensor_tensor(O[:, 0:n], A[:, 0:n], B[:, 0:n], ALU.add),
                          reads=[At, Bt], writes=[Ot])
                    sy.store("sp", [Ot], dst, O[:, 0:n])

                NG = 9
                sy.load("pool", [wt_t[0]], wt[0][:], w_src(0))
                for g in range(NG):
                    if g + 1 < NG:
                        sy.load("pool", [wt_t[(g + 1) % 2]], wt[(g + 1) % 2][:], w_src(g + 1))
                    W, Wt = wt[g % 2], wt_t[g % 2]
                    plans = [inproj_plan(4 * g + j) for j in range(4)]
                    kinds = set(p_[0] for p_ in plans)
                    need_rope = None
                    if "rope64" in kinds:
                        need_rope = 0
                    elif "nrope_q" in kinds or "nrope_k" in kinds:
                        need_rope = 1
                    for (t0, n) in blocks:
                        rpi = None
                        if need_rope is not None and t0 < S:
                            rpi = cnt["rp"] % 2
                            cnt["rp"] += 1
                            sy.load("sp", [rp_t[rpi]], rp[rpi][:, :, 0:n],
                                    rope_in[2 * need_rope:2 * need_rope + 2, :, t0:t0 + n].rearrange("a p t -> p a t"))
                        for j, (kind, dest) in enumerate(plans):
                            if kind == "v":
                                continue
                            pi = cnt["main"] % 4
                            cnt["main"] += 1
                            ps, pst = ps_h[pi], PS[pi]
                            for k in range(KC):
                                sy.op("pe", lambda e, ps=ps, W=W, j=j, k=k: e.matmul(
                                    ps[:, 0:n], lhsT=W[:, k, j * P:(j + 1) * P],
                                    rhs=XT[:, k, t0 - x0:t0 - x0 + n], start=(k == 0), stop=(k == KC - 1)),
                                    reads=[Wt, XT_t[(k, t0)]], writes=[pst], signal=(k == KC - 1))
                            run_pending(pending)
                            gen = epi_fm(kind, dest, ps, pst, t0, n, rpi)
                            pending.append(gen)
                        vj = [j for j, p_ in enumerate(plans) if p_[0] == "v"]
                        if vj:
                            c0, c1 = vj[0] * P, (vj[-1] + 1) * P
                            ncol = c1 - c0
                            for tt in range(n // P):
                                pi = cnt["main"] % 4
                                cnt["main"] += 1
                                ps, pst = ps_h[pi], PS[pi]
                                ta = t0 - x0 + tt * P
                                for k in range(KC):
                                    sy.op("pe", lambda e, ps=ps, W=W, k=k, ta=ta: e.matmul(
                                        ps[:, 0:ncol], lhsT=XT[:, k, ta:ta + P], rhs=W[:, k, c0:c1],
                                        start=(k == 0), stop=(k == KC - 1)),
                                        reads=[Wt, XT_t[(k, t0)]], writes=[pst], signal=(k == KC - 1))
                                run_pending(pending)
                                i = cnt["buf"] % NR
                                cnt["buf"] += 1
                                O, Ot = ob[i], ob_t[i]
                                sy.op("act", lambda e, O=O, ps=ps: e.copy(O[:, 0:ncol], ps[:, 0:ncol]),
                                      reads=[pst], writes=[Ot])
                                chunk = (t0 + tt * P) // P
                                heads = [plans[j][1] for j in vj]
                                h0 = heads[0]
                                sy.store("sp", [Ot],
                                         vS[h0:h0 + len(heads), :, chunk * P:(chunk + 1) * P].rearrange("h p d -> p h d"),
                                         O[:, 0:ncol].rearrange("p (h d) -> p h d", d=P))
                drain(pending)
                sy.barrier()
                sy.release(wt_t + rp_t + ob_t)

        def stage_attention(l, blocks, XT, XT_t, with_ctx):
            x0 = blocks[0][0]
            with ExitStack() as st:
                KT = [sb(st, f"KT{i}", [P, T], BF16) for i in range(2)]
                KT_t = [Trk() for _ in range(2)]
                VT = [sb(st, f"VT{i}", [P, NCH, P], BF16) for i in range(2)]
                VT_t = [Trk() for _ in range(2)]
                QT = [sb(st, f"QT{i}", [P, 2304], BF16) for i in range(2)]
                QT_t = [Trk() for _ in range(2)]
                QZ = [[sb(st, f"QZ{i}_{m}", [P, 2304], BF16) for m in range(2)] for i in range(2)]
                QZ_t = [[Trk() for m in range(2)] for i in range(2)]
                for i in range(2):
                    sy.op("dve", lambda e, i=i: e.memset(QZ[i][0][64:128, :], 0.0), writes=[QZ_t[i][0]])
                    sy.op("dve", lambda e, i=i: e.memset(QZ[i][1][0:64, :], 0.0), writes=[QZ_t[i][1]])
                NE = 4
                E = [sb(st, f"E{i}", [P, 512], BF16) for i in range(NE)]
                E_t = [Trk() for _ in range(NE)]
                SBm = [sb(st, f"SBm{i}", [P, 512], F32) for i in range(2)]
                SB_t = [Trk() for _ in range(2)]
                NB = [sb(st, f"NB{i}", [P, 512], F32) for i in range(3)]
                NB_t = [Trk() for _ in range(3)]
                RD = [sb(st, f"RD{i}", [P, 512], F32) for i in range(2)]
                RD_t = [Trk() for _ in range(2)]
                O1 = sb(st, "O1", [P, 512], F32)
                O1_t = Trk()
                O2 = sb(st, "O2", [P, 512], F32)
                O2_t = Trk()
                OD = sb(st, "OD", [P, 512], F32)
                OD_t = Trk()
                SQ = sb(st, "aSQ", [P, 512], BF16)
                SQ_t = Trk()
                RT = sb(st, "aRT", [P, 512], F32)
                RT_t = Trk()
                epsT = sb(st, "aeps", [P, 1], F32)
                eps_t = Trk()
                sy.op("dve", lambda e: e.memset(epsT[:], EPS), writes=[eps_t])
                cnt = {"s": 0, "e": 0, "acc": 0, "kv": 0, "q": 0, "nb": 0, "sb": 0, "rd": 0}
                ntok = sum(n for _, n in blocks)
                qblocks = [(t0, n) for (t0, n) in blocks if (t0 < S or with_ctx)]

                def load_kv(kchunk, vhead):
                    i = cnt["kv"] % 2
                    cnt["kv"] += 1
                    sy.load("sp", [KT_t[i]], KT[i][:], qkT[kchunk * P:(kchunk + 1) * P, :])
                    sy.load("sp", [VT_t[i]], VT[i][:], vS[vhead].rearrange("p (c d) -> p c d", d=P))
                    return i

                def load_q(qchunk):
                    i = cnt["q"] % 2
                    cnt["q"] += 1
                    sy.load("sp", [QT_t[i]], QT[i][:, 0:ntok], qkT[qchunk * P:(qchunk + 1) * P, x0:x0 + ntok])
                    return i

                def load_qz(qchunk):
                    i = cnt["q"] % 2
                    cnt["q"] += 1
                    sy.load("sp", [QZ_t[i][0]], QZ[i][0][0:64, 0:ntok], qkT[qchunk * P:qchunk * P + 64, x0:x0 + ntok])
                    sy.load("sp", [QZ_t[i][1]], QZ[i][1][64:128, 0:ntok],
                            qkT[qchunk * P + 64:(qchunk + 1) * P, x0:x0 + ntok])
                    return i

                def core(kvi, Q_, Q_t, t0, n, scale, na=None):
                    a = cnt["acc"] % 2
                    cnt["acc"] += 1
                    po, pot = ps_h[4 + a], PS[4 + a]
                    pd, pdt = ps_h[6 + a], PS[6 + a]
                    K_, K_t, V_, V_t = KT[kvi], KT_t[kvi], VT[kvi], VT_t[kvi]
                    if t0 >= S:
                        chunks = [32, 33]
                    elif na is not None:
                        j = t0 // 512
                        chunks = [c for c in range(4 * j - 2, 4 * j + 6) if 0 <= c < 32] + [32, 33]
                    else:
                        chunks = list(range(NCH))
                    q0 = t0 - x0

                    def emit_s(c):
                        si = cnt["s"] % 4
                        cnt["s"] += 1
                        ps, pst = ps_h[si], PS[si]
                        sy.op("pe", lambda e: e.matmul(ps[:, 0:n], lhsT=K_[:, c * P:(c + 1) * P],
                                                       rhs=Q_[:, q0:q0 + n], start=True, stop=True),
                              reads=[K_t, Q_t], writes=[pst])
                        return ps, pst

                    LA = 2
                    sq_ = [emit_s(chunks[i]) for i in range(min(LA, len(chunks)))]
                    for ci, c in enumerate(chunks):
                        ps, pst = sq_.pop(0)
                        if ci + LA < len(chunks):
                            sq_.append(emit_s(chunks[ci + LA]))
                        ei = cnt["e"] % NE
                        cnt["e"] += 1
                        E_, E_t_ = E[ei], E_t[ei]
                        if na is not None and c < 32:
                            h, var = na
                            j = t0 // 512
                            i_off = c - (4 * j - 2)
                            bi = cnt["nb"] % 3
                            cnt["nb"] += 1
                            sy.load("sp", [NB_t[bi]], NB[bi][:], nab_in[l, h, var, i_off])
                            sb_i = cnt["sb"] % 2
                            cnt["sb"] += 1
                            sy.op("dve", lambda e: e.scalar_tensor_tensor(
                                SBm[sb_i][:, 0:n], ps[:, 0:n], scale, NB[bi][:, 0:n], ALU.mult, ALU.add),
                                reads=[pst, NB_t[bi]], writes=[SB_t[sb_i]])
                            sy.op("act", lambda e: e.activation(E_[:, 0:n], SBm[sb_i][:, 0:n], AF.Exp),
                                  reads=[SB_t[sb_i]], writes=[E_t_])
                        else:
                            sy.op("act", lambda e: e.activation(E_[:, 0:n], ps[:, 0:n], AF.Exp, scale=scale),
                                  reads=[pst], writes=[E_t_])
                        first, last = ci == 0, ci == len(chunks) - 1
                        sy.op("pe", lambda e: e.matmul(po[:, 0:n], lhsT=V_[:, c, :], rhs=E_[:, 0:n],
                                                       start=first, stop=last),
                              reads=[V_t, E_t_], writes=[pot], signal=False)
                        sy.op("pe", lambda e: e.matmul(pd[:, 0:n], lhsT=ones_bf[:], rhs=E_[:, 0:n],
                                                       start=first, stop=last),
                              reads=[E_t_, cst], writes=[pdt], signal=True)
                    return po, pot, pd, pdt

                def recip(pd, pdt, n):
                    i = cnt["rd"] % 2
                    cnt["rd"] += 1
                    sy.op("dve", lambda e: e.reciprocal(RD[i][:, 0:n], pd[:, 0:n]), reads=[pdt], writes=[RD_t[i]])
                    return RD[i], RD_t[i]

                def finish_plain(xchunk, po, pot, pd, pdt, t0, n):
                    R, Rt = recip(pd, pdt, n)
                    sy.op("dve", lambda e: e.tensor_tensor(XT[:, xchunk, t0 - x0:t0 - x0 + n], po[:, 0:n],
                                                           R[:, 0:n], ALU.mult),
                          reads=[pot, Rt], writes=[XT_t[(xchunk, t0)]])

                sc_a = 64 ** -0.5
                for h in range(4):
                    kvi = load_kv(4 + h, h)
                    qi = load_qz(h)
                    for (t0, n) in qblocks:
                        po, pot, pd, pdt = core(kvi, QZ[qi][0], QZ_t[qi][0], t0, n, sc_a)
                        R, Rt = recip(pd, pdt, n)
                        sy.op("dve", lambda e: e.tensor_tensor(O1[:, 0:n], po[:, 0:n], R[:, 0:n], ALU.mult),
                              reads=[pot, Rt], writes=[O1_t])
                        po, pot, pd, pdt = core(kvi, QZ[qi][1], QZ_t[qi][1], t0, n, sc_a)
                        R, Rt = recip(pd, pdt, n)
                        sy.op("dve", lambda e: e.tensor_tensor(O2[:, 0:n], po[:, 0:n], R[:, 0:n], ALU.mult),
                              reads=[pot, Rt], writes=[O2_t])
                        sy.op("dve", lambda e: e.scalar_tensor_tensor(OD[:, 0:n], O2[:, 0:n], nlam[:, 0:1],
                                                                      O1[:, 0:n], ALU.mult, ALU.add),
                              reads=[O1_t, O2_t, small], writes=[OD_t])
                        sy.op("act", lambda e: e.activation(SQ[:, 0:n], OD[:, 0:n], AF.Square),
                              reads=[OD_t], writes=[SQ_t])
                        sbi = cnt["s"] % 4
                        cnt["s"] += 1
                        sy.op("pe", lambda e: e.matmul(ps_h[sbi][:, 0:n], lhsT=ones_bf[:], rhs=SQ[:, 0:n],
                                                       start=True, stop=True), reads=[SQ_t, cst], writes=[PS[sbi]])
                        sy.op("act", lambda e: e.activation(RT[:, 0:n], ps_h[sbi][:, 0:n], AF.Sqrt,
                                                            bias=epsT[:, 0:1], scale=1.0 / P),
                              reads=[PS[sbi], eps_t], writes=[RT_t])
                        sy.op("dve", lambda e: e.reciprocal(RT[:, 0:n], RT[:, 0:n]), reads=[RT_t], writes=[RT_t])
                        sy.op("dve", lambda e: e.scalar_tensor_tensor(
                            XT[:, h, t0 - x0:t0 - x0 + n], OD[:, 0:n], gdT[:, 0:1], RT[:, 0:n], ALU.mult, ALU.mult),
                            reads=[OD_t, RT_t, small], writes=[XT_t[(h, t0)]])
                sc_b = 128 ** -0.5
                for kvh in range(2):
                    kvi = load_kv(16 + kvh, 4 + kvh)
                    for gq in range(4):
                        hq = kvh * 4 + gq
                        qi = load_q(8 + hq)
                        for (t0, n) in qblocks:
                            po, pot, pd, pdt = core(kvi, QT[qi], QT_t[qi], t0, n, sc_b)
                            finish_plain(4 + hq, po, pot, pd, pdt, t0, n)
                for h in range(4):
                    kvi = load_kv(22 + h, 6 + h)
                    qi = load_q(18 + h)
                    for (t0, n) in qblocks:
                        if t0 < S:
                            j = t0 // 512
                            var = 0 if j == 0 else (2 if j == 7 else 1)
                            po, pot, pd, pdt = core(kvi, QT[qi], QT_t[qi], t0, n, sc_b, na=(h, var))
                        else:
                            po, pot, pd, pdt = core(kvi, QT[qi], QT_t[qi], t0, n, sc_b)
                        finish_plain(12 + h, po, pot, pd, pdt, t0, n)
                sy.barrier()
                sy.release(KT_t + VT_t + QT_t + NB_t + [t for l_ in QZ_t for t in l_])

        def stage_gemm_fm(wsrc_fn, ngroups, gcols, kchunks, blocks, XT, XT_t, x0, dstT, wname, epi="copy",
                          wsrc2_fn=None):
            nj = gcols // P
            with ExitStack() as st:
                nwb = 2
                wt = [sb(st, f"{wname}{i}", [P, kchunks, gcols], BF16) for i in range(nwb)]
                wt_t = [Trk() for _ in range(nwb)]
                if wsrc2_fn is not None:
                    wu = [sb(st, f"{wname}u{i}", [P, kchunks, gcols], BF16) for i in range(nwb)]
                    wu_t = [Trk() for _ in range(nwb)]
                NO = 4
                odt = BF16
                ob = [sb(st, f"{wname}o{i}", [P, 512], odt) for i in range(NO)]
                ob_t = [Trk() for _ in range(NO)]
                if epi == "swiglu":
                    sg = [sb(st, f"{wname}s{i}", [P, 512], F32) for i in range(NO)]
                    sg_t = [Trk() for _ in range(NO)]
                cnt = {"ps": 0, "o": 0}

                def issue_w(g):
                    sy.load("pool", [wt_t[g % nwb]], wt[g % nwb][:], wsrc_fn(g))
                    if wsrc2_fn is not None:
                        sy.load("pool", [wu_t[g % nwb]], wu[g % nwb][:], wsrc2_fn(g))

                issue_w(0)
                for g in range(ngroups):
                    if g + 1 < ngroups:
                        issue_w(g + 1)
                    W, Wt = wt[g % nwb], wt_t[g % nwb]
                    for (t0, n) in blocks:
                        for j in range(nj):
                            def mm(Wx, Wxt):
                                pi = cnt["ps"] % 8
                                cnt["ps"] += 1
                                ps, pst = ps_h[pi], PS[pi]
                                for k in range(kchunks):
                                    sy.op("pe", lambda e, k=k: e.matmul(
                                        ps[:, 0:n], lhsT=Wx[:, k, j * P:(j + 1) * P],
                                        rhs=XT[:, k, t0 - x0:t0 - x0 + n], start=(k == 0), stop=(k == kchunks - 1)),
                                        reads=[Wxt, XT_t[(k, t0)]], writes=[pst], signal=(k == kchunks - 1))
                                return ps, pst
                            ps, pst = mm(W, Wt)
                            oi = cnt["o"] % NO
                            cnt["o"] += 1
                            O, Ot = ob[oi], ob_t[oi]
                            row0 = g * gcols + j * P
                            if epi == "copy":
                                if oi % 2 == 0:
                                    sy.op("act", lambda e: e.copy(O[:, 0:n], ps[:, 0:n]), reads=[pst], writes=[Ot])
                                else:
                                    sy.op("dve", lambda e: e.tensor_copy(O[:, 0:n], ps[:, 0:n]), reads=[pst], writes=[Ot])
                            else:
                                ps2, pst2 = mm(wu[g % nwb], wu_t[g % nwb])
                                G, Gt = sg[oi], sg_t[oi]
                                sy.op("act", lambda e: e.activation(G[:, 0:n], ps[:, 0:n], AF.Silu),
                                      reads=[pst], writes=[Gt])
                                sy.op("dve", lambda e: e.tensor_tensor(O[:, 0:n], G[:, 0:n], ps2[:, 0:n], ALU.mult),
                                      reads=[Gt, pst2], writes=[Ot])
                            if dstT is None:
                                kk = row0 // P
                                s0 = t0 // 256
                                ns = n // 256
                                sy.store("sp", [Ot], yB[s0:s0 + ns, :, kk, :].rearrange("s p t -> p s t"),
                                         O[:, 0:n].rearrange("p (s t) -> p s t", t=256))
                            else:
                                sy.store("sp", [Ot], dstT[row0:row0 + P, t0:t0 + n], O[:, 0:n])
                sy.barrier()
                sy.release(wt_t + ob_t + (wu_t if wsrc2_fn is not None else []))

        def stage_down(l, with_ctx):
            passes = [[0, 1], [2, 3], [4, 5], [6, 7] + ([8] if with_ctx else [])]
            for pblocks in passes:
                blocks = [BLOCKS[i] for i in pblocks]
                x0 = blocks[0][0]
                ntok = sum(n for _, n in blocks)
                with ExitStack() as st:
                    HX = sb(st, "HX", [P, FC, 1280], BF16)
                    HX_t = {}
                    ld = Trk()
                    for k in range(FC):
                        for (t0, n) in blocks:
                            HX_t[(k, t0)] = ld
                    lds = [Trk() for _ in range(4)]
                    for q in range(4):
                        for k in range(11 * q, 11 * q + 11):
                            for (t0, n) in blocks:
                                HX_t[(k, t0)] = lds[q]
                        sy.load("sp", [lds[q]], HX[:, 11 * q:11 * q + 11, 0:ntok],
                                hidT[11 * q * P:(11 * q + 11) * P, x0:x0 + ntok].rearrange("(k p) t -> p k t", p=P))
                    stage_gemm_fm(
                        lambda g: w_down[l, :, g * 256:(g + 1) * 256].rearrange("(k p) n -> p k n", p=P),
                        8, 256, FC, blocks, HX, HX_t, x0, None, "wd")
                    sy.release(lds)

        def stage_transpose_out():
            with ExitStack() as st:
                hb = [sb(st, f"ohb{i}", [P, KC, 512], F32) for i in range(2)]
                hb_t = [Trk() for _ in range(2)]
                ox = [sb(st, f"oox{i}", [P, 4, D], F32) for i in range(2)]
                ox_t = [[Trk() for _ in range(16)] for _ in range(2)]
                pi = 0
                for bi, (t0, n) in enumerate(BLOCKS[:8]):
                    H, Ht = hb[bi % 2], hb_t[bi % 2]
                    OX, OXt = ox[bi % 2], ox_t[bi % 2]
                    for o in range(0, n, 256):
                        sy.load("sp", [Ht], H[:, :, o:o + 256], hB[(t0 + o) // 256])
                    for j in range(4):
                        for kg in range(4):
                            ps, pst = ps_h[pi % 8], PS[pi % 8]
                            pi += 1
                            for kk in range(4):
                                k = kg * 4 + kk
                                sy.op("pe", lambda e, ps=ps, kk=kk, k=k, j=j, H=H: e.transpose(
                                    ps[:, kk * P:(kk + 1) * P], H[:, k, j * P:(j + 1) * P], ident[:]),
                                    reads=[Ht, cst], writes=[pst], signal=(kk == 3))
                            tr = OXt[j * 4 + kg]
                            if (j * 4 + kg) % 2 == 0:
                                sy.op("act", lambda e, ps=ps, OX=OX, j=j, kg=kg: e.copy(
                                    OX[:, j, kg * 512:(kg + 1) * 512], ps[:, :]), reads=[pst], writes=[tr])
                            else:
                                sy.op("dve", lambda e, ps=ps, OX=OX, j=j, kg=kg: e.tensor_copy(
                                    OX[:, j, kg * 512:(kg + 1) * 512], ps[:, :]), reads=[pst], writes=[tr])
                    sy.store("sp", OXt, out[t0:t0 + n, :].rearrange("(j p) f -> p j f", p=P), OX[:])
                sy.barrier()
                sy.release(hb_t + [t for l_ in ox_t for t in l_])

        def dump(name, src_ap, shape, dtype):
            if name not in dbg:
                return
            rows, cols = shape
            with ExitStack() as st:
                tl = sb(st, "dbgt", [P, cols], dtype)
                tt = Trk()
                for r0 in range(0, rows, P):
                    sy.load("sp", [tt], tl[:], src_ap[r0:r0 + P, :])
                    sy.store("sp", [tt], dbg[name][r0:r0 + P, :], tl[:])
                sy.barrier()
                sy.release([tt])

        def main():
            stage_transpose_in()
            if stop_after == "x0":
                return
            for l in range(DEPTH):
                last = l == DEPTH - 1
                with_ctx = not last
                stage_mod(l)
                if l == 0 and "d_mod" in dbg:
                    tdm = Trk()
                    sy.store("sp", [small], dbg["d_mod"][:, 0:96], modb[:])
                    sy.store("sp", [small], dbg["d_mod"][:, 96:192], modc[:])
                    for ii, nm_ in enumerate(("a1", "gt1g", "a2", "gt2g")):
                        for jj in range(2):
                            o_ = 192 + (ii * 2 + jj) * 16
                            sy.store("sp", [small], dbg["d_mod"][:, o_:o_ + 16], der[nm_][jj][:])
                    with nc.allow_non_contiguous_dma(reason="debug"):
                        sy.store("sp", [small], dbg["d_mod"][:, 320:321], nlam[:])
                        sy.store("sp", [small], dbg["d_mod"][:, 321:322], gdT[:])
                    sy.barrier()
                if stop_after == "mod":
                    return
                for rb in RANGES:
                    blocks = [BLOCKS[i] for i in rb]
                    x0 = blocks[0][0]
                    with ExitStack() as st:
                        XT = sb(st, "XT", [P, KC, 2304], BF16)
                        XT_t = {(k, t0): Trk() for k in range(KC) for (t0, n) in blocks}
                        stage_norm(l, blocks, XT, XT_t, None, 1)
                        if stop_after == "norm1":
                            if "d_xt" in dbg:
                                for k_ in range(KC):
                                    sy.store("sp", [small], dbg["d_xt"][k_ * P:(k_ + 1) * P, :], XT[:, k_, 0:2048])
                                sy.barrier()
                            return
                        stage_inproj(l, blocks, XT, XT_t)
                if l == 0:
                    dump("d_qkT", qkT, (26 * P, T), BF16)
                    dump("d_vS", vS.rearrange("h p x -> (h p) x"), (10 * P, NCH * P), BF16)
                if stop_after == "inproj":
                    return
                for rb in RANGES:
                    blocks = [BLOCKS[i] for i in rb if (i < 8 or with_ctx)]
                    x0 = blocks[0][0]
                    with ExitStack() as st:
                        XT = sb(st, "XT", [P, KC, 2304], BF16)
                        XT_t = {(k, t0): Trk() for k in range(KC) for (t0, n) in blocks}
                        stage_attention(l, blocks, XT, XT_t, with_ctx)
                        stage_gemm_fm(
                            lambda g: w_out[l, :, g * 512:(g + 1) * 512].rearrange("(k p) n -> p k n", p=P),
                            4, 512, KC, blocks, XT, XT_t, x0, None, "wo")
                        if l == 0 and stop_after == "outproj":
                            continue
                        stage_norm(l, blocks, XT, XT_t, "gt1g", 2)
                        stage_gemm_fm(
                            lambda g: w_gate[l, :, g * 256:(g + 1) * 256].rearrange("(k p) n -> p k n", p=P),
                            22, 256, KC, blocks, XT, XT_t, x0, hidT, "wg", epi="swiglu",
                            wsrc2_fn=lambda g: w_up[l, :, g * 256:(g + 1) * 256].rearrange("(k p) n -> p k n", p=P))
                if l == 0 and stop_after == "outproj":
                    return
                stage_down(l, with_ctx)
                blocks = [BLOCKS[i] for i in range(9) if (i < 8 or with_ctx)]
                stage_norm(l, blocks, None, None, "gt2g", None)
                if l == 0:
                    if stop_after == "layer0":
                        return
            stage_transpose_out()

        main()
        sy.barrier()
    return nc, sy


_CACHE = {}


def _host_tables():
    if "tabs" not in _CACHE:
        (C64, S64, PM64), (C128, S128, PM128) = _rope_tables()
        rope = np.stack([C64, S64, C128, S128]).astype(np.float32)
        pmm = np.stack([PM64, PM128]).astype(np.float32)
        idx, val = _na_index_tables()
        _CACHE["tabs"] = (rope, pmm, idx, val)
    return _CACHE["tabs"]


def kernel(x, c, ctx, c_ctx, w_mod, b_mod, g_pre1, g_post1, g_pre2, g_post2,
           w_in, w_out, lam_q1, lam_k1, lam_q2, lam_k2, g_diff, g_qn, g_kn, rpb,
           w_gate, w_up, w_down):
    f = lambda a: np.ascontiguousarray(np.asarray(a, dtype=np.float32))
    x, c, ctx, c_ctx = f(x), f(c), f(ctx), f(c_ctx)
    rope, pmm, idx, val = _host_tables()
    rpb = f(rpb)
    rpb_flat = rpb.reshape(DEPTH, 4, 15 * 31)
    nab = np.where(val[None, None], rpb_flat[:, :, idx], np.float32(-30000.0)).astype(np.float32)
    shared = {
        "w_mod": f(w_mod), "b_mod": f(b_mod),
        "gains": np.stack([f(g_pre1), f(g_post1), f(g_pre2), f(g_post2)]),
        "w_in": f(w_in), "w_out": f(w_out),
        "lam": np.stack([f(lam_q1), f(lam_k1), f(lam_q2), f(lam_k2)]),
        "hg": np.stack([f(g_diff), f(g_qn), f(g_kn)]),
        "w_gate": f(w_gate), "w_up": f(w_up), "w_down": f(w_down),
        "rope": rope, "pm": pmm, "nab": np.ascontiguousarray(nab),
        "ident": np.eye(P, dtype=np.float32),
    }
    nc, _ = build_program()
    in_maps = []
    for b in range(N_CORES):
        m = dict(shared)
        m["x"] = x[b]
        m["ctx"] = ctx[b]
        m["cc"] = np.stack([c[b], c_ctx])
        in_maps.append(m)
    res = run_bass_kernel_spmd(nc, in_maps, core_ids=list(range(N_CORES)))
    _CACHE["last_results"] = res
    return np.stack([np.asarray(res.results[b]["out"], dtype=np.float32) for b in range(N_CORES)])
```

```python
import math
import numpy as np
import concourse.bass as bass
import concourse.mybir as mybir
from concourse.bass_utils import run_bass_kernel_spmd
from contextlib import ExitStack

F32 = mybir.dt.float32
BF16 = mybir.dt.bfloat16
AF = mybir.ActivationFunctionType
ALU = mybir.AluOpType

P = 128
D = 2048
KC = D // P
S = 4096
CTX = 256
T = S + CTX
DEPTH = 2
GRID_W = 64
DFF = 5632
FC = DFF // P
INC = 4608
EPS = 1e-6
NCH = T // P
N_CORES = 4
ACTIVE_CORES = (0, 1, 4, 5)

BLOCKS = [(i * 512, 512) for i in range(8)] + [(4096, 256)]
RANGES = [[0, 1, 2, 3], [4, 5, 6, 7, 8]]

DEBUG = {}


class Trk:
    __slots__ = ("w", "r", "dsem", "excl")

    def __init__(self, excl=False):
        self.w = None
        self.r = {}
        self.dsem = None
        self.excl = excl


class Sync:
    CE = ("pe", "act", "dve", "pool")

    def __init__(self, nc, n_dma_sems=80):
        self.nc = nc
        self.eng = {"pe": nc.tensor, "act": nc.scalar, "dve": nc.vector, "pool": nc.gpsimd, "sp": nc.sync}
        self.sem = {e: nc.alloc_semaphore(f"prog_{e}") for e in self.CE}
        self.cnt = {e: 0 for e in self.CE}
        self.known = {e: {} for e in self.eng}
        self.free_dsems = [nc.alloc_semaphore(f"dsem{i}") for i in range(n_dma_sems)]
        self.dcnt = {}
        self.dsem_by_num = {}
        self.n_wait = 0
        self.n_ins = 0

    def _wait(self, e, waits):
        eng = self.eng[e]
        kn = self.known[e]
        for num, (sem, val) in waits.items():
            if kn.get(num, 0) >= val:
                continue
            eng.wait_ge(sem, val)
            kn[num] = val
            self.n_wait += 1

    @staticmethod
    def _need(waits, tk):
        if tk is None:
            return
        sem, val = tk
        cur = waits.get(sem.num)
        if cur is None or cur[1] < val:
            waits[sem.num] = (sem, val)

    def _collect(self, e, reads, writes):
        waits = {}
        own = self.sem[e].num if e in self.sem else None
        for t in reads:
            self._need(waits, t.w)
            if t.excl:
                for num, tk in t.r.items():
                    if num != own:
                        self._need(waits, tk)
        for t in writes:
            self._need(waits, t.w)
            for tk in t.r.values():
                self._need(waits, tk)
        if e == "pe":
            waits.pop(self.sem["pe"].num, None)
        return waits

    @staticmethod
    def _mark(tk, reads, writes):
        sem, val = tk
        for t in reads:
            cur = t.r.get(sem.num)
            if cur is None or cur[1] < val:
                t.r[sem.num] = tk
        for t in writes:
            t.w = tk
            t.r = {}

    def op(self, e, fn, reads=(), writes=(), signal=True):
        self._wait(e, self._collect(e, reads, writes))
        ins = fn(self.eng[e])
        self.n_ins += 1
        if signal:
            self.cnt[e] += 1
            ins.then_inc(self.sem[e], 1)
            tk = (self.sem[e], self.cnt[e])
        else:
            tk = (self.sem[e], self.cnt[e] + 1)
        self._mark(tk, reads, writes)
        return ins

    def _dsem(self, t):
        if t.dsem is None:
            t.dsem = self.free_dsems.pop()
            self.dsem_by_num[t.dsem.num] = t.dsem
            self.dcnt.setdefault(t.dsem.num, 0)
        return t.dsem

    def release(self, trks):
        for t in trks:
            if t.dsem is not None:
                self.free_dsems.append(t.dsem)
                t.dsem = None

    def load(self, q, trks, out_ap, in_ap, **kw):
        self._wait(q, self._collect(q, (), trks))
        sem = self._dsem(trks[0])
        ins = self.eng[q].dma_start(out=out_ap, in_=in_ap, **kw)
        self.dcnt[sem.num] += 1
        ins.then_inc(sem, 16)
        self.n_ins += 1
        self._mark((sem, 16 * self.dcnt[sem.num]), (), trks)

    def store(self, q, trks, out_ap, in_ap, **kw):
        self._wait(q, self._collect(q, trks, ()))
        sem = self._dsem(trks[0])
        ins = self.eng[q].dma_start(out=out_ap, in_=in_ap, **kw)
        self.dcnt[sem.num] += 1
        ins.then_inc(sem, 16)
        self.n_ins += 1
        self._mark((sem, 16 * self.dcnt[sem.num]), trks, ())

    def barrier(self):
        waits = {}
        for e in self.CE:
            if self.cnt[e] > 0:
                waits[self.sem[e].num] = (self.sem[e], self.cnt[e])
        for num, c in self.dcnt.items():
            if c > 0:
                waits[num] = (self.dsem_by_num[num], 16 * c)
        for e in self.eng:
            self._wait(e, dict(waits))


def _rope_tables():
    t = np.arange(S)
    row = (t // GRID_W).astype(np.float32)
    col = (t % GRID_W).astype(np.float32)

    def tab(d):
        half = d // 2
        quarter = half // 2
        inv = (np.float32(10000.0) ** (-np.arange(quarter, dtype=np.float32) / np.float32(quarter))).astype(np.float32)
        C = np.zeros((P, S), np.float32)
        Sg = np.zeros((P, S), np.float32)
        PM = np.zeros((P, P), np.float32)
        for p in range(P):
            m = p % d
            pos = row if (m // half) == 0 else col
            j = m % half
            i = j % quarter
            second = j // quarter
            ang = (pos * inv[i]).astype(np.float32)
            C[p] = np.cos(ang)
            Sg[p] = np.sin(ang) * (-1.0 if second == 0 else 1.0)
            partner = p + quarter if second == 0 else p - quarter
            PM[partner, p] = 1.0
        return C, Sg, PM

    return tab(64), tab(128)


def _na_index_tables():
    idx = np.zeros((3, 8, P, 512), np.int64)
    val = np.zeros((3, 8, P, 512), bool)
    for v, j in enumerate((0, 3, 7)):
        for i in range(8):
            kc = 4 * j - 2 + i
            kk = np.arange(P)
            krow = 2 * kc + kk // 64
            kcol = kk % 64
            qq = np.arange(512)
            qrow = 8 * j + qq // 64
            qcol = qq % 64
            r0 = np.clip(qrow - 4, 0, 64 - 8)
            c0 = np.clip(qcol - 8, 0, 64 - 16)
            rok = (krow[:, None] >= r0[None, :]) & (krow[:, None] < r0[None, :] + 8)
            cok = (kcol[:, None] >= c0[None, :]) & (kcol[:, None] < c0[None, :] + 16)
            roff = krow[:, None] - qrow[None, :] + 7
            coff = np.clip(kcol[:, None] - qcol[None, :], -15, 15) + 15
            ok = rok & cok & (krow[:, None] >= 0) & (krow[:, None] < 64)
            idx[v, i] = np.where(ok, roff * 31 + coff, 0)
            val[v, i] = ok
    return idx, val


def build_program():
    nc = bass.Bass("TRN2", target_bir_lowering=False)
    dt = nc.dram_tensor

    def din(name, shape, dtype=F32):
        return dt(name, list(shape), dtype, kind="ExternalInput").ap()

    x_in = din("x", [S, D])
    ctx_in = din("ctx", [CTX, D])
    cc_in = din("cc", [2, D])
    w_mod = din("w_mod", [DEPTH, D, 6 * D])
    b_mod = din("b_mod", [DEPTH, 6 * D])
    gains = din("gains", [4, DEPTH, D])
    w_in = din("w_in", [DEPTH, D, INC])
    w_out = din("w_out", [DEPTH, D, D])
    lam_in = din("lam", [4, DEPTH, 64])
    hg_in = din("hg", [3, DEPTH, P])
    w_gate = din("w_gate", [DEPTH, D, DFF])
    w_up = din("w_up", [DEPTH, D, DFF])
    w_down = din("w_down", [DEPTH, DFF, D])
    rope_in = din("rope", [4, P, S])
    pm_in = din("pm", [2, P, P])
    nab_in = din("nab", [DEPTH, 4, 3, 8, P, 512])
    ident_in = din("ident", [P, P])
    out = dt("out", [S, D], F32, kind="ExternalOutput").ap()

    NSUB = T // 256
    hB = dt("hB", [NSUB, P, KC, 256], F32, kind="Internal").ap()
    yB = dt("yB", [NSUB, P, KC, 256], BF16, kind="Internal").ap()
    qkT = dt("qkT", [26 * P, T], BF16, kind="Internal").ap()
    vS = dt("vS", [10, P, NCH * P], BF16, kind="Internal").ap()
    hidT = dt("hidT", [DFF, T], BF16, kind="Internal").ap()

    dbg = {}
    for name, shape, dtype in DEBUG.get("outs", []):
        dbg[name] = dt(name, list(shape), dtype, kind="ExternalOutput").ap()

    sy = Sync(nc)
    stop_after = DEBUG.get("stop_after")

    with ExitStack() as top:
        uid = [0]

        def sb(stack, name, shape, dtype):
            uid[0] += 1
            return stack.enter_context(nc.sbuf_tensor(f"s{uid[0]}_{name}", list(shape), dtype))

        ps_h = [top.enter_context(nc.psum_tensor(f"ps{i}", [P, 512], F32)) for i in range(8)]
        PS = [Trk(excl=True) for _ in range(8)]

        ident = sb(top, "ident", [P, P], F32)
        ones_bf = sb(top, "ones_bf", [P, P], BF16)
        ones_f = sb(top, "ones_f", [P, P], F32)
        pm = sb(top, "pm", [P, 2, P], F32)
        cst = Trk()
        sy.load("sp", [cst], ident[:], ident_in[:, :])
        sy.load("sp", [cst], pm[:], pm_in.rearrange("a p q -> p a q"))
        sy.op("dve", lambda e: e.memset(ones_bf[:], 1.0), writes=[cst])
        sy.op("dve", lambda e: e.memset(ones_f[:], 1.0), writes=[cst])
        pmb = sb(top, "pmb", [P, 2, P], BF16)
        sy.op("dve", lambda e: e.tensor_copy(pmb[:], pm[:]), reads=[cst], writes=[cst])
        modb = sb(top, "modb", [P, 96], F32)
        modc = sb(top, "modc", [P, 96], F32)
        der = {n: [sb(top, f"{n}{i}", [P, KC], F32) for i in range(2)]
               for n in ("a1", "gt1g", "a2", "gt2g")}
        gT = sb(top, "gT", [P, 4, KC], F32)
        hgT = sb(top, "hgT", [P, 3], F32)
        gdT = sb(top, "gdT", [P, 1], F32)
        nlam = sb(top, "nlam", [P, 1], F32)
        small = Trk()

        sy.barrier()

        def stage_transpose_in():
            with ExitStack() as st:
                xin = [sb(st, f"xin{i}", [P, 4, D], F32) for i in range(2)]
                xin_t = [Trk() for _ in range(2)]
                hb = [sb(st, f"hbo{i}", [P, KC, 512], F32) for i in range(2)]
                hb_t = [[Trk() for _ in range(KC)] for _ in range(2)]
                pi = 0
                for bi, (t0, n) in enumerate(BLOCKS):
                    nt = n // P
                    X, Xt = xin[bi % 2], xin_t[bi % 2]
                    H, Ht = hb[bi % 2], hb_t[bi % 2]
                    if t0 < S:
                        src = x_in[t0:t0 + n, :]
                    else:
                        src = ctx_in[:, :]
                    sy.load("sp", [Xt], X[:, 0:nt, :], src.rearrange("(j p) f -> p j f", p=P))
                    for k in range(KC):
                        ps, pst = ps_h[pi % 8], PS[pi % 8]
                        pi += 1
                        for j in range(nt):
                            sy.op("pe", lambda e, j=j, k=k, ps=ps, X=X: e.transpose(
                                ps[:, j * P:(j + 1) * P], X[:, j, k * P:(k + 1) * P], ident[:]),
                                reads=[Xt, cst], writes=[pst], signal=(j == nt - 1))
                        eng = "act" if k % 2 == 0 else "dve"
                        if eng == "act":
                            sy.op("act", lambda e, ps=ps, H=H, k=k, n=n: e.copy(H[:, k, 0:n], ps[:, 0:n]),
                                  reads=[pst], writes=[Ht[k]])
                        else:
                            sy.op("dve", lambda e, ps=ps, H=H, k=k, n=n: e.tensor_copy(H[:, k, 0:n], ps[:, 0:n]),
                                  reads=[pst], writes=[Ht[k]])
                    for o in range(0, n, 256):
                        sy.store("sp", Ht, hB[(t0 + o) // 256], H[:, :, o:o + 256])
                sy.barrier()
                sy.release(xin_t + [t for l in hb_t for t in l])

        def stage_mod(l):
            with ExitStack() as st:
                cT = sb(st, "cT", [P, 2, KC], F32)
                scf = sb(st, "scf", [P, KC, 2], F32)
                scb = sb(st, "scb", [P, KC, 2], BF16)
                bmT = sb(st, "bmT", [P, 96], F32)
                lamc = sb(st, "lamc", [64, 4], F32)
                prod = sb(st, "prod", [64, 2], F32)
                ex = sb(st, "ex", [P, 2], F32)
                tmp = sb(st, "tmpm", [P, KC], F32)
                wm = [sb(st, f"wm{i}", [P, KC, 512], BF16) for i in range(2)]
                wm_t = [Trk() for _ in range(2)]
                tk = Trk()
                with nc.allow_non_contiguous_dma(reason="tiny transposed vector loads"):
                    sy.load("sp", [tk], cT[:], cc_in.rearrange("a (k p) -> p a k", p=P))
                    sy.load("sp", [tk], bmT[:], b_mod[l].rearrange("(c p) -> p c", p=P))
                    for a_ in range(4):
                        sy.load("sp", [small], gT[:, a_, :], gains[a_, l, :].rearrange("(k p) -> p k", p=P))
                    sy.load("sp", [small], hgT[:], hg_in[:, l, :].rearrange("a p -> p a"))
                    sy.load("sp", [tk], lamc[:], lam_in[:, l, :].rearrange("a p -> p a"))
                for a in range(2):
                    sy.op("act", lambda e, a=a: e.activation(scf[:, :, a], cT[:, a, :], AF.Silu),
                          reads=[tk], writes=[tk])
                sy.op("dve", lambda e: e.tensor_copy(scb[:], scf[:]), reads=[tk], writes=[tk])
                lam_init = 0.8 - 0.6 * math.exp(-0.3 * l)
                sy.op("dve", lambda e: e.tensor_tensor(prod[:, 0:1], lamc[:, 0:1], lamc[:, 1:2], ALU.mult),
                      reads=[tk], writes=[tk])
                sy.op("dve", lambda e: e.tensor_tensor(prod[:, 1:2], lamc[:, 2:3], lamc[:, 3:4], ALU.mult),
                      reads=[tk], writes=[tk])
                sy.op("pe", lambda e: e.matmul(ps_h[7][:, 0:2], lhsT=ones_f[0:64, :], rhs=prod[:, :],
                                               start=True, stop=True), reads=[tk, cst], writes=[PS[7]])
                sy.op("act", lambda e: e.activation(ex[:], ps_h[7][:, 0:2], AF.Exp), reads=[PS[7]], writes=[tk])
                sy.op("dve", lambda e: e.tensor_tensor(nlam[:], ex[:, 1:2], ex[:, 0:1], ALU.subtract),
                      reads=[tk], writes=[small])
                sy.op("dve", lambda e: e.tensor_scalar(nlam[:], nlam[:], -lam_init, None, ALU.add),
                      reads=[small], writes=[small])
                sy.op("dve", lambda e: e.tensor_scalar(gdT[:], hgT[:, 0:1], 1.0 - lam_init, None, ALU.mult),
                      reads=[small], writes=[small])
                NG = 24
                sy.load("pool", [wm_t[0]], wm[0][:], w_mod[l, :, 0:512].rearrange("(k p) n -> p k n", p=P))
                for g in range(NG):
                    if g + 1 < NG:
                        sy.load("pool", [wm_t[(g + 1) % 2]], wm[(g + 1) % 2][:],
                                w_mod[l, :, (g + 1) * 512:(g + 2) * 512].rearrange("(k p) n -> p k n", p=P))
                    W, Wt = wm[g % 2], wm_t[g % 2]
                    for j in range(4):
                        c = 4 * g + j
                        for k in range(KC):
                            sy.op("pe", lambda e, W=W, j=j, k=k, c=c: e.matmul(
                                ps_h[6][:, 2 * c:2 * c + 2], lhsT=W[:, k, j * P:(j + 1) * P], rhs=scb[:, k, :],
                                start=(k == 0), stop=(k == KC - 1)),
                                reads=[Wt, tk], writes=[PS[6]], signal=(k == KC - 1))
                psv = ps_h[6][:, 0:192].rearrange("p (c a) -> p c a", a=2)
                sy.op("dve", lambda e: e.tensor_tensor(modb[:], psv[:, :, 0], bmT[:], ALU.add),
                      reads=[PS[6], tk], writes=[small])
                sy.op("dve", lambda e: e.tensor_tensor(modc[:], psv[:, :, 1], bmT[:], ALU.add),
                      reads=[PS[6], tk], writes=[small])
                for i, mod in enumerate((modb, modc)):
                    for nm, sc_off, gidx in (("a1", 16, 0), ("a2", 64, 2)):
                        sy.op("dve", lambda e, mod=mod, sc_off=sc_off: e.tensor_scalar(
                            tmp[:], mod[:, sc_off:sc_off + 16], 1.0, None, ALU.add), reads=[small, tk], writes=[tk])
                        sy.op("dve", lambda e, nm=nm, i=i, gidx=gidx: e.tensor_tensor(
                            der[nm][i][:], tmp[:], gT[:, gidx, :], ALU.mult), reads=[tk, small], writes=[small])
                    for nm, gt_off, gidx in (("gt1g", 32, 1), ("gt2g", 80, 3)):
                        sy.op("dve", lambda e, nm=nm, i=i, gidx=gidx, mod=mod, gt_off=gt_off: e.tensor_tensor(
                            der[nm][i][:], mod[:, gt_off:gt_off + 16], gT[:, gidx, :], ALU.mult),
                            reads=[small], writes=[small])
                sy.barrier()
                sy.release([tk] + wm_t)

        def ssq_rstd(st_tiles, src_fn, srck_trks, n, d_count, ps_idx):
            sq, sq_t = st_tiles["sq"], st_tiles["sq_t"]
            ps, pst = ps_h[ps_idx], PS[ps_idx]
            for k in range(KC):
                s_, s_t = sq[k % len(sq)], sq_t[k % len(sq)]
                if k % 2 == 0:
                    sy.op("act", lambda e, s_=s_, k=k: e.activation(s_[:, 0:n], src_fn(k), AF.Square),
                          reads=[srck_trks[k]], writes=[s_t])
                else:
                    sy.op("dve", lambda e, s_=s_, k=k: e.tensor_tensor(s_[:, 0:n], src_fn(k), src_fn(k), ALU.mult),
                          reads=[srck_trks[k]], writes=[s_t])
                sy.op("pe", lambda e, s_=s_, k=k: e.matmul(ps[:, 0:n], lhsT=ones_bf[:], rhs=s_[:, 0:n],
                                                          start=(k == 0), stop=(k == KC - 1)),
                      reads=[s_t, cst], writes=[pst], signal=True)
            rt, rt_t = st_tiles["rt"], st_tiles["rt_t"]
            sy.op("act", lambda e: e.activation(rt[:, 0:n], ps[:, 0:n], AF.Sqrt, bias=st_tiles["eps"][:, 0:1],
                                                scale=1.0 / d_count),
                  reads=[pst, st_tiles["eps_t"]], writes=[rt_t])
            sy.op("dve", lambda e: e.reciprocal(rt[:, 0:n], rt[:, 0:n]), reads=[rt_t], writes=[rt_t])
            return rt, rt_t

        def stage_norm(l, blocks, XT, XT_t, resid, modn):
            with ExitStack() as st:
                NB = 2
                Hs = [sb(st, f"nH{i}", [P, KC, 256], F32) for i in range(NB)]
                H_t = [[Trk() for _ in range(KC)] for _ in range(NB)]
                Ys = [sb(st, f"nY{i}", [P, KC, 256], BF16) for i in range(NB)] if resid else None
                Y_t = [[Trk() for _ in range(KC)] for _ in range(NB)]
                tiles = {
                    "sq": [sb(st, f"nsq{i}", [P, 512], BF16) for i in range(4)],
                    "sq_t": [Trk() for _ in range(4)],
                    "rt": None, "rt_t": None,
                    "eps": sb(st, "neps", [P, 1], F32),
                }
                eps_t = Trk()
                sy.op("dve", lambda e: e.memset(tiles["eps"][:], EPS), writes=[eps_t])
                tiles["eps_t"] = eps_t
                rts = [sb(st, f"nrt{i}", [P, 512], F32) for i in range(4)]
                rt_ts = [Trk() for _ in range(4)]
                tmps = [sb(st, f"ntmp{i}", [P, 512], F32) for i in range(4)]
                tmp_ts = [Trk() for _ in range(4)]
                ri = 0
                ti = 0
                x0 = blocks[0][0] if XT is not None else 0
                subs = []
                for (bt0, bn) in blocks:
                    for o in range(0, bn, 256):
                        subs.append((bt0 + o, 256, bt0))
                for bi, (t0, n, kt0) in enumerate(subs):
                    is_ctx = 1 if t0 >= S else 0
                    H, Ht = Hs[bi % NB], H_t[bi % NB]
                    sy.load("sp", Ht, H[:, :, 0:n], hB[t0 // 256])
                    if resid:
                        Y, Yt = Ys[bi % NB], Y_t[bi % NB]
                        sy.load("sp", Yt, Y[:, :, 0:n], yB[t0 // 256])
                        tiles["rt"], tiles["rt_t"] = rts[ri % 4], rt_ts[ri % 4]
                        ri += 1
                        r1, r1t = ssq_rstd(tiles, lambda k, Y=Y: Y[:, k, 0:n], Yt, n, D, 6)
                        gtg = der[resid][is_ctx]
                        for k in range(KC):
                            tm, tmt = tmps[ti % 4], tmp_ts[ti % 4]
                            ti += 1
                            sy.op("dve", lambda e, tm=tm, Y=Y, k=k, r1=r1: e.scalar_tensor_tensor(
                                tm[:, 0:n], Y[:, k, 0:n], gtg[:, k:k + 1], r1[:, 0:n], ALU.mult, ALU.mult),
                                reads=[Yt[k], r1t, small], writes=[tmt])
                            sy.op("pool", lambda e, tm=tm, H=H, k=k: e.tensor_tensor(
                                H[:, k, 0:n], tm[:, 0:n], H[:, k, 0:n], ALU.add),
                                reads=[tmt, Ht[k]], writes=[Ht[k]])
                    if modn:
                        tiles["rt"], tiles["rt_t"] = rts[ri % 4], rt_ts[ri % 4]
                        ri += 1
                        r2, r2t = ssq_rstd(tiles, lambda k, H=H: H[:, k, 0:n], Ht, n, D, 7)
                        a = der["a1" if modn == 1 else "a2"][is_ctx]
                        mod = modc if is_ctx else modb
                        sh_off = 0 if modn == 1 else 48
                        for k in range(KC):
                            tm, tmt = tmps[ti % 4], tmp_ts[ti % 4]
                            ti += 1
                            sy.op("dve", lambda e, tm=tm, H=H, k=k, r2=r2: e.tensor_tensor(
                                tm[:, 0:n], H[:, k, 0:n], r2[:, 0:n], ALU.mult),
                                reads=[Ht[k], r2t], writes=[tmt])
                            sy.op("act", lambda e, tm=tm, k=k, a=a, mod=mod: e.activation(
                                XT[:, k, t0 - x0:t0 - x0 + n], tm[:, 0:n], AF.Identity,
                                bias=mod[:, sh_off + k:sh_off + k + 1], scale=a[:, k:k + 1]),
                                reads=[tmt, small], writes=[XT_t[(k, kt0)]])
                    if resid:
                        sy.store("sp", Ht, hB[t0 // 256], H[:, :, 0:n])
                sy.barrier()
                sy.release([t for l_ in H_t for t in l_] + [t for l_ in Y_t for t in l_])

        def run_pending(pending):
            alive = []
            for g in pending:
                try:
                    next(g)
                    alive.append(g)
                except StopIteration:
                    pass
            pending[:] = alive

        def drain(pending):
            while pending:
                run_pending(pending)

        def inproj_plan(c):
            if c < 4:
                return ("rope64", c)
            if c < 8:
                return ("rope64", c)
            if c < 12:
                return ("v", c - 8)
            if c < 20:
                return ("nrope_q", 8 + (c - 12))
            if c < 22:
                return ("nrope_k", 16 + (c - 20))
            if c < 24:
                return ("v", 4 + (c - 22))
            if c < 28:
                return ("plain", 18 + (c - 24))
            if c < 32:
                return ("plain", 22 + (c - 28))
            return ("v", 6 + (c - 32))

        def stage_inproj(l, blocks, XT, XT_t):
            x0 = blocks[0][0]
            with ExitStack() as st:
                wt = [sb(st, f"wi{i}", [P, KC, 512], BF16) for i in range(2)]
                wt_t = [Trk() for _ in range(2)]
                rp = [sb(st, f"rp{i}", [P, 2, 512], F32) for i in range(2)]
                rp_t = [Trk() for _ in range(2)]
                NR = 6
                xf = [sb(st, f"xf{i}", [P, 512], F32) for i in range(NR)]
                xf_t = [Trk() for _ in range(NR)]
                xb = [sb(st, f"xb{i}", [P, 512], BF16) for i in range(NR)]
                xb_t = [Trk() for _ in range(NR)]
                sq = [sb(st, f"isq{i}", [P, 512], BF16) for i in range(2)]
                sq_t = [Trk() for _ in range(2)]
                rt = [sb(st, f"irt{i}", [P, 512], F32) for i in range(2)]
                rt_t = [Trk() for _ in range(2)]
                t1 = [sb(st, f"it1{i}", [P, 512], F32) for i in range(NR)]
                t1_t = [Trk() for _ in range(NR)]
                t2 = [sb(st, f"it2{i}", [P, 512], F32) for i in range(NR)]
                t2_t = [Trk() for _ in range(NR)]
                ob = [sb(st, f"iob{i}", [P, 512], BF16) for i in range(NR)]
                ob_t = [Trk() for _ in range(NR)]
                epsT = sb(st, "ieps", [P, 1], F32)
                eps_t = Trk()
                sy.op("dve", lambda e: e.memset(epsT[:], EPS), writes=[eps_t])
                cnt = {"main": 0, "aux": 0, "buf": 0, "rp": 0}
                pending = []

                def w_src(g):
                    return w_in[l, :, g * 512:(g + 1) * 512].rearrange("(k p) n -> p k n", p=P)

                def epi_fm(kind, dest, ps, pst, t0, n, rpi):
                    i = cnt["buf"] % NR
                    cnt["buf"] += 1
                    O, Ot = ob[i], ob_t[i]
                    latent = t0 < S
                    dst = qkT[dest * P:(dest + 1) * P, t0:t0 + n]
                    if DEBUG.get("inproj_mode") == "plain_all":
                        kind = "plain"
                    if DEBUG.get("inproj_mode") == "no_rope" and kind == "rope64":
                        kind = "plain"
                    if DEBUG.get("inproj_mode") == "no_nrope" and kind.startswith("nrope"):
                        kind = "plain"
                    if kind == "plain" or (kind == "rope64" and not latent):
                        sy.op("act", lambda e: e.copy(O[:, 0:n], ps[:, 0:n]), reads=[pst], writes=[Ot])
                        sy.store("sp", [Ot], dst, O[:, 0:n])
                        return
                    X, Xt_ = xf[i], xf_t[i]
                    XB, XBt = xb[i], xb_t[i]
                    if kind == "rope64":
                        sy.op("act", lambda e: e.copy(XB[:, 0:n], ps[:, 0:n]), reads=[pst], writes=[XBt])
                        tsel, pmi = 0, 0
                        xsrc, xsrc_t = ps, pst
                    else:
                        s_, s_t = sq[cnt["aux"] % 2], sq_t[cnt["aux"] % 2]
                        r_, r_t = rt[cnt["aux"] % 2], rt_t[cnt["aux"] % 2]
                        pa, pat = ps_h[6 + cnt["aux"] % 2], PS[6 + cnt["aux"] % 2]
                        cnt["aux"] += 1
                        sy.op("act", lambda e: e.activation(s_[:, 0:n], ps[:, 0:n], AF.Square),
                              reads=[pst], writes=[s_t])
                        yield
                        sy.op("pe", lambda e: e.matmul(pa[:, 0:n], lhsT=ones_bf[:], rhs=s_[:, 0:n],
                                                       start=True, stop=True), reads=[s_t, cst], writes=[pat])
                        sy.op("act", lambda e: e.activation(r_[:, 0:n], pa[:, 0:n], AF.Sqrt, bias=epsT[:, 0:1],
                                                            scale=1.0 / P), reads=[pat, eps_t], writes=[r_t])
                        sy.op("dve", lambda e: e.reciprocal(r_[:, 0:n], r_[:, 0:n]), reads=[r_t], writes=[r_t])
                        gcol = 1 if kind == "nrope_q" else 2
                        sy.op("act", lambda e: e.activation(X[:, 0:n], ps[:, 0:n], AF.Identity,
                                                            scale=hgT[:, gcol:gcol + 1]),
                              reads=[pst, small], writes=[Xt_])
                        sy.op("dve", lambda e: e.tensor_tensor(X[:, 0:n], X[:, 0:n], r_[:, 0:n], ALU.mult),
                              reads=[Xt_, r_t], writes=[Xt_])
                        tsel, pmi = 1, 1
                        xsrc, xsrc_t = X, Xt_
                        if not latent:
                            sy.op("act", lambda e: e.copy(O[:, 0:n], X[:, 0:n]), reads=[Xt_], writes=[Ot])
                            sy.store("sp", [Ot], dst, O[:, 0:n])
                            return
                        sy.op("act", lambda e: e.copy(XB[:, 0:n], X[:, 0:n]), reads=[Xt_], writes=[XBt])
                    yield
                    pb_i = 4 + cnt["main"] % 2
                    pb, pbt = ps_h[pb_i], PS[pb_i]
                    sy.op("pe", lambda e: e.matmul(pb[:, 0:n], lhsT=pmb[:, pmi, :], rhs=XB[:, 0:n],
                                                   start=True, stop=True), reads=[XBt, cst], writes=[pbt])
                    R, Rt = rp[rpi], rp_t[rpi]
                    A, At = t1[i], t1_t[i]
                    B, Bt = t2[i], t2_t[i]
                    sy.op("dve", lambda e: e.tensor_tensor(A[:, 0:n], xsrc[:, 0:n], R[:, 0, 0:n], ALU.mult),
                          reads=[xsrc_t, Rt], writes=[At])
                    sy.op("dve", lambda e: e.tensor_tensor(B[:, 0:n], pb[:, 0:n], R[:, 1, 0:n], ALU.mult),
                          reads=[pbt, Rt], writes=[Bt])
                    sy.op("dve", lambda e: e.tensor_tensor(O[:, 0:n], A[:, 0:n], B[:, 0:n], ALU.add),
                          reads=[At, Bt], writes=[Ot])
                    sy.store("sp", [Ot], dst, O[:, 0:n])

                NG = 9
                sy.load("pool", [wt_t[0]], wt[0][:], w_src(0))
                for g in range(NG):
                    if g + 1 < NG:
                        sy.load("pool", [wt_t[(g + 1) % 2]], wt[(g + 1) % 2][:], w_src(g + 1))
                    W, Wt = wt[g % 2], wt_t[g % 2]
                    plans = [inproj_plan(4 * g + j) for j in range(4)]
                    kinds = set(p_[0] for p_ in plans)
                    need_rope = None
                    if "rope64" in kinds:
                        need_rope = 0
                    elif "nrope_q" in kinds or "nrope_k" in kinds:
                        need_rope = 1
                    for (t0, n) in blocks:
                        rpi = None
                        if need_rope is not None and t0 < S:
                            rpi = cnt["rp"] % 2
                            cnt["rp"] += 1
                            sy.load("sp", [rp_t[rpi]], rp[rpi][:, :, 0:n],
                                    rope_in[2 * need_rope:2 * need_rope + 2, :, t0:t0 + n].rearrange("a p t -> p a t"))
                        for j, (kind, dest) in enumerate(plans):
                            if kind == "v":
                                continue
                            pi = cnt["main"] % 4
                            cnt["main"] += 1
                            ps, pst = ps_h[pi], PS[pi]
                            for k in range(KC):
                                sy.op("pe", lambda e, ps=ps, W=W, j=j, k=k: e.matmul(
                                    ps[:, 0:n], lhsT=W[:, k, j * P:(j + 1) * P],
                                    rhs=XT[:, k, t0 - x0:t0 - x0 + n], start=(k == 0), stop=(k == KC - 1)),
                                    reads=[Wt, XT_t[(k, t0)]], writes=[pst], signal=(k == KC - 1))
                            run_pending(pending)
                            gen = epi_fm(kind, dest, ps, pst, t0, n, rpi)
                            pending.append(gen)
                        vj = [j for j, p_ in enumerate(plans) if p_[0] == "v"]
                        if vj:
                            c0, c1 = vj[0] * P, (vj[-1] + 1) * P
                            ncol = c1 - c0
                            for tt in range(n // P):
                                pi = cnt["main"] % 4
                                cnt["main"] += 1
                                ps, pst = ps_h[pi], PS[pi]
                                ta = t0 - x0 + tt * P
                                for k in range(KC):
                                    sy.op("pe", lambda e, ps=ps, W=W, k=k, ta=ta: e.matmul(
                                        ps[:, 0:ncol], lhsT=XT[:, k, ta:ta + P], rhs=W[:, k, c0:c1],
                                        start=(k == 0), stop=(k == KC - 1)),
                                        reads=[Wt, XT_t[(k, t0)]], writes=[pst], signal=(k == KC - 1))
                                run_pending(pending)
                                i = cnt["buf"] % NR
                                cnt["buf"] += 1
                                O, Ot = ob[i], ob_t[i]
                                sy.op("act", lambda e, O=O, ps=ps: e.copy(O[:, 0:ncol], ps[:, 0:ncol]),
                                      reads=[pst], writes=[Ot])
                                chunk = (t0 + tt * P) // P
                                heads = [plans[j][1] for j in vj]
                                h0 = heads[0]
                                sy.store("sp", [Ot],
                                         vS[h0:h0 + len(heads), :, chunk * P:(chunk + 1) * P].rearrange("h p d -> p h d"),
                                         O[:, 0:ncol].rearrange("p (h d) -> p h d", d=P))
                drain(pending)
                sy.barrier()
                sy.release(wt_t + rp_t + ob_t)

        def stage_attention(l, blocks, XT, XT_t, with_ctx):
            x0 = blocks[0][0]
            with ExitStack() as st:
                KT = [sb(st, f"KT{i}", [P, T], BF16) for i in range(2)]
                KT_t = [Trk() for _ in range(2)]
                VT = [sb(st, f"VT{i}", [P, NCH, P], BF16) for i in range(2)]
                VT_t = [Trk() for _ in range(2)]
                QT = [sb(st, f"QT{i}", [P, 2304], BF16) for i in range(2)]
                QT_t = [Trk() for _ in range(2)]
                QZ = [[sb(st, f"QZ{i}_{m}", [P, 2304], BF16) for m in range(2)] for i in range(2)]
                QZ_t = [[Trk() for m in range(2)] for i in range(2)]
                for i in range(2):
                    sy.op("dve", lambda e, i=i: e.memset(QZ[i][0][64:128, :], 0.0), writes=[QZ_t[i][0]])
                    sy.op("dve", lambda e, i=i: e.memset(QZ[i][1][0:64, :], 0.0), writes=[QZ_t[i][1]])
                NE = 4
                E = [sb(st, f"E{i}", [P, 512], BF16) for i in range(NE)]
                E_t = [Trk() for _ in range(NE)]
                SBm = [sb(st, f"SBm{i}", [P, 512], F32) for i in range(2)]
                SB_t = [Trk() for _ in range(2)]
                NB = [sb(st, f"NB{i}", [P, 512], F32) for i in range(3)]
                NB_t = [Trk() for _ in range(3)]
                RD = [sb(st, f"RD{i}", [P, 512], F32) for i in range(2)]
                RD_t = [Trk() for _ in range(2)]
                O1 = sb(st, "O1", [P, 512], F32)
                O1_t = Trk()
                O2 = sb(st, "O2", [P, 512], F32)
                O2_t = Trk()
                OD = sb(st, "OD", [P, 512], F32)
                OD_t = Trk()
                SQ = sb(st, "aSQ", [P, 512], BF16)
                SQ_t = Trk()
                RT = sb(st, "aRT", [P, 512], F32)
                RT_t = Trk()
                epsT = sb(st, "aeps", [P, 1], F32)
                eps_t = Trk()
                sy.op("dve", lambda e: e.memset(epsT[:], EPS), writes=[eps_t])
                cnt = {"s": 0, "e": 0, "acc": 0, "kv": 0, "q": 0, "nb": 0, "sb": 0, "rd": 0}
                ntok = sum(n for _, n in blocks)
                qblocks = [(t0, n) for (t0, n) in blocks if (t0 < S or with_ctx)]

                def load_kv(kchunk, vhead):
                    i = cnt["kv"] % 2
                    cnt["kv"] += 1
                    sy.load("sp", [KT_t[i]], KT[i][:], qkT[kchunk * P:(kchunk + 1) * P, :])
                    sy.load("sp", [VT_t[i]], VT[i][:], vS[vhead].rearrange("p (c d) -> p c d", d=P))
                    return i

                def load_q(qchunk):
                    i = cnt["q"] % 2
                    cnt["q"] += 1
                    sy.load("sp", [QT_t[i]], QT[i][:, 0:ntok], qkT[qchunk * P:(qchunk + 1) * P, x0:x0 + ntok])
                    return i

                def load_qz(qchunk):
                    i = cnt["q"] % 2
                    cnt["q"] += 1
                    sy.load("sp", [QZ_t[i][0]], QZ[i][0][0:64, 0:ntok], qkT[qchunk * P:qchunk * P + 64, x0:x0 + ntok])
                    sy.load("sp", [QZ_t[i][1]], QZ[i][1][64:128, 0:ntok],
                            qkT[qchunk * P + 64:(qchunk + 1) * P, x0:x0 + ntok])
                    return i

                def core(kvi, Q_, Q_t, t0, n, scale, na=None):
                    a = cnt["acc"] % 2
                    cnt["acc"] += 1
                    po, pot = ps_h[4 + a], PS[4 + a]
                    pd, pdt = ps_h[6 + a], PS[6 + a]
                    K_, K_t, V_, V_t = KT[kvi], KT_t[kvi], VT[kvi], VT_t[kvi]
                    if t0 >= S:
                        chunks = [32, 33]
                    elif na is not None:
                        j = t0 // 512
                        chunks = [c for c in range(4 * j - 2, 4 * j + 6) if 0 <= c < 32] + [32, 33]
                    else:
                        chunks = list(range(NCH))
                    q0 = t0 - x0

                    def emit_s(c):
                        si = cnt["s"] % 4
                        cnt["s"] += 1
                        ps, pst = ps_h[si], PS[si]
                        sy.op("pe", lambda e: e.matmul(ps[:, 0:n], lhsT=K_[:, c * P:(c + 1) * P],
                                                       rhs=Q_[:, q0:q0 + n], start=True, stop=True),
                              reads=[K_t, Q_t], writes=[pst])
                        return ps, pst

                    LA = 2
                    sq_ = [emit_s(chunks[i]) for i in range(min(LA, len(chunks)))]
                    for ci, c in enumerate(chunks):
                        ps, pst = sq_.pop(0)
                        if ci + LA < len(chunks):
                            sq_.append(emit_s(chunks[ci + LA]))
                        ei = cnt["e"] % NE
                        cnt["e"] += 1
                        E_, E_t_ = E[ei], E_t[ei]
                        if na is not None and c < 32:
                            h, var = na
                            j = t0 // 512
                            i_off = c - (4 * j - 2)
                            bi = cnt["nb"] % 3
                            cnt["nb"] += 1
                            sy.load("sp", [NB_t[bi]], NB[bi][:], nab_in[l, h, var, i_off])
                            sb_i = cnt["sb"] % 2
                            cnt["sb"] += 1
                            sy.op("dve", lambda e: e.scalar_tensor_tensor(
                                SBm[sb_i][:, 0:n], ps[:, 0:n], scale, NB[bi][:, 0:n], ALU.mult, ALU.add),
                                reads=[pst, NB_t[bi]], writes=[SB_t[sb_i]])
                            sy.op("act", lambda e: e.activation(E_[:, 0:n], SBm[sb_i][:, 0:n], AF.Exp),
                                  reads=[SB_t[sb_i]], writes=[E_t_])
                        else:
                            sy.op("act", lambda e: e.activation(E_[:, 0:n], ps[:, 0:n], AF.Exp, scale=scale),
                                  reads=[pst], writes=[E_t_])
                        first, last = ci == 0, ci == len(chunks) - 1
                        sy.op("pe", lambda e: e.matmul(po[:, 0:n], lhsT=V_[:, c, :], rhs=E_[:, 0:n],
                                                       start=first, stop=last),
                              reads=[V_t, E_t_], writes=[pot], signal=False)
                        sy.op("pe", lambda e: e.matmul(pd[:, 0:n], lhsT=ones_bf[:], rhs=E_[:, 0:n],
                                                       start=first, stop=last),
                              reads=[E_t_, cst], writes=[pdt], signal=True)
                    return po, pot, pd, pdt

                def recip(pd, pdt, n):
                    i = cnt["rd"] % 2
                    cnt["rd"] += 1
                    sy.op("dve", lambda e: e.reciprocal(RD[i][:, 0:n], pd[:, 0:n]), reads=[pdt], writes=[RD_t[i]])
                    return RD[i], RD_t[i]

                def finish_plain(xchunk, po, pot, pd, pdt, t0, n):
                    R, Rt = recip(pd, pdt, n)
                    sy.op("dve", lambda e: e.tensor_tensor(XT[:, xchunk, t0 - x0:t0 - x0 + n], po[:, 0:n],
                                                           R[:, 0:n], ALU.mult),
                          reads=[pot, Rt], writes=[XT_t[(xchunk, t0)]])

                sc_a = 64 ** -0.5
                for h in range(4):
                    kvi = load_kv(4 + h, h)
                    qi = load_qz(h)
                    for (t0, n) in qblocks:
                        po, pot, pd, pdt = core(kvi, QZ[qi][0], QZ_t[qi][0], t0, n, sc_a)
                        R, Rt = recip(pd, pdt, n)
                        sy.op("dve", lambda e: e.tensor_tensor(O1[:, 0:n], po[:, 0:n], R[:, 0:n], ALU.mult),
                              reads=[pot, Rt], writes=[O1_t])
                        po, pot, pd, pdt = core(kvi, QZ[qi][1], QZ_t[qi][1], t0, n, sc_a)
                        R, Rt = recip(pd, pdt, n)
                        sy.op("dve", lambda e: e.tensor_tensor(O2[:, 0:n], po[:, 0:n], R[:, 0:n], ALU.mult),
                              reads=[pot, Rt], writes=[O2_t])
                        sy.op("dve", lambda e: e.scalar_tensor_tensor(OD[:, 0:n], O2[:, 0:n], nlam[:, 0:1],
                                                                      O1[:, 0:n], ALU.mult, ALU.add),
                              reads=[O1_t, O2_t, small], writes=[OD_t])
                        sy.op("act", lambda e: e.activation(SQ[:, 0:n], OD[:, 0:n], AF.Square),
                              reads=[OD_t], writes=[SQ_t])
                        sbi = cnt["s"] % 4
                        cnt["s"] += 1
                        sy.op("pe", lambda e: e.matmul(ps_h[sbi][:, 0:n], lhsT=ones_bf[:], rhs=SQ[:, 0:n],
                                                       start=True, stop=True), reads=[SQ_t, cst], writes=[PS[sbi]])
                        sy.op("act", lambda e: e.activation(RT[:, 0:n], ps_h[sbi][:, 0:n], AF.Sqrt,
                                                            bias=epsT[:, 0:1], scale=1.0 / P),
                              reads=[PS[sbi], eps_t], writes=[RT_t])
                        sy.op("dve", lambda e: e.reciprocal(RT[:, 0:n], RT[:, 0:n]), reads=[RT_t], writes=[RT_t])
                        sy.op("dve", lambda e: e.scalar_tensor_tensor(
                            XT[:, h, t0 - x0:t0 - x0 + n], OD[:, 0:n], gdT[:, 0:1], RT[:, 0:n], ALU.mult, ALU.mult),
                            reads=[OD_t, RT_t, small], writes=[XT_t[(h, t0)]])
                sc_b = 128 ** -0.5
                for kvh in range(2):
                    kvi = load_kv(16 + kvh, 4 + kvh)
                    for gq in range(4):
                        hq = kvh * 4 + gq
                        qi = load_q(8 + hq)
                        for (t0, n) in qblocks:
                            po, pot, pd, pdt = core(kvi, QT[qi], QT_t[qi], t0, n, sc_b)
                            finish_plain(4 + hq, po, pot, pd, pdt, t0, n)
                for h in range(4):
                    kvi = load_kv(22 + h, 6 + h)
                    qi = load_q(18 + h)
                    for (t0, n) in qblocks:
                        if t0 < S:
                            j = t0 // 512
                            var = 0 if j == 0 else (2 if j == 7 else 1)
                            po, pot, pd, pdt = core(kvi, QT[qi], QT_t[qi], t0, n, sc_b, na=(h, var))
                        else:
                            po, pot, pd, pdt = core(kvi, QT[qi], QT_t[qi], t0, n, sc_b)
                        finish_plain(12 + h, po, pot, pd, pdt, t0, n)
                sy.barrier()
                sy.release(KT_t + VT_t + QT_t + NB_t + [t for l_ in QZ_t for t in l_])

        def stage_gemm_fm(wsrc_fn, ngroups, gcols, kchunks, blocks, XT, XT_t, x0, dstT, wname, epi="copy",
                          wsrc2_fn=None):
            nj = gcols // P
            with ExitStack() as st:
                nwb = 2
                wt = [sb(st, f"{wname}{i}", [P, kchunks, gcols], BF16) for i in range(nwb)]
                wt_t = [Trk() for _ in range(nwb)]
                if wsrc2_fn is not None:
                    wu = [sb(st, f"{wname}u{i}", [P, kchunks, gcols], BF16) for i in range(nwb)]
                    wu_t = [Trk() for _ in range(nwb)]
                NO = 4
                odt = BF16
                ob = [sb(st, f"{wname}o{i}", [P, 512], odt) for i in range(NO)]
                ob_t = [Trk() for _ in range(NO)]
                if epi == "swiglu":
                    sg = [sb(st, f"{wname}s{i}", [P, 512], F32) for i in range(NO)]
                    sg_t = [Trk() for _ in range(NO)]
                cnt = {"ps": 0, "o": 0}

                def issue_w(g):
                    sy.load("pool", [wt_t[g % nwb]], wt[g % nwb][:], wsrc_fn(g))
                    if wsrc2_fn is not None:
                        sy.load("pool", [wu_t[g % nwb]], wu[g % nwb][:], wsrc2_fn(g))

                issue_w(0)
                for g in range(ngroups):
                    if g + 1 < ngroups:
                        issue_w(g + 1)
                    W, Wt = wt[g % nwb], wt_t[g % nwb]
                    for (t0, n) in blocks:
                        for j in range(nj):
                            def mm(Wx, Wxt):
                                pi = cnt["ps"] % 8
                                cnt["ps"] += 1
                                ps, pst = ps_h[pi], PS[pi]
                                for k in range(kchunks):
                                    sy.op("pe", lambda e, k=k: e.matmul(
                                        ps[:, 0:n], lhsT=Wx[:, k, j * P:(j + 1) * P],
                                        rhs=XT[:, k, t0 - x0:t0 - x0 + n], start=(k == 0), stop=(k == kchunks - 1)),
                                        reads=[Wxt, XT_t[(k, t0)]], writes=[pst], signal=(k == kchunks - 1))
                                return ps, pst
                            ps, pst = mm(W, Wt)
                            oi = cnt["o"] % NO
                            cnt["o"] += 1
                            O, Ot = ob[oi], ob_t[oi]
                            row0 = g * gcols + j * P
                            if epi == "copy":
                                if oi % 2 == 0:
                                    sy.op("act", lambda e: e.copy(O[:, 0:n], ps[:, 0:n]), reads=[pst], writes=[Ot])
                                else:
                                    sy.op("dve", lambda e: e.tensor_copy(O[:, 0:n], ps[:, 0:n]), reads=[pst], writes=[Ot])
                            else:
                                ps2, pst2 = mm(wu[g % nwb], wu_t[g % nwb])
                                G, Gt = sg[oi], sg_t[oi]
                                sy.op("act", lambda e: e.activation(G[:, 0:n], ps[:, 0:n], AF.Silu),
                                      reads=[pst], writes=[Gt])
                                sy.op("dve", lambda e: e.tensor_tensor(O[:, 0:n], G[:, 0:n], ps2[:, 0:n], ALU.mult),
                                      reads=[Gt, pst2], writes=[Ot])
                            if dstT is None:
                                kk = row0 // P
                                s0 = t0 // 256
                                ns = n // 256
                                sy.store("sp", [Ot], yB[s0:s0 + ns, :, kk, :].rearrange("s p t -> p s t"),
                                         O[:, 0:n].rearrange("p (s t) -> p s t", t=256))
                            else:
                                sy.store("sp", [Ot], dstT[row0:row0 + P, t0:t0 + n], O[:, 0:n])
                sy.barrier()
                sy.release(wt_t + ob_t + (wu_t if wsrc2_fn is not None else []))

        def stage_down(l, with_ctx):
            passes = [[0, 1], [2, 3], [4, 5], [6, 7] + ([8] if with_ctx else [])]
            for pblocks in passes:
                blocks = [BLOCKS[i] for i in pblocks]
                x0 = blocks[0][0]
                ntok = sum(n for _, n in blocks)
                with ExitStack() as st:
                    HX = sb(st, "HX", [P, FC, 1280], BF16)
                    HX_t = {}
                    ld = Trk()
                    for k in range(FC):
                        for (t0, n) in blocks:
                            HX_t[(k, t0)] = ld
                    lds = [Trk() for _ in range(4)]
                    for q in range(4):
                        for k in range(11 * q, 11 * q + 11):
                            for (t0, n) in blocks:
                                HX_t[(k, t0)] = lds[q]
                        sy.load("sp", [lds[q]], HX[:, 11 * q:11 * q + 11, 0:ntok],
                                hidT[11 * q * P:(11 * q + 11) * P, x0:x0 + ntok].rearrange("(k p) t -> p k t", p=P))
                    stage_gemm_fm(
                        lambda g: w_down[l, :, g * 256:(g + 1) * 256].rearrange("(k p) n -> p k n", p=P),
                        8, 256, FC, blocks, HX, HX_t, x0, None, "wd")
                    sy.release(lds)

        def stage_transpose_out():
            with ExitStack() as st:
                hb = [sb(st, f"ohb{i}", [P, KC, 512], F32) for i in range(2)]
                hb_t = [Trk() for _ in range(2)]
                ox = [sb(st, f"oox{i}", [P, 4, D], F32) for i in range(2)]
                ox_t = [[Trk() for _ in range(16)] for _ in range(2)]
                pi = 0
                for bi, (t0, n) in enumerate(BLOCKS[:8]):
                    H, Ht = hb[bi % 2], hb_t[bi % 2]
                    OX, OXt = ox[bi % 2], ox_t[bi % 2]
                    for o in range(0, n, 256):
                        sy.load("sp", [Ht], H[:, :, o:o + 256], hB[(t0 + o) // 256])
                    for j in range(4):
                        for kg in range(4):
                            ps, pst = ps_h[pi % 8], PS[pi % 8]
                            pi += 1
                            for kk in range(4):
                                k = kg * 4 + kk
                                sy.op("pe", lambda e, ps=ps, kk=kk, k=k, j=j, H=H: e.transpose(
                                    ps[:, kk * P:(kk + 1) * P], H[:, k, j * P:(j + 1) * P], ident[:]),
                                    reads=[Ht, cst], writes=[pst], signal=(kk == 3))
                            tr = OXt[j * 4 + kg]
                            if (j * 4 + kg) % 2 == 0:
                                sy.op("act", lambda e, ps=ps, OX=OX, j=j, kg=kg: e.copy(
                                    OX[:, j, kg * 512:(kg + 1) * 512], ps[:, :]), reads=[pst], writes=[tr])
                            else:
                                sy.op("dve", lambda e, ps=ps, OX=OX, j=j, kg=kg: e.tensor_copy(
                                    OX[:, j, kg * 512:(kg + 1) * 512], ps[:, :]), reads=[pst], writes=[tr])
                    sy.store("sp", OXt, out[t0:t0 + n, :].rearrange("(j p) f -> p j f", p=P), OX[:])
                sy.barrier()
                sy.release(hb_t + [t for l_ in ox_t for t in l_])

        def dump(name, src_ap, shape, dtype):
            if name not in dbg:
                return
            rows, cols = shape
            with ExitStack() as st:
                tl = sb(st, "dbgt", [P, cols], dtype)
                tt = Trk()
                for r0 in range(0, rows, P):
                    sy.load("sp", [tt], tl[:], src_ap[r0:r0 + P, :])
                    sy.store("sp", [tt], dbg[name][r0:r0 + P, :], tl[:])
                sy.barrier()
                sy.release([tt])

        def main():
            stage_transpose_in()
            if stop_after == "x0":
                return
            for l in range(DEPTH):
                last = l == DEPTH - 1
                with_ctx = not last
                stage_mod(l)
                if l == 0 and "d_mod" in dbg:
                    tdm = Trk()
                    sy.store("sp", [small], dbg["d_mod"][:, 0:96], modb[:])
                    sy.store("sp", [small], dbg["d_mod"][:, 96:192], modc[:])
                    for ii, nm_ in enumerate(("a1", "gt1g", "a2", "gt2g")):
                        for jj in range(2):
                            o_ = 192 + (ii * 2 + jj) * 16
                            sy.store("sp", [small], dbg["d_mod"][:, o_:o_ + 16], der[nm_][jj][:])
                    with nc.allow_non_contiguous_dma(reason="debug"):
                        sy.store("sp", [small], dbg["d_mod"][:, 320:321], nlam[:])
                        sy.store("sp", [small], dbg["d_mod"][:, 321:322], gdT[:])
                    sy.barrier()
                if stop_after == "mod":
                    return
                for rb in RANGES:
                    blocks = [BLOCKS[i] for i in rb]
                    x0 = blocks[0][0]
                    with ExitStack() as st:
                        XT = sb(st, "XT", [P, KC, 2304], BF16)
                        XT_t = {(k, t0): Trk() for k in range(KC) for (t0, n) in blocks}
                        stage_norm(l, blocks, XT, XT_t, None, 1)
                        if stop_after == "norm1":
                            if "d_xt" in dbg:
                                for k_ in range(KC):
                                    sy.store("sp", [small], dbg["d_xt"][k_ * P:(k_ + 1) * P, :], XT[:, k_, 0:2048])
                                sy.barrier()
                            return
                        stage_inproj(l, blocks, XT, XT_t)
                if l == 0:
                    dump("d_qkT", qkT, (26 * P, T), BF16)
                    dump("d_vS", vS.rearrange("h p x -> (h p) x"), (10 * P, NCH * P), BF16)
                if stop_after == "inproj":
                    return
                for rb in RANGES:
                    blocks = [BLOCKS[i] for i in rb if (i < 8 or with_ctx)]
                    x0 = blocks[0][0]
                    with ExitStack() as st:
                        XT = sb(st, "XT", [P, KC, 2304], BF16)
                        XT_t = {(k, t0): Trk() for k in range(KC) for (t0, n) in blocks}
                        stage_attention(l, blocks, XT, XT_t, with_ctx)
                        stage_gemm_fm(
                            lambda g: w_out[l, :, g * 512:(g + 1) * 512].rearrange("(k p) n -> p k n", p=P),
                            4, 512, KC, blocks, XT, XT_t, x0, None, "wo")
                        if l == 0 and stop_after == "outproj":
                            continue
                        stage_norm(l, blocks, XT, XT_t, "gt1g", 2)
                        stage_gemm_fm(
                            lambda g: w_gate[l, :, g * 256:(g + 1) * 256].rearrange("(k p) n -> p k n", p=P),
                            22, 256, KC, blocks, XT, XT_t, x0, hidT, "wg", epi="swiglu",
                            wsrc2_fn=lambda g: w_up[l, :, g * 256:(g + 1) * 256].rearrange("(k p) n -> p k n", p=P))
                if l == 0 and stop_after == "outproj":
                    return
                stage_down(l, with_ctx)
                blocks = [BLOCKS[i] for i in range(9) if (i < 8 or with_ctx)]
                stage_norm(l, blocks, None, None, "gt2g", None)
                if l == 0:
                    if stop_after == "layer0":
                        return
            stage_transpose_out()

        main()
        sy.barrier()
    return nc, sy


_CACHE = {}


def _host_tables():
    if "tabs" not in _CACHE:
        (C64, S64, PM64), (C128, S128, PM128) = _rope_tables()
        rope = np.stack([C64, S64, C128, S128]).astype(np.float32)
        pmm = np.stack([PM64, PM128]).astype(np.float32)
        idx, val = _na_index_tables()
        _CACHE["tabs"] = (rope, pmm, idx, val)
    return _CACHE["tabs"]


def kernel(x, c, ctx, c_ctx, w_mod, b_mod, g_pre1, g_post1, g_pre2, g_post2,
           w_in, w_out, lam_q1, lam_k1, lam_q2, lam_k2, g_diff, g_qn, g_kn, rpb,
           w_gate, w_up, w_down):
    f = lambda a: np.ascontiguousarray(np.asarray(a, dtype=np.float32))
    x, c, ctx, c_ctx = f(x), f(c), f(ctx), f(c_ctx)
    rope, pmm, idx, val = _host_tables()
    rpb = f(rpb)
    rpb_flat = rpb.reshape(DEPTH, 4, 15 * 31)
    nab = np.where(val[None, None], rpb_flat[:, :, idx], np.float32(-30000.0)).astype(np.float32)
    shared = {
        "w_mod": f(w_mod), "b_mod": f(b_mod),
        "gains": np.stack([f(g_pre1), f(g_post1), f(g_pre2), f(g_post2)]),
        "w_in": f(w_in), "w_out": f(w_out),
        "lam": np.stack([f(lam_q1), f(lam_k1), f(lam_q2), f(lam_k2)]),
        "hg": np.stack([f(g_diff), f(g_qn), f(g_kn)]),
        "w_gate": f(w_gate), "w_up": f(w_up), "w_down": f(w_down),
        "rope": rope, "pm": pmm, "nab": np.ascontiguousarray(nab),
        "ident": np.eye(P, dtype=np.float32),
    }
    nc, _ = build_program()
    if N_CORES == 1:
        m = dict(shared)
        m["x"], m["ctx"], m["cc"] = x[0], ctx[0], np.stack([c[0], c_ctx])
        res = run_bass_kernel_spmd(nc, [m], core_ids=[0])
        _CACHE["last_results"] = res
        return np.stack([np.asarray(res.results[0]["out"], dtype=np.float32)])
    active = ACTIVE_CORES
    zero = {k: np.zeros_like(v) for k, v in shared.items()}
    zero["x"] = np.zeros_like(x[0])
    zero["ctx"] = np.zeros_like(ctx[0])
    zero["cc"] = np.zeros((2, D), np.float32)
    in_maps = [zero] * 8
    in_maps = list(in_maps)
    for b, core in enumerate(active):
        m = dict(shared)
        m["x"] = x[b]
        m["ctx"] = ctx[b]
        m["cc"] = np.stack([c[b], c_ctx])
        in_maps[core] = m
    res = run_bass_kernel_spmd(nc, in_maps, core_ids=list(range(8)))
    _CACHE["last_results"] = res
    return np.stack([np.asarray(res.results[core]["out"], dtype=np.float32) for core in active])
```

```python
import math
import numpy as np
import concourse.bass as bass
import concourse.mybir as mybir
from concourse.bass_utils import run_bass_kernel_spmd
from contextlib import ExitStack

F32 = mybir.dt.float32
BF16 = mybir.dt.bfloat16
AF = mybir.ActivationFunctionType
ALU = mybir.AluOpType

P = 128
D = 2048
KC = D // P
S = 4096
CTX = 256
T = S + CTX
DEPTH = 2
GRID_W = 64
DFF = 5632
FC = DFF // P
INC = 4608
EPS = 1e-6
NCH = T // P
N_CORES = 4
ACTIVE_CORES = (0, 1, 4, 5)

BLOCKS = [(i * 512, 512) for i in range(8)] + [(4096, 256)]
RANGES = [[0, 1, 2, 3], [4, 5, 6, 7, 8]]

DEBUG = {}


class Trk:
    __slots__ = ("w", "r", "dsem", "excl")

    def __init__(self, excl=False):
        self.w = None
        self.r = {}
        self.dsem = None
        self.excl = excl


class Sync:
    CE = ("pe", "act", "dve", "pool")

    def __init__(self, nc, n_dma_sems=80):
        self.nc = nc
        self.eng = {"pe": nc.tensor, "act": nc.scalar, "dve": nc.vector, "pool": nc.gpsimd, "sp": nc.sync}
        self.sem = {e: nc.alloc_semaphore(f"prog_{e}") for e in self.CE}
        self.cnt = {e: 0 for e in self.CE}
        self.known = {e: {} for e in self.eng}
        self.free_dsems = [nc.alloc_semaphore(f"dsem{i}") for i in range(n_dma_sems)]
        self.dcnt = {}
        self.dsem_by_num = {}
        self.n_wait = 0
        self.n_ins = 0

    def _wait(self, e, waits):
        eng = self.eng[e]
        kn = self.known[e]
        for num, (sem, val) in waits.items():
            if kn.get(num, 0) >= val:
                continue
            eng.wait_ge(sem, val)
            kn[num] = val
            self.n_wait += 1

    @staticmethod
    def _need(waits, tk):
        if tk is None:
            return
        sem, val = tk
        cur = waits.get(sem.num)
        if cur is None or cur[1] < val:
            waits[sem.num] = (sem, val)

    def _collect(self, e, reads, writes):
        waits = {}
        own = self.sem[e].num if e in self.sem else None
        for t in reads:
            self._need(waits, t.w)
            if t.excl:
                for num, tk in t.r.items():
                    if num != own:
                        self._need(waits, tk)
        for t in writes:
            self._need(waits, t.w)
            for tk in t.r.values():
                self._need(waits, tk)
        if e == "pe":
            waits.pop(self.sem["pe"].num, None)
        return waits

    @staticmethod
    def _mark(tk, reads, writes):
        sem, val = tk
        for t in reads:
            cur = t.r.get(sem.num)
            if cur is None or cur[1] < val:
                t.r[sem.num] = tk
        for t in writes:
            t.w = tk
            t.r = {}

    def op(self, e, fn, reads=(), writes=(), signal=True):
        self._wait(e, self._collect(e, reads, writes))
        ins = fn(self.eng[e])
        self.n_ins += 1
        if signal:
            self.cnt[e] += 1
            ins.then_inc(self.sem[e], 1)
            tk = (self.sem[e], self.cnt[e])
        else:
            tk = (self.sem[e], self.cnt[e] + 1)
        self._mark(tk, reads, writes)
        return ins

    def _dsem(self, t):
        if t.dsem is None:
            t.dsem = self.free_dsems.pop()
            self.dsem_by_num[t.dsem.num] = t.dsem
            self.dcnt.setdefault(t.dsem.num, 0)
        return t.dsem

    def release(self, trks):
        for t in trks:
            if t.dsem is not None:
                self.free_dsems.append(t.dsem)
                t.dsem = None

    def load(self, q, trks, out_ap, in_ap, **kw):
        self._wait(q, self._collect(q, (), trks))
        sem = self._dsem(trks[0])
        ins = self.eng[q].dma_start(out=out_ap, in_=in_ap, **kw)
        self.dcnt[sem.num] += 1
        ins.then_inc(sem, 16)
        self.n_ins += 1
        self._mark((sem, 16 * self.dcnt[sem.num]), (), trks)

    def store(self, q, trks, out_ap, in_ap, **kw):
        self._wait(q, self._collect(q, trks, ()))
        sem = self._dsem(trks[0])
        ins = self.eng[q].dma_start(out=out_ap, in_=in_ap, **kw)
        self.dcnt[sem.num] += 1
        ins.then_inc(sem, 16)
        self.n_ins += 1
        self._mark((sem, 16 * self.dcnt[sem.num]), trks, ())

    def barrier(self):
        waits = {}
        for e in self.CE:
            if self.cnt[e] > 0:
                waits[self.sem[e].num] = (self.sem[e], self.cnt[e])
        for num, c in self.dcnt.items():
            if c > 0:
                waits[num] = (self.dsem_by_num[num], 16 * c)
        for e in self.eng:
            self._wait(e, dict(waits))


def _rope_tables():
    t = np.arange(S)
    row = (t // GRID_W).astype(np.float32)
    col = (t % GRID_W).astype(np.float32)

    def tab(d):
        half = d // 2
        quarter = half // 2
        inv = (np.float32(10000.0) ** (-np.arange(quarter, dtype=np.float32) / np.float32(quarter))).astype(np.float32)
        C = np.zeros((P, S), np.float32)
        Sg = np.zeros((P, S), np.float32)
        PM = np.zeros((P, P), np.float32)
        for p in range(P):
            m = p % d
            pos = row if (m // half) == 0 else col
            j = m % half
            i = j % quarter
            second = j // quarter
            ang = (pos * inv[i]).astype(np.float32)
            C[p] = np.cos(ang)
            Sg[p] = np.sin(ang) * (-1.0 if second == 0 else 1.0)
            partner = p + quarter if second == 0 else p - quarter
            PM[partner, p] = 1.0
        return C, Sg, PM

    return tab(64), tab(128)


def _na_index_tables():
    idx = np.zeros((3, 8, P, 512), np.int64)
    val = np.zeros((3, 8, P, 512), bool)
    for v, j in enumerate((0, 3, 7)):
        for i in range(8):
            kc = 4 * j - 2 + i
            kk = np.arange(P)
            krow = 2 * kc + kk // 64
            kcol = kk % 64
            qq = np.arange(512)
            qrow = 8 * j + qq // 64
            qcol = qq % 64
            r0 = np.clip(qrow - 4, 0, 64 - 8)
            c0 = np.clip(qcol - 8, 0, 64 - 16)
            rok = (krow[:, None] >= r0[None, :]) & (krow[:, None] < r0[None, :] + 8)
            cok = (kcol[:, None] >= c0[None, :]) & (kcol[:, None] < c0[None, :] + 16)
            roff = krow[:, None] - qrow[None, :] + 7
            coff = np.clip(kcol[:, None] - qcol[None, :], -15, 15) + 15
            ok = rok & cok & (krow[:, None] >= 0) & (krow[:, None] < 64)
            idx[v, i] = np.where(ok, roff * 31 + coff, 0)
            val[v, i] = ok
    return idx, val


def build_program():
    nc = bass.Bass("TRN2", target_bir_lowering=False)
    dt = nc.dram_tensor

    def din(name, shape, dtype=F32):
        return dt(name, list(shape), dtype, kind="ExternalInput").ap()

    x_in = din("x", [S, D])
    ctx_in = din("ctx", [CTX, D])
    cc_in = din("cc", [2, D])
    w_mod = din("w_mod", [DEPTH, D, 6 * D])
    b_mod = din("b_mod", [DEPTH, 6 * D])
    gains = din("gains", [4, DEPTH, D])
    w_in = din("w_in", [DEPTH, D, INC])
    w_out = din("w_out", [DEPTH, D, D])
    lam_in = din("lam", [4, DEPTH, 64])
    hg_in = din("hg", [3, DEPTH, P])
    w_gate = din("w_gate", [DEPTH, D, DFF])
    w_up = din("w_up", [DEPTH, D, DFF])
    w_down = din("w_down", [DEPTH, DFF, D])
    rope_in = din("rope", [4, P, S])
    pm_in = din("pm", [2, P, P])
    nab_in = din("nab", [DEPTH, 4, 3, 8, P, 512])
    ident_in = din("ident", [P, P])
    out = dt("out", [S, D], F32, kind="ExternalOutput").ap()

    NSUB = T // 256
    hB = dt("hB", [NSUB, P, KC, 256], F32, kind="Internal").ap()
    yB = dt("yB", [NSUB, P, KC, 256], BF16, kind="Internal").ap()
    qkT = dt("qkT", [26 * P, T], BF16, kind="Internal").ap()
    vS = dt("vS", [10, P, NCH * P], BF16, kind="Internal").ap()
    hidT = dt("hidT", [DFF, T], BF16, kind="Internal").ap()

    dbg = {}
    for name, shape, dtype in DEBUG.get("outs", []):
        dbg[name] = dt(name, list(shape), dtype, kind="ExternalOutput").ap()

    sy = Sync(nc)
    stop_after = DEBUG.get("stop_after")

    with ExitStack() as top:
        uid = [0]

        def sb(stack, name, shape, dtype):
            uid[0] += 1
            return stack.enter_context(nc.sbuf_tensor(f"s{uid[0]}_{name}", list(shape), dtype))

        ps_h = [top.enter_context(nc.psum_tensor(f"ps{i}", [P, 512], F32)) for i in range(8)]
        PS = [Trk(excl=True) for _ in range(8)]

        ident = sb(top, "ident", [P, P], F32)
        ones_bf = sb(top, "ones_bf", [P, P], BF16)
        ones_f = sb(top, "ones_f", [P, P], F32)
        pm = sb(top, "pm", [P, 2, P], F32)
        cst = Trk()
        sy.load("sp", [cst], ident[:], ident_in[:, :])
        sy.load("sp", [cst], pm[:], pm_in.rearrange("a p q -> p a q"))
        sy.op("dve", lambda e: e.memset(ones_bf[:], 1.0), writes=[cst])
        sy.op("dve", lambda e: e.memset(ones_f[:], 1.0), writes=[cst])
        pmb = sb(top, "pmb", [P, 2, P], BF16)
        sy.op("dve", lambda e: e.tensor_copy(pmb[:], pm[:]), reads=[cst], writes=[cst])
        modb = sb(top, "modb", [P, 96], F32)
        modc = sb(top, "modc", [P, 96], F32)
        der = {n: [sb(top, f"{n}{i}", [P, KC], F32) for i in range(2)]
               for n in ("a1", "gt1g", "a2", "gt2g")}
        gT = sb(top, "gT", [P, 4, KC], F32)
        hgT = sb(top, "hgT", [P, 3], F32)
        gdT = sb(top, "gdT", [P, 1], F32)
        nlam = sb(top, "nlam", [P, 1], F32)
        small = Trk()

        sy.barrier()

        def stage_transpose_in():
            with ExitStack() as st:
                xin = [sb(st, f"xin{i}", [P, 4, D], F32) for i in range(2)]
                xin_t = [Trk() for _ in range(2)]
                hb = [sb(st, f"hbo{i}", [P, KC, 512], F32) for i in range(2)]
                hb_t = [[Trk() for _ in range(KC)] for _ in range(2)]
                pi = 0
                for bi, (t0, n) in enumerate(BLOCKS):
                    nt = n // P
                    X, Xt = xin[bi % 2], xin_t[bi % 2]
                    H, Ht = hb[bi % 2], hb_t[bi % 2]
                    if t0 < S:
                        src = x_in[t0:t0 + n, :]
                    else:
                        src = ctx_in[:, :]
                    sy.load("sp", [Xt], X[:, 0:nt, :], src.rearrange("(j p) f -> p j f", p=P))
                    for k in range(KC):
                        ps, pst = ps_h[pi % 8], PS[pi % 8]
                        pi += 1
                        for j in range(nt):
                            sy.op("pe", lambda e, j=j, k=k, ps=ps, X=X: e.transpose(
                                ps[:, j * P:(j + 1) * P], X[:, j, k * P:(k + 1) * P], ident[:]),
                                reads=[Xt, cst], writes=[pst], signal=(j == nt - 1))
                        eng = "act" if k % 2 == 0 else "dve"
                        if eng == "act":
                            sy.op("act", lambda e, ps=ps, H=H, k=k, n=n: e.copy(H[:, k, 0:n], ps[:, 0:n]),
                                  reads=[pst], writes=[Ht[k]])
                        else:
                            sy.op("dve", lambda e, ps=ps, H=H, k=k, n=n: e.tensor_copy(H[:, k, 0:n], ps[:, 0:n]),
                                  reads=[pst], writes=[Ht[k]])
                    for o in range(0, n, 256):
                        sy.store("sp", Ht, hB[(t0 + o) // 256], H[:, :, o:o + 256])
                sy.barrier()
                sy.release(xin_t + [t for l in hb_t for t in l])

        def stage_mod(l):
            with ExitStack() as st:
                cT = sb(st, "cT", [P, 2, KC], F32)
                scf = sb(st, "scf", [P, KC, 2], F32)
                scb = sb(st, "scb", [P, KC, 2], BF16)
                bmT = sb(st, "bmT", [P, 96], F32)
                lamc = sb(st, "lamc", [64, 4], F32)
                prod = sb(st, "prod", [64, 2], F32)
                ex = sb(st, "ex", [P, 2], F32)
                tmp = sb(st, "tmpm", [P, KC], F32)
                wm = [sb(st, f"wm{i}", [P, KC, 512], BF16) for i in range(2)]
                wm_t = [Trk() for _ in range(2)]
                tk = Trk()
                with nc.allow_non_contiguous_dma(reason="tiny transposed vector loads"):
                    sy.load("sp", [tk], cT[:], cc_in.rearrange("a (k p) -> p a k", p=P))
                    sy.load("sp", [tk], bmT[:], b_mod[l].rearrange("(c p) -> p c", p=P))
                    for a_ in range(4):
                        sy.load("sp", [small], gT[:, a_, :], gains[a_, l, :].rearrange("(k p) -> p k", p=P))
                    sy.load("sp", [small], hgT[:], hg_in[:, l, :].rearrange("a p -> p a"))
                    sy.load("sp", [tk], lamc[:], lam_in[:, l, :].rearrange("a p -> p a"))
                for a in range(2):
                    sy.op("act", lambda e, a=a: e.activation(scf[:, :, a], cT[:, a, :], AF.Silu),
                          reads=[tk], writes=[tk])
                sy.op("dve", lambda e: e.tensor_copy(scb[:], scf[:]), reads=[tk], writes=[tk])
                lam_init = 0.8 - 0.6 * math.exp(-0.3 * l)
                sy.op("dve", lambda e: e.tensor_tensor(prod[:, 0:1], lamc[:, 0:1], lamc[:, 1:2], ALU.mult),
                      reads=[tk], writes=[tk])
                sy.op("dve", lambda e: e.tensor_tensor(prod[:, 1:2], lamc[:, 2:3], lamc[:, 3:4], ALU.mult),
                      reads=[tk], writes=[tk])
                sy.op("pe", lambda e: e.matmul(ps_h[7][:, 0:2], lhsT=ones_f[0:64, :], rhs=prod[:, :],
                                               start=True, stop=True), reads=[tk, cst], writes=[PS[7]])
                sy.op("act", lambda e: e.activation(ex[:], ps_h[7][:, 0:2], AF.Exp), reads=[PS[7]], writes=[tk])
                sy.op("dve", lambda e: e.tensor_tensor(nlam[:], ex[:, 1:2], ex[:, 0:1], ALU.subtract),
                      reads=[tk], writes=[small])
                sy.op("dve", lambda e: e.tensor_scalar(nlam[:], nlam[:], -lam_init, None, ALU.add),
                      reads=[small], writes=[small])
                sy.op("dve", lambda e: e.tensor_scalar(gdT[:], hgT[:, 0:1], 1.0 - lam_init, None, ALU.mult),
                      reads=[small], writes=[small])
                NG = 24
                sy.load("pool", [wm_t[0]], wm[0][:], w_mod[l, :, 0:512].rearrange("(k p) n -> p k n", p=P))
                for g in range(NG):
                    if g + 1 < NG:
                        sy.load("pool", [wm_t[(g + 1) % 2]], wm[(g + 1) % 2][:],
                                w_mod[l, :, (g + 1) * 512:(g + 2) * 512].rearrange("(k p) n -> p k n", p=P))
                    W, Wt = wm[g % 2], wm_t[g % 2]
                    for j in range(4):
                        c = 4 * g + j
                        for k in range(KC):
                            sy.op("pe", lambda e, W=W, j=j, k=k, c=c: e.matmul(
                                ps_h[6][:, 2 * c:2 * c + 2], lhsT=W[:, k, j * P:(j + 1) * P], rhs=scb[:, k, :],
                                start=(k == 0), stop=(k == KC - 1)),
                                reads=[Wt, tk], writes=[PS[6]], signal=(k == KC - 1))
                psv = ps_h[6][:, 0:192].rearrange("p (c a) -> p c a", a=2)
                sy.op("dve", lambda e: e.tensor_tensor(modb[:], psv[:, :, 0], bmT[:], ALU.add),
                      reads=[PS[6], tk], writes=[small])
                sy.op("dve", lambda e: e.tensor_tensor(modc[:], psv[:, :, 1], bmT[:], ALU.add),
                      reads=[PS[6], tk], writes=[small])
                for i, mod in enumerate((modb, modc)):
                    for nm, sc_off, gidx in (("a1", 16, 0), ("a2", 64, 2)):
                        sy.op("dve", lambda e, mod=mod, sc_off=sc_off: e.tensor_scalar(
                            tmp[:], mod[:, sc_off:sc_off + 16], 1.0, None, ALU.add), reads=[small, tk], writes=[tk])
                        sy.op("dve", lambda e, nm=nm, i=i, gidx=gidx: e.tensor_tensor(
                            der[nm][i][:], tmp[:], gT[:, gidx, :], ALU.mult), reads=[tk, small], writes=[small])
                    for nm, gt_off, gidx in (("gt1g", 32, 1), ("gt2g", 80, 3)):
                        sy.op("dve", lambda e, nm=nm, i=i, gidx=gidx, mod=mod, gt_off=gt_off: e.tensor_tensor(
                            der[nm][i][:], mod[:, gt_off:gt_off + 16], gT[:, gidx, :], ALU.mult),
                            reads=[small], writes=[small])
                sy.barrier()
                sy.release([tk] + wm_t)

        def ssq_rstd(st_tiles, src_fn, srck_trks, n, d_count, ps_idx):
            sq, sq_t = st_tiles["sq"], st_tiles["sq_t"]
            ps, pst = ps_h[ps_idx], PS[ps_idx]
            for k in range(KC):
                s_, s_t = sq[k % len(sq)], sq_t[k % len(sq)]
                if k % 2 == 0:
                    sy.op("act", lambda e, s_=s_, k=k: e.activation(s_[:, 0:n], src_fn(k), AF.Square),
                          reads=[srck_trks[k]], writes=[s_t])
                else:
                    sy.op("dve", lambda e, s_=s_, k=k: e.tensor_tensor(s_[:, 0:n], src_fn(k), src_fn(k), ALU.mult),
                          reads=[srck_trks[k]], writes=[s_t])
                sy.op("pe", lambda e, s_=s_, k=k: e.matmul(ps[:, 0:n], lhsT=ones_bf[:], rhs=s_[:, 0:n],
                                                          start=(k == 0), stop=(k == KC - 1)),
                      reads=[s_t, cst], writes=[pst], signal=True)
            rt, rt_t = st_tiles["rt"], st_tiles["rt_t"]
            sy.op("act", lambda e: e.activation(rt[:, 0:n], ps[:, 0:n], AF.Sqrt, bias=st_tiles["eps"][:, 0:1],
                                                scale=1.0 / d_count),
                  reads=[pst, st_tiles["eps_t"]], writes=[rt_t])
            sy.op("dve", lambda e: e.reciprocal(rt[:, 0:n], rt[:, 0:n]), reads=[rt_t], writes=[rt_t])
            return rt, rt_t

        def stage_norm(l, blocks, XT, XT_t, resid, modn):
            with ExitStack() as st:
                NB = 2
                Hs = [sb(st, f"nH{i}", [P, KC, 256], F32) for i in range(NB)]
                H_t = [[Trk() for _ in range(KC)] for _ in range(NB)]
                Ys = [sb(st, f"nY{i}", [P, KC, 256], BF16) for i in range(NB)] if resid else None
                Y_t = [[Trk() for _ in range(KC)] for _ in range(NB)]
                tiles = {
                    "sq": [sb(st, f"nsq{i}", [P, 512], BF16) for i in range(4)],
                    "sq_t": [Trk() for _ in range(4)],
                    "rt": None, "rt_t": None,
                    "eps": sb(st, "neps", [P, 1], F32),
                }
                eps_t = Trk()
                sy.op("dve", lambda e: e.memset(tiles["eps"][:], EPS), writes=[eps_t])
                tiles["eps_t"] = eps_t
                rts = [sb(st, f"nrt{i}", [P, 512], F32) for i in range(4)]
                rt_ts = [Trk() for _ in range(4)]
                tmps = [sb(st, f"ntmp{i}", [P, 512], F32) for i in range(4)]
                tmp_ts = [Trk() for _ in range(4)]
                ri = 0
                ti = 0
                x0 = blocks[0][0] if XT is not None else 0
                subs = []
                for (bt0, bn) in blocks:
                    for o in range(0, bn, 256):
                        subs.append((bt0 + o, 256, bt0))
                for bi, (t0, n, kt0) in enumerate(subs):
                    is_ctx = 1 if t0 >= S else 0
                    H, Ht = Hs[bi % NB], H_t[bi % NB]
                    sy.load("sp", Ht, H[:, :, 0:n], hB[t0 // 256])
                    if resid:
                        Y, Yt = Ys[bi % NB], Y_t[bi % NB]
                        sy.load("sp", Yt, Y[:, :, 0:n], yB[t0 // 256])
                        tiles["rt"], tiles["rt_t"] = rts[ri % 4], rt_ts[ri % 4]
                        ri += 1
                        r1, r1t = ssq_rstd(tiles, lambda k, Y=Y: Y[:, k, 0:n], Yt, n, D, 6)
                        gtg = der[resid][is_ctx]
                        for k in range(KC):
                            tm, tmt = tmps[ti % 4], tmp_ts[ti % 4]
                            ti += 1
                            sy.op("dve", lambda e, tm=tm, Y=Y, k=k, r1=r1: e.scalar_tensor_tensor(
                                tm[:, 0:n], Y[:, k, 0:n], gtg[:, k:k + 1], r1[:, 0:n], ALU.mult, ALU.mult),
                                reads=[Yt[k], r1t, small], writes=[tmt])
                            sy.op("pool", lambda e, tm=tm, H=H, k=k: e.tensor_tensor(
                                H[:, k, 0:n], tm[:, 0:n], H[:, k, 0:n], ALU.add),
                                reads=[tmt, Ht[k]], writes=[Ht[k]])
                    if modn:
                        tiles["rt"], tiles["rt_t"] = rts[ri % 4], rt_ts[ri % 4]
                        ri += 1
                        r2, r2t = ssq_rstd(tiles, lambda k, H=H: H[:, k, 0:n], Ht, n, D, 7)
                        a = der["a1" if modn == 1 else "a2"][is_ctx]
                        mod = modc if is_ctx else modb
                        sh_off = 0 if modn == 1 else 48
                        for k in range(KC):
                            tm, tmt = tmps[ti % 4], tmp_ts[ti % 4]
                            ti += 1
                            sy.op("dve", lambda e, tm=tm, H=H, k=k, r2=r2: e.tensor_tensor(
                                tm[:, 0:n], H[:, k, 0:n], r2[:, 0:n], ALU.mult),
                                reads=[Ht[k], r2t], writes=[tmt])
                            sy.op("act", lambda e, tm=tm, k=k, a=a, mod=mod: e.activation(
                                XT[:, k, t0 - x0:t0 - x0 + n], tm[:, 0:n], AF.Identity,
                                bias=mod[:, sh_off + k:sh_off + k + 1], scale=a[:, k:k + 1]),
                                reads=[tmt, small], writes=[XT_t[(k, kt0)]])
                    if resid:
                        sy.store("sp", Ht, hB[t0 // 256], H[:, :, 0:n])
                sy.barrier()
                sy.release([t for l_ in H_t for t in l_] + [t for l_ in Y_t for t in l_])

        def run_pending(pending):
            alive = []
            for g in pending:
                try:
                    next(g)
                    alive.append(g)
                except StopIteration:
                    pass
            pending[:] = alive

        def drain(pending):
            while pending:
                run_pending(pending)

        def inproj_plan(c):
            if c < 4:
                return ("rope64", c)
            if c < 8:
                return ("rope64", c)
            if c < 12:
                return ("v", c - 8)
            if c < 20:
                return ("nrope_q", 8 + (c - 12))
            if c < 22:
                return ("nrope_k", 16 + (c - 20))
            if c < 24:
                return ("v", 4 + (c - 22))
            if c < 28:
                return ("plain", 18 + (c - 24))
            if c < 32:
                return ("plain", 22 + (c - 28))
            return ("v", 6 + (c - 32))

        def stage_inproj(l, blocks, XT, XT_t):
            x0 = blocks[0][0]
            with ExitStack() as st:
                wt = [sb(st, f"wi{i}", [P, KC, 512], BF16) for i in range(2)]
                wt_t = [Trk() for _ in range(2)]
                rp = [sb(st, f"rp{i}", [P, 2, 512], F32) for i in range(2)]
                rp_t = [Trk() for _ in range(2)]
                NR = 6
                xf = [sb(st, f"xf{i}", [P, 512], F32) for i in range(NR)]
                xf_t = [Trk() for _ in range(NR)]
                xb = [sb(st, f"xb{i}", [P, 512], BF16) for i in range(NR)]
                xb_t = [Trk() for _ in range(NR)]
                sq = [sb(st, f"isq{i}", [P, 512], BF16) for i in range(2)]
                sq_t = [Trk() for _ in range(2)]
                rt = [sb(st, f"irt{i}", [P, 512], F32) for i in range(2)]
                rt_t = [Trk() for _ in range(2)]
                t1 = [sb(st, f"it1{i}", [P, 512], F32) for i in range(NR)]
                t1_t = [Trk() for _ in range(NR)]
                t2 = [sb(st, f"it2{i}", [P, 512], F32) for i in range(NR)]
                t2_t = [Trk() for _ in range(NR)]
                ob = [sb(st, f"iob{i}", [P, 512], BF16) for i in range(NR)]
                ob_t = [Trk() for _ in range(NR)]
                epsT = sb(st, "ieps", [P, 1], F32)
                eps_t = Trk()
                sy.op("dve", lambda e: e.memset(epsT[:], EPS), writes=[eps_t])
                cnt = {"main": 0, "aux": 0, "buf": 0, "rp": 0}
                pending = []

                def w_src(g):
                    return w_in[l, :, g * 512:(g + 1) * 512].rearrange("(k p) n -> p k n", p=P)

                def epi_fm(kind, dest, ps, pst, t0, n, rpi):
                    i = cnt["buf"] % NR
                    cnt["buf"] += 1
                    O, Ot = ob[i], ob_t[i]
                    latent = t0 < S
                    dst = qkT[dest * P:(dest + 1) * P, t0:t0 + n]
                    if DEBUG.get("inproj_mode") == "plain_all":
                        kind = "plain"
                    if DEBUG.get("inproj_mode") == "no_rope" and kind == "rope64":
                        kind = "plain"
                    if DEBUG.get("inproj_mode") == "no_nrope" and kind.startswith("nrope"):
                        kind = "plain"
                    if kind == "plain" or (kind == "rope64" and not latent):
                        sy.op("act", lambda e: e.copy(O[:, 0:n], ps[:, 0:n]), reads=[pst], writes=[Ot])
                        sy.store("sp", [Ot], dst, O[:, 0:n])
                        return
                    X, Xt_ = xf[i], xf_t[i]
                    XB, XBt = xb[i], xb_t[i]
                    R, Rt = (rp[rpi], rp_t[rpi]) if rpi is not None else (None, None)
                    A, At = t1[i], t1_t[i]
                    B, Bt = t2[i], t2_t[i]
                    if kind == "rope64":
                        sy.op("act", lambda e: e.copy(XB[:, 0:n], ps[:, 0:n]), reads=[pst], writes=[XBt])
                        yield
                        pb_i = 4 + cnt["main"] % 2
                        pb, pbt = ps_h[pb_i], PS[pb_i]
                        sy.op("pe", lambda e: e.matmul(pb[:, 0:n], lhsT=pmb[:, 0, :], rhs=XB[:, 0:n],
                                                       start=True, stop=True), reads=[XBt, cst], writes=[pbt])
                        sy.op("dve", lambda e: e.tensor_tensor(A[:, 0:n], ps[:, 0:n], R[:, 0, 0:n], ALU.mult),
                              reads=[pst, Rt], writes=[At])
                        sy.op("dve", lambda e: e.tensor_tensor(B[:, 0:n], pb[:, 0:n], R[:, 1, 0:n], ALU.mult),
                              reads=[pbt, Rt], writes=[Bt])
                        sy.op("dve", lambda e: e.tensor_tensor(O[:, 0:n], A[:, 0:n], B[:, 0:n], ALU.add),
                              reads=[At, Bt], writes=[Ot])
                        sy.store("sp", [Ot], dst, O[:, 0:n])
                        return
                    s_, s_t = sq[cnt["aux"] % 2], sq_t[cnt["aux"] % 2]
                    r_, r_t = rt[cnt["aux"] % 2], rt_t[cnt["aux"] % 2]
                    pa, pat = ps_h[6 + cnt["aux"] % 2], PS[6 + cnt["aux"] % 2]
                    cnt["aux"] += 1
                    gcol = 1 if kind == "nrope_q" else 2
                    sy.op("act", lambda e: e.activation(s_[:, 0:n], ps[:, 0:n], AF.Square),
                          reads=[pst], writes=[s_t])
                    sy.op("act", lambda e: e.activation(X[:, 0:n], ps[:, 0:n], AF.Identity,
                                                        scale=hgT[:, gcol:gcol + 1]),
                          reads=[pst, small], writes=[Xt_])
                    if latent:
                        sy.op("act", lambda e: e.activation(XB[:, 0:n], ps[:, 0:n], AF.Identity,
                                                            scale=hgT[:, gcol:gcol + 1]),
                              reads=[pst, small], writes=[XBt])
                    yield
                    sy.op("pe", lambda e: e.matmul(pa[:, 0:n], lhsT=ones_bf[:], rhs=s_[:, 0:n],
                                                   start=True, stop=True), reads=[s_t, cst], writes=[pat])
                    sy.op("act", lambda e: e.activation(r_[:, 0:n], pa[:, 0:n], AF.Ln, bias=epsT[:, 0:1],
                                                        scale=1.0 / P), reads=[pat, eps_t], writes=[r_t])
                    sy.op("act", lambda e: e.activation(r_[:, 0:n], r_[:, 0:n], AF.Exp, scale=-0.5),
                          reads=[r_t], writes=[r_t])
                    if not latent:
                        sy.op("dve", lambda e: e.tensor_tensor(O[:, 0:n], X[:, 0:n], r_[:, 0:n], ALU.mult),
                              reads=[Xt_, r_t], writes=[Ot])
                        sy.store("sp", [Ot], dst, O[:, 0:n])
                        return
                    pb_i = 4 + cnt["main"] % 2
                    pb, pbt = ps_h[pb_i], PS[pb_i]
                    sy.op("pe", lambda e: e.matmul(pb[:, 0:n], lhsT=pmb[:, 1, :], rhs=XB[:, 0:n],
                                                   start=True, stop=True), reads=[XBt, cst], writes=[pbt])
                    sy.op("dve", lambda e: e.tensor_tensor(A[:, 0:n], X[:, 0:n], R[:, 0, 0:n], ALU.mult),
                          reads=[Xt_, Rt], writes=[At])
                    sy.op("dve", lambda e: e.tensor_tensor(B[:, 0:n], pb[:, 0:n], R[:, 1, 0:n], ALU.mult),
                          reads=[pbt, Rt], writes=[Bt])
                    yield
                    sy.op("dve", lambda e: e.tensor_tensor(A[:, 0:n], A[:, 0:n], B[:, 0:n], ALU.add),
                          reads=[At, Bt], writes=[At])
                    sy.op("dve", lambda e: e.tensor_tensor(O[:, 0:n], A[:, 0:n], r_[:, 0:n], ALU.mult),
                          reads=[At, r_t], writes=[Ot])
                    sy.store("sp", [Ot], dst, O[:, 0:n])

                NG = 9
                sy.load("pool", [wt_t[0]], wt[0][:], w_src(0))
                for g in range(NG):
                    if g + 1 < NG:
                        sy.load("pool", [wt_t[(g + 1) % 2]], wt[(g + 1) % 2][:], w_src(g + 1))
                    W, Wt = wt[g % 2], wt_t[g % 2]
                    plans = [inproj_plan(4 * g + j) for j in range(4)]
                    kinds = set(p_[0] for p_ in plans)
                    need_rope = None
                    if "rope64" in kinds:
                        need_rope = 0
                    elif "nrope_q" in kinds or "nrope_k" in kinds:
                        need_rope = 1
                    for (t0, n) in blocks:
                        rpi = None
                        if need_rope is not None and t0 < S:
                            rpi = cnt["rp"] % 2
                            cnt["rp"] += 1
                            sy.load("sp", [rp_t[rpi]], rp[rpi][:, :, 0:n],
                                    rope_in[2 * need_rope:2 * need_rope + 2, :, t0:t0 + n].rearrange("a p t -> p a t"))
                        for j, (kind, dest) in enumerate(plans):
                            if kind == "v":
                                continue
                            pi = cnt["main"] % 4
                            cnt["main"] += 1
                            ps, pst = ps_h[pi], PS[pi]
                            for k in range(KC):
                                sy.op("pe", lambda e, ps=ps, W=W, j=j, k=k: e.matmul(
                                    ps[:, 0:n], lhsT=W[:, k, j * P:(j + 1) * P],
                                    rhs=XT[:, k, t0 - x0:t0 - x0 + n], start=(k == 0), stop=(k == KC - 1)),
                                    reads=[Wt, XT_t[(k, t0)]], writes=[pst], signal=(k == KC - 1))
                            run_pending(pending)
                            gen = epi_fm(kind, dest, ps, pst, t0, n, rpi)
                            pending.append(gen)
                        vj = [j for j, p_ in enumerate(plans) if p_[0] == "v"]
                        if vj:
                            c0, c1 = vj[0] * P, (vj[-1] + 1) * P
                            ncol = c1 - c0
                            for tt in range(n // P):
                                pi = cnt["main"] % 4
                                cnt["main"] += 1
                                ps, pst = ps_h[pi], PS[pi]
                                ta = t0 - x0 + tt * P
                                for k in range(KC):
                                    sy.op("pe", lambda e, ps=ps, W=W, k=k, ta=ta: e.matmul(
                                        ps[:, 0:ncol], lhsT=XT[:, k, ta:ta + P], rhs=W[:, k, c0:c1],
                                        start=(k == 0), stop=(k == KC - 1)),
                                        reads=[Wt, XT_t[(k, t0)]], writes=[pst], signal=(k == KC - 1))
                                run_pending(pending)
                                i = cnt["buf"] % NR
                                cnt["buf"] += 1
                                O, Ot = ob[i], ob_t[i]
                                sy.op("act", lambda e, O=O, ps=ps: e.copy(O[:, 0:ncol], ps[:, 0:ncol]),
                                      reads=[pst], writes=[Ot])
                                chunk = (t0 + tt * P) // P
                                heads = [plans[j][1] for j in vj]
                                h0 = heads[0]
                                sy.store("sp", [Ot],
                                         vS[h0:h0 + len(heads), :, chunk * P:(chunk + 1) * P].rearrange("h p d -> p h d"),
                                         O[:, 0:ncol].rearrange("p (h d) -> p h d", d=P))
                drain(pending)
                sy.barrier()
                sy.release(wt_t + rp_t + ob_t)

        def stage_attention(l, blocks, XT, XT_t, with_ctx):
            x0 = blocks[0][0]
            with ExitStack() as st:
                KT = [sb(st, f"KT{i}", [P, T], BF16) for i in range(2)]
                KT_t = [Trk() for _ in range(2)]
                VT = [sb(st, f"VT{i}", [P, NCH, P], BF16) for i in range(2)]
                VT_t = [Trk() for _ in range(2)]
                QT = [sb(st, f"QT{i}", [P, 2304], BF16) for i in range(2)]
                QT_t = [Trk() for _ in range(2)]
                QZ = [[sb(st, f"QZ{i}_{m}", [P, 2304], BF16) for m in range(2)] for i in range(2)]
                QZ_t = [[Trk() for m in range(2)] for i in range(2)]
                for i in range(2):
                    sy.op("dve", lambda e, i=i: e.memset(QZ[i][0][64:128, :], 0.0), writes=[QZ_t[i][0]])
                    sy.op("dve", lambda e, i=i: e.memset(QZ[i][1][0:64, :], 0.0), writes=[QZ_t[i][1]])
                NE = 4
                E = [sb(st, f"E{i}", [P, 512], BF16) for i in range(NE)]
                E_t = [Trk() for _ in range(NE)]
                SBm = [sb(st, f"SBm{i}", [P, 512], F32) for i in range(2)]
                SB_t = [Trk() for _ in range(2)]
                NB = [sb(st, f"NB{i}", [P, 512], F32) for i in range(3)]
                NB_t = [Trk() for _ in range(3)]
                RD = [sb(st, f"RD{i}", [P, 512], F32) for i in range(2)]
                RD_t = [Trk() for _ in range(2)]
                O1 = sb(st, "O1", [P, 512], F32)
                O1_t = Trk()
                O2 = sb(st, "O2", [P, 512], F32)
                O2_t = Trk()
                OD = sb(st, "OD", [P, 512], F32)
                OD_t = Trk()
                SQ = sb(st, "aSQ", [P, 512], BF16)
                SQ_t = Trk()
                RT = sb(st, "aRT", [P, 512], F32)
                RT_t = Trk()
                epsT = sb(st, "aeps", [P, 1], F32)
                eps_t = Trk()
                sy.op("dve", lambda e: e.memset(epsT[:], EPS), writes=[eps_t])
                cnt = {"s": 0, "e": 0, "acc": 0, "kv": 0, "q": 0, "nb": 0, "sb": 0, "rd": 0}
                ntok = sum(n for _, n in blocks)
                qblocks = [(t0, n) for (t0, n) in blocks if (t0 < S or with_ctx)]

                def load_kv(kchunk, vhead):
                    i = cnt["kv"] % 2
                    cnt["kv"] += 1
                    sy.load("sp", [KT_t[i]], KT[i][:], qkT[kchunk * P:(kchunk + 1) * P, :])
                    sy.load("sp", [VT_t[i]], VT[i][:], vS[vhead].rearrange("p (c d) -> p c d", d=P))
                    return i

                def load_q(qchunk):
                    i = cnt["q"] % 2
                    cnt["q"] += 1
                    sy.load("sp", [QT_t[i]], QT[i][:, 0:ntok], qkT[qchunk * P:(qchunk + 1) * P, x0:x0 + ntok])
                    return i

                def load_qz(qchunk):
                    i = cnt["q"] % 2
                    cnt["q"] += 1
                    sy.load("sp", [QZ_t[i][0]], QZ[i][0][0:64, 0:ntok], qkT[qchunk * P:qchunk * P + 64, x0:x0 + ntok])
                    sy.load("sp", [QZ_t[i][1]], QZ[i][1][64:128, 0:ntok],
                            qkT[qchunk * P + 64:(qchunk + 1) * P, x0:x0 + ntok])
                    return i

                def core(kvi, Q_, Q_t, t0, n, scale, na=None):
                    a = cnt["acc"] % 2
                    cnt["acc"] += 1
                    po, pot = ps_h[4 + a], PS[4 + a]
                    pd, pdt = ps_h[6 + a], PS[6 + a]
                    K_, K_t, V_, V_t = KT[kvi], KT_t[kvi], VT[kvi], VT_t[kvi]
                    if t0 >= S:
                        chunks = [32, 33]
                    elif na is not None:
                        j = t0 // 512
                        chunks = [c for c in range(4 * j - 2, 4 * j + 6) if 0 <= c < 32] + [32, 33]
                    else:
                        chunks = list(range(NCH))
                    q0 = t0 - x0

                    def emit_s(c):
                        si = cnt["s"] % 4
                        cnt["s"] += 1
                        ps, pst = ps_h[si], PS[si]
                        sy.op("pe", lambda e: e.matmul(ps[:, 0:n], lhsT=K_[:, c * P:(c + 1) * P],
                                                       rhs=Q_[:, q0:q0 + n], start=True, stop=True),
                              reads=[K_t, Q_t], writes=[pst])
                        return ps, pst

                    LA = 2
                    sq_ = [emit_s(chunks[i]) for i in range(min(LA, len(chunks)))]
                    for ci, c in enumerate(chunks):
                        ps, pst = sq_.pop(0)
                        if ci + LA < len(chunks):
                            sq_.append(emit_s(chunks[ci + LA]))
                        ei = cnt["e"] % NE
                        cnt["e"] += 1
                        E_, E_t_ = E[ei], E_t[ei]
                        if na is not None and c < 32:
                            h, var = na
                            j = t0 // 512
                            i_off = c - (4 * j - 2)
                            bi = cnt["nb"] % 3
                            cnt["nb"] += 1
                            sy.load("sp", [NB_t[bi]], NB[bi][:], nab_in[l, h, var, i_off])
                            sb_i = cnt["sb"] % 2
                            cnt["sb"] += 1
                            sy.op("dve", lambda e: e.scalar_tensor_tensor(
                                SBm[sb_i][:, 0:n], ps[:, 0:n], scale, NB[bi][:, 0:n], ALU.mult, ALU.add),
                                reads=[pst, NB_t[bi]], writes=[SB_t[sb_i]])
                            sy.op("act", lambda e: e.activation(E_[:, 0:n], SBm[sb_i][:, 0:n], AF.Exp),
                                  reads=[SB_t[sb_i]], writes=[E_t_])
                        else:
                            sy.op("act", lambda e: e.activation(E_[:, 0:n], ps[:, 0:n], AF.Exp, scale=scale),
                                  reads=[pst], writes=[E_t_])
                        first, last = ci == 0, ci == len(chunks) - 1
                        sy.op("pe", lambda e: e.matmul(po[:, 0:n], lhsT=V_[:, c, :], rhs=E_[:, 0:n],
                                                       start=first, stop=last),
                              reads=[V_t, E_t_], writes=[pot], signal=False)
                        sy.op("pe", lambda e: e.matmul(pd[:, 0:n], lhsT=ones_bf[:], rhs=E_[:, 0:n],
                                                       start=first, stop=last),
                              reads=[E_t_, cst], writes=[pdt], signal=True)
                    return po, pot, pd, pdt

                def recip(pd, pdt, n):
                    i = cnt["rd"] % 2
                    cnt["rd"] += 1
                    sy.op("dve", lambda e: e.reciprocal(RD[i][:, 0:n], pd[:, 0:n]), reads=[pdt], writes=[RD_t[i]])
                    return RD[i], RD_t[i]

                def finish_plain(xchunk, po, pot, pd, pdt, t0, n):
                    R, Rt = recip(pd, pdt, n)
                    sy.op("dve", lambda e: e.tensor_tensor(XT[:, xchunk, t0 - x0:t0 - x0 + n], po[:, 0:n],
                                                           R[:, 0:n], ALU.mult),
                          reads=[pot, Rt], writes=[XT_t[(xchunk, t0)]])

                sc_a = 64 ** -0.5
                for h in range(4):
                    kvi = load_kv(4 + h, h)
                    qi = load_qz(h)
                    for (t0, n) in qblocks:
                        po, pot, pd, pdt = core(kvi, QZ[qi][0], QZ_t[qi][0], t0, n, sc_a)
                        R, Rt = recip(pd, pdt, n)
                        sy.op("dve", lambda e: e.tensor_tensor(O1[:, 0:n], po[:, 0:n], R[:, 0:n], ALU.mult),
                              reads=[pot, Rt], writes=[O1_t])
                        po, pot, pd, pdt = core(kvi, QZ[qi][1], QZ_t[qi][1], t0, n, sc_a)
                        R, Rt = recip(pd, pdt, n)
                        sy.op("dve", lambda e: e.tensor_tensor(O2[:, 0:n], po[:, 0:n], R[:, 0:n], ALU.mult),
                              reads=[pot, Rt], writes=[O2_t])
                        sy.op("dve", lambda e: e.scalar_tensor_tensor(OD[:, 0:n], O2[:, 0:n], nlam[:, 0:1],
                                                                      O1[:, 0:n], ALU.mult, ALU.add),
                              reads=[O1_t, O2_t, small], writes=[OD_t])
                        sy.op("act", lambda e: e.activation(SQ[:, 0:n], OD[:, 0:n], AF.Square),
                              reads=[OD_t], writes=[SQ_t])
                        sbi = cnt["s"] % 4
                        cnt["s"] += 1
                        sy.op("pe", lambda e: e.matmul(ps_h[sbi][:, 0:n], lhsT=ones_bf[:], rhs=SQ[:, 0:n],
                                                       start=True, stop=True), reads=[SQ_t, cst], writes=[PS[sbi]])
                        sy.op("act", lambda e: e.activation(RT[:, 0:n], ps_h[sbi][:, 0:n], AF.Ln,
                                                            bias=epsT[:, 0:1], scale=1.0 / P),
                              reads=[PS[sbi], eps_t], writes=[RT_t])
                        sy.op("act", lambda e: e.activation(RT[:, 0:n], RT[:, 0:n], AF.Exp, scale=-0.5),
                              reads=[RT_t], writes=[RT_t])
                        sy.op("dve", lambda e: e.scalar_tensor_tensor(
                            XT[:, h, t0 - x0:t0 - x0 + n], OD[:, 0:n], gdT[:, 0:1], RT[:, 0:n], ALU.mult, ALU.mult),
                            reads=[OD_t, RT_t, small], writes=[XT_t[(h, t0)]])
                sc_b = 128 ** -0.5
                for kvh in range(2):
                    kvi = load_kv(16 + kvh, 4 + kvh)
                    for gq in range(4):
                        hq = kvh * 4 + gq
                        qi = load_q(8 + hq)
                        for (t0, n) in qblocks:
                            po, pot, pd, pdt = core(kvi, QT[qi], QT_t[qi], t0, n, sc_b)
                            finish_plain(4 + hq, po, pot, pd, pdt, t0, n)
                for h in range(4):
                    kvi = load_kv(22 + h, 6 + h)
                    qi = load_q(18 + h)
                    for (t0, n) in qblocks:
                        if t0 < S:
                            j = t0 // 512
                            var = 0 if j == 0 else (2 if j == 7 else 1)
                            po, pot, pd, pdt = core(kvi, QT[qi], QT_t[qi], t0, n, sc_b, na=(h, var))
                        else:
                            po, pot, pd, pdt = core(kvi, QT[qi], QT_t[qi], t0, n, sc_b)
                        finish_plain(12 + h, po, pot, pd, pdt, t0, n)
                sy.barrier()
                sy.release(KT_t + VT_t + QT_t + NB_t + [t for l_ in QZ_t for t in l_])

        def stage_gemm_fm(wsrc_fn, ngroups, gcols, kchunks, blocks, XT, XT_t, x0, dstT, wname, epi="copy",
                          wsrc2_fn=None):
            nj = gcols // P
            with ExitStack() as st:
                nwb = 2
                wt = [sb(st, f"{wname}{i}", [P, kchunks, gcols], BF16) for i in range(nwb)]
                wt_t = [Trk() for _ in range(nwb)]
                if wsrc2_fn is not None:
                    wu = [sb(st, f"{wname}u{i}", [P, kchunks, gcols], BF16) for i in range(nwb)]
                    wu_t = [Trk() for _ in range(nwb)]
                NO = 4
                odt = BF16
                ob = [sb(st, f"{wname}o{i}", [P, 512], odt) for i in range(NO)]
                ob_t = [Trk() for _ in range(NO)]
                if epi == "swiglu":
                    sg = [sb(st, f"{wname}s{i}", [P, 512], F32) for i in range(NO)]
                    sg_t = [Trk() for _ in range(NO)]
                cnt = {"ps": 0, "o": 0}

                def issue_w(g):
                    sy.load("pool", [wt_t[g % nwb]], wt[g % nwb][:], wsrc_fn(g))
                    if wsrc2_fn is not None:
                        sy.load("pool", [wu_t[g % nwb]], wu[g % nwb][:], wsrc2_fn(g))

                issue_w(0)
                for g in range(ngroups):
                    if g + 1 < ngroups:
                        issue_w(g + 1)
                    W, Wt = wt[g % nwb], wt_t[g % nwb]
                    for (t0, n) in blocks:
                        for j in range(nj):
                            def mm(Wx, Wxt):
                                pi = cnt["ps"] % 8
                                cnt["ps"] += 1
                                ps, pst = ps_h[pi], PS[pi]
                                for k in range(kchunks):
                                    sy.op("pe", lambda e, k=k: e.matmul(
                                        ps[:, 0:n], lhsT=Wx[:, k, j * P:(j + 1) * P],
                                        rhs=XT[:, k, t0 - x0:t0 - x0 + n], start=(k == 0), stop=(k == kchunks - 1)),
                                        reads=[Wxt, XT_t[(k, t0)]], writes=[pst], signal=(k == kchunks - 1))
                                return ps, pst
                            ps, pst = mm(W, Wt)
                            oi = cnt["o"] % NO
                            cnt["o"] += 1
                            O, Ot = ob[oi], ob_t[oi]
                            row0 = g * gcols + j * P
                            if epi == "copy":
                                if oi % 2 == 0:
                                    sy.op("act", lambda e: e.copy(O[:, 0:n], ps[:, 0:n]), reads=[pst], writes=[Ot])
                                else:
                                    sy.op("dve", lambda e: e.tensor_copy(O[:, 0:n], ps[:, 0:n]), reads=[pst], writes=[Ot])
                            else:
                                ps2, pst2 = mm(wu[g % nwb], wu_t[g % nwb])
                                G, Gt = sg[oi], sg_t[oi]
                                sy.op("act", lambda e: e.activation(G[:, 0:n], ps[:, 0:n], AF.Silu),
                                      reads=[pst], writes=[Gt])
                                sy.op("dve", lambda e: e.tensor_tensor(O[:, 0:n], G[:, 0:n], ps2[:, 0:n], ALU.mult),
                                      reads=[Gt, pst2], writes=[Ot])
                            if dstT is None:
                                kk = row0 // P
                                s0 = t0 // 256
                                ns = n // 256
                                sy.store("sp", [Ot], yB[s0:s0 + ns, :, kk, :].rearrange("s p t -> p s t"),
                                         O[:, 0:n].rearrange("p (s t) -> p s t", t=256))
                            else:
                                sy.store("sp", [Ot], dstT[row0:row0 + P, t0:t0 + n], O[:, 0:n])
                sy.barrier()
                sy.release(wt_t + ob_t + (wu_t if wsrc2_fn is not None else []))

        def stage_down(l, with_ctx):
            passes = [[0, 1], [2, 3], [4, 5], [6, 7] + ([8] if with_ctx else [])]
            for pblocks in passes:
                blocks = [BLOCKS[i] for i in pblocks]
                x0 = blocks[0][0]
                ntok = sum(n for _, n in blocks)
                with ExitStack() as st:
                    HX = sb(st, "HX", [P, FC, 1280], BF16)
                    HX_t = {}
                    ld = Trk()
                    for k in range(FC):
                        for (t0, n) in blocks:
                            HX_t[(k, t0)] = ld
                    lds = [Trk() for _ in range(4)]
                    for q in range(4):
                        for k in range(11 * q, 11 * q + 11):
                            for (t0, n) in blocks:
                                HX_t[(k, t0)] = lds[q]
                        sy.load("sp", [lds[q]], HX[:, 11 * q:11 * q + 11, 0:ntok],
                                hidT[11 * q * P:(11 * q + 11) * P, x0:x0 + ntok].rearrange("(k p) t -> p k t", p=P))
                    stage_gemm_fm(
                        lambda g: w_down[l, :, g * 256:(g + 1) * 256].rearrange("(k p) n -> p k n", p=P),
                        8, 256, FC, blocks, HX, HX_t, x0, None, "wd")
                    sy.release(lds)

        def stage_transpose_out():
            with ExitStack() as st:
                hb = [sb(st, f"ohb{i}", [P, KC, 512], F32) for i in range(2)]
                hb_t = [Trk() for _ in range(2)]
                ox = [sb(st, f"oox{i}", [P, 4, D], F32) for i in range(2)]
                ox_t = [[Trk() for _ in range(16)] for _ in range(2)]
                pi = 0
                for bi, (t0, n) in enumerate(BLOCKS[:8]):
                    H, Ht = hb[bi % 2], hb_t[bi % 2]
                    OX, OXt = ox[bi % 2], ox_t[bi % 2]
                    for o in range(0, n, 256):
                        sy.load("sp", [Ht], H[:, :, o:o + 256], hB[(t0 + o) // 256])
                    for j in range(4):
                        for kg in range(4):
                            ps, pst = ps_h[pi % 8], PS[pi % 8]
                            pi += 1
                            for kk in range(4):
                                k = kg * 4 + kk
                                sy.op("pe", lambda e, ps=ps, kk=kk, k=k, j=j, H=H: e.transpose(
                                    ps[:, kk * P:(kk + 1) * P], H[:, k, j * P:(j + 1) * P], ident[:]),
                                    reads=[Ht, cst], writes=[pst], signal=(kk == 3))
                            tr = OXt[j * 4 + kg]
                            if (j * 4 + kg) % 2 == 0:
                                sy.op("act", lambda e, ps=ps, OX=OX, j=j, kg=kg: e.copy(
                                    OX[:, j, kg * 512:(kg + 1) * 512], ps[:, :]), reads=[pst], writes=[tr])
                            else:
                                sy.op("dve", lambda e, ps=ps, OX=OX, j=j, kg=kg: e.tensor_copy(
                                    OX[:, j, kg * 512:(kg + 1) * 512], ps[:, :]), reads=[pst], writes=[tr])
                    sy.store("sp", OXt, out[t0:t0 + n, :].rearrange("(j p) f -> p j f", p=P), OX[:])
                sy.barrier()
                sy.release(hb_t + [t for l_ in ox_t for t in l_])

        def dump(name, src_ap, shape, dtype):
            if name not in dbg:
                return
            rows, cols = shape
            with ExitStack() as st:
                tl = sb(st, "dbgt", [P, cols], dtype)
                tt = Trk()
                for r0 in range(0, rows, P):
                    sy.load("sp", [tt], tl[:], src_ap[r0:r0 + P, :])
                    sy.store("sp", [tt], dbg[name][r0:r0 + P, :], tl[:])
                sy.barrier()
                sy.release([tt])

        def main():
            stage_transpose_in()
            if stop_after == "x0":
                return
            for l in range(DEPTH):
                last = l == DEPTH - 1
                with_ctx = not last
                stage_mod(l)
                if l == 0 and "d_mod" in dbg:
                    tdm = Trk()
                    sy.store("sp", [small], dbg["d_mod"][:, 0:96], modb[:])
                    sy.store("sp", [small], dbg["d_mod"][:, 96:192], modc[:])
                    for ii, nm_ in enumerate(("a1", "gt1g", "a2", "gt2g")):
                        for jj in range(2):
                            o_ = 192 + (ii * 2 + jj) * 16
                            sy.store("sp", [small], dbg["d_mod"][:, o_:o_ + 16], der[nm_][jj][:])
                    with nc.allow_non_contiguous_dma(reason="debug"):
                        sy.store("sp", [small], dbg["d_mod"][:, 320:321], nlam[:])
                        sy.store("sp", [small], dbg["d_mod"][:, 321:322], gdT[:])
                    sy.barrier()
                if stop_after == "mod":
                    return
                for rb in RANGES:
                    blocks = [BLOCKS[i] for i in rb]
                    x0 = blocks[0][0]
                    with ExitStack() as st:
                        XT = sb(st, "XT", [P, KC, 2304], BF16)
                        XT_t = {(k, t0): Trk() for k in range(KC) for (t0, n) in blocks}
                        stage_norm(l, blocks, XT, XT_t, None, 1)
                        if stop_after == "norm1":
                            if "d_xt" in dbg:
                                for k_ in range(KC):
                                    sy.store("sp", [small], dbg["d_xt"][k_ * P:(k_ + 1) * P, :], XT[:, k_, 0:2048])
                                sy.barrier()
                            return
                        stage_inproj(l, blocks, XT, XT_t)
                if l == 0:
                    dump("d_qkT", qkT, (26 * P, T), BF16)
                    dump("d_vS", vS.rearrange("h p x -> (h p) x"), (10 * P, NCH * P), BF16)
                if stop_after == "inproj":
                    return
                for rb in RANGES:
                    blocks = [BLOCKS[i] for i in rb if (i < 8 or with_ctx)]
                    x0 = blocks[0][0]
                    with ExitStack() as st:
                        XT = sb(st, "XT", [P, KC, 2304], BF16)
                        XT_t = {(k, t0): Trk() for k in range(KC) for (t0, n) in blocks}
                        stage_attention(l, blocks, XT, XT_t, with_ctx)
                        stage_gemm_fm(
                            lambda g: w_out[l, :, g * 512:(g + 1) * 512].rearrange("(k p) n -> p k n", p=P),
                            4, 512, KC, blocks, XT, XT_t, x0, None, "wo")
                        if l == 0 and stop_after == "outproj":
                            continue
                        stage_norm(l, blocks, XT, XT_t, "gt1g", 2)
                        stage_gemm_fm(
                            lambda g: w_gate[l, :, g * 256:(g + 1) * 256].rearrange("(k p) n -> p k n", p=P),
                            22, 256, KC, blocks, XT, XT_t, x0, hidT, "wg", epi="swiglu",
                            wsrc2_fn=lambda g: w_up[l, :, g * 256:(g + 1) * 256].rearrange("(k p) n -> p k n", p=P))
                if l == 0 and stop_after == "outproj":
                    return
                stage_down(l, with_ctx)
                blocks = [BLOCKS[i] for i in range(9) if (i < 8 or with_ctx)]
                stage_norm(l, blocks, None, None, "gt2g", None)
                if l == 0:
                    if stop_after == "layer0":
                        return
            stage_transpose_out()

        main()
        sy.barrier()
    return nc, sy


_CACHE = {}


def _host_tables():
    if "tabs" not in _CACHE:
        (C64, S64, PM64), (C128, S128, PM128) = _rope_tables()
        rope = np.stack([C64, S64, C128, S128]).astype(np.float32)
        pmm = np.stack([PM64, PM128]).astype(np.float32)
        idx, val = _na_index_tables()
        _CACHE["tabs"] = (rope, pmm, idx, val)
    return _CACHE["tabs"]


def kernel(x, c, ctx, c_ctx, w_mod, b_mod, g_pre1, g_post1, g_pre2, g_post2,
           w_in, w_out, lam_q1, lam_k1, lam_q2, lam_k2, g_diff, g_qn, g_kn, rpb,
           w_gate, w_up, w_down):
    f = lambda a: np.ascontiguousarray(np.asarray(a, dtype=np.float32))
    x, c, ctx, c_ctx = f(x), f(c), f(ctx), f(c_ctx)
    rope, pmm, idx, val = _host_tables()
    rpb = f(rpb)
    rpb_flat = rpb.reshape(DEPTH, 4, 15 * 31)
    nab = np.where(val[None, None], rpb_flat[:, :, idx], np.float32(-30000.0)).astype(np.float32)
    shared = {
        "w_mod": f(w_mod), "b_mod": f(b_mod),
        "gains": np.stack([f(g_pre1), f(g_post1), f(g_pre2), f(g_post2)]),
        "w_in": f(w_in), "w_out": f(w_out),
        "lam": np.stack([f(lam_q1), f(lam_k1), f(lam_q2), f(lam_k2)]),
        "hg": np.stack([f(g_diff), f(g_qn), f(g_kn)]),
        "w_gate": f(w_gate), "w_up": f(w_up), "w_down": f(w_down),
        "rope": rope, "pm": pmm, "nab": np.ascontiguousarray(nab),
        "ident": np.eye(P, dtype=np.float32),
    }
    nc, _ = build_program()
    if N_CORES == 1:
        m = dict(shared)
        m["x"], m["ctx"], m["cc"] = x[0], ctx[0], np.stack([c[0], c_ctx])
        res = run_bass_kernel_spmd(nc, [m], core_ids=[0])
        _CACHE["last_results"] = res
        return np.stack([np.asarray(res.results[0]["out"], dtype=np.float32)])
    active = ACTIVE_CORES
    zero = {k: (v if k in ("ident", "pm") else np.zeros_like(v)) for k, v in shared.items()}
    zero["x"] = np.zeros_like(x[0])
    zero["ctx"] = np.zeros_like(ctx[0])
    zero["cc"] = np.zeros((2, D), np.float32)
    in_maps = [zero] * 8
    in_maps = list(in_maps)
    for b, core in enumerate(active):
        m = dict(shared)
        m["x"] = x[b]
        m["ctx"] = ctx[b]
        m["cc"] = np.stack([c[b], c_ctx])
        in_maps[core] = m
    res = run_bass_kernel_spmd(nc, in_maps, core_ids=list(range(8)))
    _CACHE["last_results"] = res
    return np.stack([np.asarray(res.results[core]["out"], dtype=np.float32) for core in active])
```
